# Optimizing a Trainium2 kernel written in Bass

```python
import math, functools
import jax, jax.numpy as jnp
from jax import lax
import numpy as np

D_MODEL = 4096
BATCH = 1
SEQ = 8192
DEPTH = 1
DEC_BATCH = 128
DEC_SEQ = 8
PAST_LEN = 8192
PAGE_SIZE = 128

HEAD_DIM = 128
ATTN_DIM = D_MODEL // 2
N_HEADS = ATTN_DIM // HEAD_DIM
N_KV_HEADS = 4
GQA_GROUP = N_HEADS // N_KV_HEADS
KV_DIM = N_KV_HEADS * HEAD_DIM
CONV_DIM = D_MODEL - ATTN_DIM
CONV_GROUPS = CONV_DIM // HEAD_DIM
CONV_WIDTH = 3
WINDOW = 128
BLOCK = 128
N_BUCKETS = 32
MAX_DISTANCE = 128
D_FF = 11008
N_MOD = 9
PROJ_DIM = ATTN_DIM + 2 * KV_DIM + 3 * CONV_DIM
EPS = 1e-6
NEG = -1e30

kernel_name = "hymba_swa_sink_shortconv_macaron_step"


def _rmsnorm(x, g):
    xf = x.astype(jnp.float32)
    y = xf * lax.rsqrt(jnp.mean(xf * xf, axis=-1, keepdims=True) + EPS)
    return (y * g.astype(jnp.float32)).astype(x.dtype)


def _modulate(h, shift, scale):
    return h * (1 + scale) + shift


def _swiglu(h, w1, w3, w2):
    return (jax.nn.silu(h @ w1) * (h @ w3)) @ w2


def _t5_bucket(dist):
    n = jnp.maximum(dist, 0)
    max_exact = N_BUCKETS // 2
    nf = jnp.maximum(n, 1).astype(jnp.float32)
    large = max_exact + (jnp.log(nf / max_exact) / math.log(MAX_DISTANCE / max_exact)
                         * (N_BUCKETS - max_exact)).astype(jnp.int32)
    large = jnp.minimum(large, N_BUCKETS - 1)
    return jnp.where(n < max_exact, n, large)


def _rel_bias(dist, table):
    b = jnp.moveaxis(table[_t5_bucket(dist)].astype(jnp.float32), -1, 0)
    return b.reshape(N_KV_HEADS, GQA_GROUP, *dist.shape)


def _sink_attention(q, k, v, bias, valid, sinks):
    s = jnp.einsum('...qhgd,...khd->...hgqk', q, k).astype(jnp.float32) * (HEAD_DIM ** -0.5) + bias
    s = jnp.where(valid, s, NEG)
    sk = sinks.astype(jnp.float32).reshape(N_KV_HEADS, GQA_GROUP, 1, 1)
    m = jnp.maximum(jnp.max(s, axis=-1, keepdims=True), sk)
    p = jnp.exp(s - m)
    denom = jnp.sum(p, axis=-1, keepdims=True) + jnp.exp(sk - m)
    p = (p / denom).astype(v.dtype)
    return jnp.einsum('...hgqk,...khd->...qhgd', p, v)


def _split(proj):
    cuts = [ATTN_DIM, ATTN_DIM + KV_DIM, ATTN_DIM + 2 * KV_DIM,
            ATTN_DIM + 2 * KV_DIM + CONV_DIM, ATTN_DIM + 2 * KV_DIM + 2 * CONV_DIM]
    return jnp.split(proj, cuts, axis=-1)


def _causal_conv(ucat, conv_w, s_len):
    y = ucat[:, 0:s_len] * conv_w[0]
    for j in range(1, CONV_WIDTH):
        y = y + ucat[:, j:j + s_len] * conv_w[j]
    return y


def _prompt_mixer(proj, sinks, conv_w, rel_bias):
    bp, s_len, _ = proj.shape
    q, k, v, gb, gc, hc = _split(proj)
    nb = s_len // BLOCK
    q = q.reshape(bp, nb, BLOCK, N_KV_HEADS, GQA_GROUP, HEAD_DIM)
    k = k.reshape(bp, s_len, N_KV_HEADS, HEAD_DIM)
    v = v.reshape(bp, s_len, N_KV_HEADS, HEAD_DIM)
    pad = ((0, 0), (1, 0), (0, 0), (0, 0), (0, 0))
    kb = k.reshape(bp, nb, BLOCK, N_KV_HEADS, HEAD_DIM)
    vb = v.reshape(bp, nb, BLOCK, N_KV_HEADS, HEAD_DIM)
    kk = jnp.concatenate([jnp.pad(kb, pad)[:, :-1], kb], axis=2)
    vv = jnp.concatenate([jnp.pad(vb, pad)[:, :-1], vb], axis=2)
    r = jnp.arange(BLOCK)
    cidx = jnp.arange(2 * BLOCK)
    dist = BLOCK + r[:, None] - cidx[None, :]
    kpos = (jnp.arange(nb)[:, None] - 1) * BLOCK + cidx[None, :]
    valid = ((dist >= 0) & (dist <= WINDOW))[None] & (kpos >= 0)[:, None, :]
    valid = valid[:, None, None]
    o = _sink_attention(q, kk, vv, _rel_bias(dist, rel_bias), valid, sinks)
    o = o.reshape(bp, s_len, ATTN_DIM)
    u = gc * hc
    upad = jnp.pad(u, ((0, 0), (CONV_WIDTH - 1, 0), (0, 0)))
    yc = gb * _causal_conv(upad, conv_w, s_len)
    wb = min(WINDOW, s_len)
    state = (k[:, s_len - wb:], v[:, s_len - wb:], u[:, s_len - (CONV_WIDTH - 1):])
    return jnp.concatenate([o, yc], axis=-1), state


def _sample_mixer(proj, ck, cv, cconv, sinks, conv_w, rel_bias):
    bd, s_len, _ = proj.shape
    wb = ck.shape[1]
    q, k, v, gb, gc, hc = _split(proj)
    q = q.reshape(bd, s_len, N_KV_HEADS, GQA_GROUP, HEAD_DIM)
    k = k.reshape(bd, s_len, N_KV_HEADS, HEAD_DIM)
    v = v.reshape(bd, s_len, N_KV_HEADS, HEAD_DIM)
    kk = jnp.concatenate([ck.astype(k.dtype), k], axis=1)
    vv = jnp.concatenate([cv.astype(v.dtype), v], axis=1)
    qpos = PAST_LEN + jnp.arange(s_len)
    kpos = jnp.concatenate([PAST_LEN - wb + jnp.arange(wb), qpos])
    dist = qpos[:, None] - kpos[None, :]
    valid = (dist >= 0) & (dist <= WINDOW) & (kpos >= 0)[None, :]
    o = _sink_attention(q, kk, vv, _rel_bias(dist, rel_bias), valid, sinks)
    o = o.reshape(bd, s_len, ATTN_DIM)
    u = gc * hc
    ucat = jnp.concatenate([cconv.astype(u.dtype), u], axis=1)
    yc = gb * _causal_conv(ucat, conv_w, s_len)
    state = (kk[:, -wb:], vv[:, -wb:], ucat[:, -(CONV_WIDTH - 1):])
    return jnp.concatenate([o, yc], axis=-1), state


def _layer(x, c, mix_fn, g1, w1a, w3a, w2a, g_mix, w_in, w_out, g2, w1b, w3b, w2b, w_ada, b_ada):
    mod = (jax.nn.silu(c) @ w_ada + b_ada).reshape(c.shape[0], N_MOD, D_MODEL)[:, :, None, :]
    h = _modulate(_rmsnorm(x, g1), mod[:, 0], mod[:, 1])
    x = x + 0.5 * mod[:, 2] * _swiglu(h, w1a, w3a, w2a)
    h = _modulate(_rmsnorm(x, g_mix), mod[:, 3], mod[:, 4])
    mixed, state = mix_fn(h @ w_in)
    x = x + mod[:, 5] * (mixed @ w_out)
    h = _modulate(_rmsnorm(x, g2), mod[:, 6], mod[:, 7])
    x = x + 0.5 * mod[:, 8] * _swiglu(h, w1b, w3b, w2b)
    return x, state


def setup_inputs(seed: int = 0) -> dict:
    key = jax.random.key(seed)
    ks = jax.random.split(key, 32)
    f32 = jnp.float32
    wb = min(WINDOW, PAST_LEN)
    nrm = lambda k, shape, s: jax.random.normal(k, shape, f32) * s
    gain = lambda k: 1.0 + 0.05 * jax.random.normal(k, (DEPTH, D_MODEL), f32)
    sd, sf = D_MODEL ** -0.5, D_FF ** -0.5
    return {
        "x_prompt": nrm(ks[0], (BATCH, SEQ, D_MODEL), 1.0),
        "x_sample": nrm(ks[1], (DEC_BATCH, DEC_SEQ, D_MODEL), 1.0),
        "c_prompt": nrm(ks[2], (BATCH, D_MODEL), 1.0),
        "c_sample": nrm(ks[3], (DEC_BATCH, D_MODEL), 1.0),
        "cache_k": nrm(ks[4], (DEPTH, DEC_BATCH, wb, N_KV_HEADS, HEAD_DIM), 1.0),
        "cache_v": nrm(ks[5], (DEPTH, DEC_BATCH, wb, N_KV_HEADS, HEAD_DIM), 1.0),
        "state_conv": nrm(ks[6], (DEPTH, DEC_BATCH, CONV_WIDTH - 1, CONV_DIM), 1.0),
        "rel_bias": nrm(ks[7], (N_BUCKETS, N_HEADS), 0.5),
        "g_ffn1": gain(ks[8]),
        "w1_ffn1": nrm(ks[9], (DEPTH, D_MODEL, D_FF), sd),
        "w3_ffn1": nrm(ks[10], (DEPTH, D_MODEL, D_FF), sd),
        "w2_ffn1": nrm(ks[11], (DEPTH, D_FF, D_MODEL), sf),
        "g_mix": gain(ks[12]),
        "w_in": nrm(ks[13], (DEPTH, D_MODEL, PROJ_DIM), sd),
        "sinks": nrm(ks[14], (DEPTH, N_HEADS), 1.0),
        "conv_w": nrm(ks[15], (DEPTH, CONV_WIDTH, CONV_DIM), CONV_WIDTH ** -0.5),
        "w_out": nrm(ks[16], (DEPTH, D_MODEL, D_MODEL), sd),
        "g_ffn2": gain(ks[17]),
        "w1_ffn2": nrm(ks[18], (DEPTH, D_MODEL, D_FF), sd),
        "w3_ffn2": nrm(ks[19], (DEPTH, D_MODEL, D_FF), sd),
        "w2_ffn2": nrm(ks[20], (DEPTH, D_FF, D_MODEL), sf),
        "w_ada": nrm(ks[21], (DEPTH, D_MODEL, N_MOD * D_MODEL), 0.5 * sd),
        "b_ada": nrm(ks[22], (DEPTH, N_MOD * D_MODEL), 0.02),
        "g_final": 1.0 + 0.05 * jax.random.normal(ks[23], (D_MODEL,), f32),
    }


def reference(x_prompt, x_sample, c_prompt, c_sample, cache_k, cache_v, state_conv, rel_bias,
              g_ffn1, w1_ffn1, w3_ffn1, w2_ffn1, g_mix, w_in, sinks, conv_w, w_out,
              g_ffn2, w1_ffn2, w3_ffn2, w2_ffn2, w_ada, b_ada, g_final):
    xp, xs = x_prompt, x_sample
    kp, vp, cp, ksn, vsn, csn = [], [], [], [], [], []
    for l in range(DEPTH):
        w = (g_ffn1[l], w1_ffn1[l], w3_ffn1[l], w2_ffn1[l], g_mix[l], w_in[l], w_out[l],
             g_ffn2[l], w1_ffn2[l], w3_ffn2[l], w2_ffn2[l], w_ada[l], b_ada[l])
        pmix = functools.partial(_prompt_mixer, sinks=sinks[l], conv_w=conv_w[l], rel_bias=rel_bias)
        smix = functools.partial(_sample_mixer, ck=cache_k[l], cv=cache_v[l], cconv=state_conv[l],
                                 sinks=sinks[l], conv_w=conv_w[l], rel_bias=rel_bias)
        xp, sp = _layer(xp, c_prompt, pmix, *w)
        xs, ss = _layer(xs, c_sample, smix, *w)
        kp.append(sp[0]); vp.append(sp[1]); cp.append(sp[2])
        ksn.append(ss[0]); vsn.append(ss[1]); csn.append(ss[2])
    y_prompt = _rmsnorm(xp, g_final)
    y_sample = _rmsnorm(xs, g_final)
    k_win_prompt, v_win_prompt, conv_prompt = jnp.stack(kp), jnp.stack(vp), jnp.stack(cp)
    k_win_sample, v_win_sample, conv_sample = jnp.stack(ksn), jnp.stack(vsn), jnp.stack(csn)
    return (y_prompt, y_sample, k_win_prompt, v_win_prompt, conv_prompt, k_win_sample, v_win_sample, conv_sample)
```

```python
import math
import os
import numpy as np
import concourse.bass as bass
import concourse.mybir as mybir
from concourse.bass_utils import run_bass_kernel_spmd

F32 = mybir.dt.float32
BF16 = mybir.dt.bfloat16
U8 = mybir.dt.uint8
AF = mybir.ActivationFunctionType
ALU = mybir.AluOpType
AX = mybir.AxisListType

NCORES = 8
D = 4096
KC = 32
DFF = 11008
FC = 86
T0 = 1280
T1 = 1152
NR = 17
PIECE = 8
NXB = int(os.environ.get('KNXB', '4'))
NS = 4
NW = 10
EPS = 1e-6
NEG = -1e30
SCALE = 128 ** -0.5
ENG = ('pe', 'act', 'dve', 'pool', 'sp')


class Buf:
    __slots__ = ('w', 'r')

    def __init__(self):
        self.w = None
        self.r = []


class Sched:
    def __init__(self):
        self.ops = {e: [] for e in ENG}
        self.dma_cnt = {}
        self.out_tickets = []

    def op(self, eng, fn, reads=(), writes=(), deps=(), dma=None, dep_only=(), reg_only=()):
        d = list(deps)
        for b in list(reads) + list(dep_only):
            if b.w is not None:
                d.append(b.w)
        for b in writes:
            if b.w is not None:
                d.append(b.w)
            d.extend(b.r)
        idx = len(self.ops[eng])
        if dma is not None:
            n = self.dma_cnt.get(dma, 0) + 1
            self.dma_cnt[dma] = n
            t = ('d', dma, 16 * n)
        else:
            t = ('c', eng, idx)
        self.ops[eng].append([fn, d, False, dma])
        for b in list(reads) + list(reg_only):
            b.r.append(t)
        for b in writes:
            b.w = t
            b.r = []
        return t


    def simulate(self):
        for e in ENG:
            for o in self.ops[e]:
                for t in o[1]:
                    if t[0] == 'c' and not (t[1] == 'pe' and e == 'pe'):
                        assert self.ops[t[1]][t[2]][3] is None, ("compute ticket on a DMA op", e, t)
                        self.ops[t[1]][t[2]][2] = True
        cum = {}
        for e in ENG:
            c = 0
            arr = []
            for o in self.ops[e]:
                if o[2] and o[3] is None:
                    c += 1
                arr.append(c)
            cum[e] = arr
        semc = {e: 0 for e in ENG}
        semd = {}
        pc = {e: 0 for e in ENG}
        progress = True
        while progress:
            progress = False
            for e in ENG:
                while pc[e] < len(self.ops[e]):
                    fn, deps, inc, dma = self.ops[e][pc[e]]
                    ok = True
                    for t in deps:
                        if t[0] == 'c':
                            if t[1] == 'pe' and e == 'pe':
                                continue
                            if semc[t[1]] < cum[t[1]][t[2]]:
                                ok = False
                                break
                        else:
                            if semd.get(t[1], 0) < t[2]:
                                ok = False
                                break
                    if not ok:
                        break
                    if dma is not None:
                        semd[dma] = semd.get(dma, 0) + 16
                    elif inc:
                        semc[e] += 1
                    pc[e] += 1
                    progress = True
        stuck = {e: (pc[e], len(self.ops[e])) for e in ENG if pc[e] < len(self.ops[e])}
        if stuck:
            print("DEADLOCK", stuck)
            for e in stuck:
                fn, deps, inc, dma = self.ops[e][pc[e]]
                for t in deps:
                    if t[0] == 'c' and not (t[1] == 'pe' and e == 'pe'):
                        if semc[t[1]] < cum[t[1]][t[2]]:
                            print("  ", e, "op", pc[e], "waits", t, "need", cum[t[1]][t[2]], "have", semc[t[1]], "target pc", pc[t[1]])
                    elif t[0] == 'd' and semd.get(t[1], 0) < t[2]:
                        print("  ", e, "op", pc[e], "waits dma", t, "have", semd.get(t[1], 0))
        else:
            print("simulate: no deadlock; ops", {e: len(self.ops[e]) for e in ENG})
        return not stuck

    def emit(self, nc, block, sems, dma_sems):
        for e in ENG:
            for o in self.ops[e]:
                for t in o[1]:
                    if t[0] == 'c' and not (t[1] == 'pe' and e == 'pe'):
                        self.ops[t[1]][t[2]][2] = True
        cum = {}
        for e in ENG:
            c = 0
            arr = []
            for o in self.ops[e]:
                if o[2]:
                    c += 1
                arr.append(c)
            cum[e] = arr
        ops = self.ops

        def run(e, eng):
            waited = {}
            for o in ops[e]:
                fn, deps, inc, dma = o
                need = {}
                for t in deps:
                    if t[0] == 'c':
                        if t[1] == 'pe' and e == 'pe':
                            continue
                        key = ('c', t[1])
                        val = cum[t[1]][t[2]]
                    else:
                        key = ('d', t[1])
                        val = t[2]
                    if val > need.get(key, 0):
                        need[key] = val
                for key, val in need.items():
                    if waited.get(key, 0) < val:
                        sem = sems[key[1]] if key[0] == 'c' else dma_sems[key[1]]
                        eng.wait_ge(sem, val)
                        waited[key] = val
                ins = fn(eng)
                if dma is not None:
                    ins.then_inc(dma_sems[dma], 16)
                elif inc:
                    ins.then_inc(sems[e], 1)

        @block.tensor
        def _(eng):
            run('pe', eng)

        @block.scalar
        def _(eng):
            run('act', eng)

        @block.vector
        def _(eng):
            run('dve', eng)

        @block.gpsimd
        def _(eng):
            run('pool', eng)

        @block.sync
        def _(eng):
            run('sp', eng)


def mk(method, *args, **kw):
    return lambda e: getattr(e, method)(*args, **kw)


def col_groups(c0, n):
    out = []
    while n > 0:
        m = min(512, n)
        out.append((c0, m))
        c0 += m
        n -= m
    return out


def t5_bucket_np(dist):
    n = np.maximum(dist, 0)
    nf = np.maximum(n, 1).astype(np.float32)
    large = 16 + (np.log(nf / np.float32(16)) / np.float32(math.log(128 / 16)) * np.float32(16)).astype(np.int32)
    large = np.minimum(large, 31)
    return np.where(n < 16, n, large)


def build_program(stop_after=None, debug=False):
    nc = bass.Bass("TRN2", target_bir_lowering=False)
    S = Sched()

    def din(name, shape, dt=F32):
        return nc.dram_tensor(name, list(shape), dt, kind="ExternalInput").ap()

    def dout(name, shape, dt=F32):
        return nc.dram_tensor(name, list(shape), dt, kind="ExternalOutput").ap()

    def dint(name, shape, dt=F32):
        return nc.dram_tensor(name, list(shape), dt, kind="Internal").ap()

    DSPEC = {
        'xT_d': ('in', 'xT', [KC, 128, T0], F32),
        'cT_d': ('in', 'cT', [128, KC, NR], F32),
        'bada_d': ('in', 'badaT', [128, 288], F32),
        'wada_d': ('in', 'w_ada', [72, 128, KC * 512], F32),
        'w1a_d': ('in', 'w1a', [FC, 128, D], F32),
        'w3a_d': ('in', 'w3a', [FC, 128, D], F32),
        'w2a_d': ('in', 'w2a', [KC, 128, DFF], F32),
        'win_d': ('in', 'w_in', [72, 128, D], F32),
        'wout_d': ('in', 'w_out', [KC, 128, D], F32),
        'w1b_d': ('in', 'w1b', [FC, 128, D], F32),
        'w3b_d': ('in', 'w3b', [FC, 128, D], F32),
        'w2b_d': ('in', 'w2b', [KC, 128, DFF], F32),
        'gains_d': ('in', 'gains', [128, 4, KC], F32),
        'convw_d': ('in', 'convw', [128, 16, 3], F32),
        'sinks_d': ('in', 'sinks_bc', [128, 16], F32),
        'sinkS_d': ('in', 'sinkS', [32, 4], F32),
        'relb_d': ('in', 'relb', [32, 16], F32),
        'onehot_d': ('in', 'onehot', [33, 384], F32),
        'halo_d': ('in', 'halo', [128, 2], F32),
        'stT_d': ('in', 'stT', [128, 16, 16, 2], F32),
        'ck_d': ('in', 'ck', [16, 128, 512], F32),
        'cv_d': ('in', 'cv', [16, 128, 512], F32),
        'yT_d': ('out', 'yT', [KC, 128, T1], F32),
        'kwin_d': ('out', 'kwinT', [4, 128, 128], F32),
        'vwin_d': ('out', 'vwinT', [4, 128, 128], F32),
        'knew_d': ('out', 'knewT', [4, 128, 128], F32),
        'vnew_d': ('out', 'vnewT', [4, 128, 128], F32),
        'kwsc_d': ('out', 'kwsc', [16, 120, 512], F32),
        'vwsc_d': ('out', 'vwsc', [16, 120, 512], F32),
        'convo_d': ('out', 'convo', [128, 16, NR, 2], F32),
        'aT_d': ('int', 'aT_s', [FC, 128, T0], BF16),
        'x1_d': ('int', 'x1T_s', [KC, 128, T0], F32),
        'x2_d': ('int', 'x2T_s', [KC, 128, T1], F32),
        'x3_d': ('int', 'x3T_s', [KC, 128, T1], F32),
        'mix_d': ('int', 'mix_s', [KC, 128, T1], BF16),
        'mod_d': ('int', 'mod_s', [128, 9, KC, NR], F32),
        'R_d': ('int', 'R_s', [128, 16, 384], F32),
        'dbg_h_d': ('out', 'dbg_h', [128, KC, T0], BF16),
        'dbg_c_d': ('out', 'dbg_c', [128, 128], F32),
    }

    class _G:
        def __init__(self):
            self.c = {}

        def __getattr__(self, var):
            c = self.__dict__["c"]
            if var not in c:
                kind, nm, shape, dt = DSPEC[var]
                k = {"in": "ExternalInput", "out": "ExternalOutput", "int": ("ExternalOutput" if debug else "Internal")}[kind]
                c[var] = nc.dram_tensor(nm, list(shape), dt, kind=k).ap()
            return c[var]

    G = _G()

    ARENA = int(os.environ.get('KARENA', '207')) * 1024
    arena_cm = nc.sbuf_tensor("arena", [128, ARENA], U8)
    psum_cm = nc.psum_tensor("ps", [128, 8, 512], F32)
    arena = arena_cm.__enter__()
    ps = psum_cm.__enter__()

    class Alloc:
        def __init__(self):
            self.off = 0

        def get(self, nbytes, dt, shape=None, parts=128):
            o = self.off
            self.off = (o + nbytes + 63) // 64 * 64
            assert self.off <= ARENA, ("arena overflow", self.off)
            v = arena[0:parts, o:o + nbytes].bitcast(dt)
            if shape is not None and len(shape) > 1:
                names = " ".join("abcdefg"[i] for i in range(len(shape)))
                kw = {"abcdefg"[i]: shape[i] for i in range(len(shape) - 1)}
                v = v.rearrange("p (%s) -> p %s" % (names, names), **kw)
            return v

    A = Alloc()
    ident_bf = A.get(256, BF16)
    ident_f = A.get(512, F32)
    ones_f = A.get(512, F32)
    gains = A.get(4 * KC * 4, F32, (4, KC))
    halo = A.get(8, F32)
    convw = A.get(16 * 3 * 4, F32, (16, 3))
    sinks = A.get(64, F32)
    par = A.get(3 * KC * NR * 4, F32, (3, KC, NR))
    junk = A.get(64, F32)
    eps_ap = A.get(4, F32)
    stage = [A.get(PIECE * 128 * 4, F32, (PIECE, 128)) for _ in range(NS)]
    wbf = [A.get(PIECE * 128 * 2, BF16, (PIECE, 128)) for _ in range(NW)]
    stage_b = [Buf() for _ in range(NS)]
    wbf_b = [Buf() for _ in range(NW)]
    PH0 = A.off
    bank_b = [Buf() for _ in range(8)]
    par_b = Buf()
    parg_b = Buf()
    const_b = Buf()

    st = {"wi": 0, "gi": 0, "cast": 0}

    def barrier():
        ts = []
        for e in ('pe', 'act', 'dve', 'pool'):
            for idx in range(len(S.ops[e]) - 1, -1, -1):
                if S.ops[e][idx][3] is None:
                    ts.append(('c', e, idx))
                    break
        for k, n in S.dma_cnt.items():
            if not k.startswith("st"):
                ts.append(('d', k, 16 * n))
        S.op('pe', mk('matmul', ps[:, 7, 0:8], lhsT=ident_f[:, 0:128], rhs=ident_f[:, 0:8], start=True, stop=True), deps=ts)
        S.op('act', mk('copy', out=junk[:, 0:4], in_=junk[:, 4:8]), deps=ts)
        S.op('dve', mk('tensor_copy', out=junk[:, 8:12], in_=junk[:, 12:16]), deps=ts)
        S.op('pool', mk('memset', junk[:, 14:16], 0.0), deps=ts)

    def pdma(out, in_, key, reads=(), writes=(), deps=()):
        return S.op('pool', mk('dma_start', out=out, in_=in_), reads=reads, writes=writes, deps=deps, dma=key)

    def gemm(mats, nk, n_list, act, groups, epilogue, pre_n=None, post_n=None, act_reads=(), k_outer=False, act_reads_k=None):
        pieces = []
        k0 = 0
        while k0 < nk:
            m = min(PIECE, nk - k0)
            pieces.append((k0, m))
            k0 += m
        n_list = list(n_list)
        R = len(mats) * len(pieces)
        tiles = [(n, mi, pi) for n in n_list for mi in range(len(mats)) for pi in range(len(pieces))]
        slot_of = {}
        ld = {"j": 0}

        def load_upto(jmax):
            jmax = min(jmax, len(tiles))
            while ld["j"] < jmax:
                n_, mi, pi = tiles[ld["j"]]
                ld["j"] += 1
                k0, m = pieces[pi]
                wi = st["wi"]
                st["wi"] += 1
                ss, ws = wi % NS, wi % NW
                src = mats[mi][n_][:, k0 * 128:(k0 + m) * 128].rearrange("p (k j) -> p k j", k=m)
                S.op('sp', lambda e, o=stage[ss][:, 0:m, :], i=src: e.dma_start(out=o, in_=i),
                     writes=[stage_b[ss]], dma="st%d" % ss)
                ceng = 'act' if (st["cast"] % 2 == 0) else 'dve'
                st["cast"] += 1
                if ceng == 'act':
                    S.op('act', lambda e, o=wbf[ws][:, 0:m, :], i=stage[ss][:, 0:m, :]: e.copy(out=o, in_=i),
                         reads=[stage_b[ss]], writes=[wbf_b[ws]])
                else:
                    S.op('dve', lambda e, o=wbf[ws][:, 0:m, :], i=stage[ss][:, 0:m, :]: e.tensor_copy(out=o, in_=i),
                         reads=[stage_b[ss]], writes=[wbf_b[ws]])
                slot_of[(n_, mi, pi)] = ws

        for ni, n in enumerate(n_list):
            if pre_n is not None:
                pre_n(n)
            if not k_outer:
                assert R <= NW
                load_upto(ni * R + NW)
                slots = [[slot_of[(n, mi, pi)] for pi in range(len(pieces))] for mi in range(len(mats))]
            ng = len(groups)
            if k_outer:
                assert len(mats) == 1 and ng <= 2
                bks = []
                for _g in range(ng):
                    bks.append(st["gi"] % 4)
                    st["gi"] += 1
                cnt = 0
                for pi, (k0, m) in enumerate(pieces):
                    load_upto(ni * R + pi + NW)
                    ws = slot_of[(n, 0, pi)]
                    for kk in range(m):
                        first = (cnt == 0)
                        last = (cnt == nk - 1)
                        for gi_, (c0, cl) in enumerate(groups):
                            bk = bks[gi_]
                            kw = {}
                            ark = act_reads_k(k0) if act_reads_k is not None else []
                            if first:
                                kw["writes"] = [bank_b[bk]]
                                kw["dep_only"] = list(act_reads)
                            if kk == 0 and gi_ == 0:
                                kw["dep_only"] = kw.get("dep_only", []) + [wbf_b[ws]] + ark
                            if kk == m - 1 and gi_ == ng - 1:
                                kw["reg_only"] = [wbf_b[ws]] + ark
                            if last and gi_ == ng - 1:
                                kw["reg_only"] = kw.get("reg_only", []) + list(act_reads)
                            t = S.op('pe', lambda e, o=ps[:, bk, 0:cl], l=wbf[ws][:, kk, :], r=act(k0 + kk, c0, cl), f=first, la=last:
                                     e.matmul(o, lhsT=l, rhs=r, start=f, stop=la), **kw)
                            if last:
                                bank_b[bk].w = t
                                bank_b[bk].r = []
                        cnt += 1
                for gi_, (c0, cl) in enumerate(groups):
                    epilogue(n, gi_, [bks[gi_]], c0, cl)
                if post_n is not None:
                    post_n(n)
                continue
            ng = len(groups)
            for gi_, (c0, cl) in enumerate(groups):
                banks = []
                for mi in range(len(mats)):
                    bk = st["gi"] % 4
                    st["gi"] += 1
                    banks.append(bk)
                    tot = nk
                    cnt = 0
                    for pi, (k0, m) in enumerate(pieces):
                        ws = slots[mi][pi]
                        for kk in range(m):
                            first = (cnt == 0)
                            last = (cnt == tot - 1)
                            kw = {}
                            if first:
                                kw["writes"] = [bank_b[bk]]
                                kw["dep_only"] = list(act_reads)
                            if kk == 0:
                                kw["dep_only"] = kw.get("dep_only", []) + [wbf_b[ws]]
                            if kk == m - 1 and gi_ == ng - 1:
                                kw["reg_only"] = [wbf_b[ws]]
                            if last:
                                kw["reg_only"] = kw.get("reg_only", []) + list(act_reads)
                            t = S.op('pe', lambda e, o=ps[:, bk, 0:cl], l=wbf[ws][:, kk, :], r=act(k0 + kk, c0, cl), f=first, la=last:
                                     e.matmul(o, lhsT=l, rhs=r, start=f, stop=la), **kw)
                            if last:
                                bank_b[bk].w = t
                                bank_b[bk].r = []
                            cnt += 1
                epilogue(n, gi_, banks, c0, cl)
            if post_n is not None:
                post_n(n)

    def setup():
        S.op('dve', mk('memset', ones_f, 1.0), writes=[const_b])
        S.op('dve', mk('memset', ident_f, 0.0), writes=[const_b])
        S.op('pool', mk('affine_select', out=ident_f, in_=ones_f, pattern=[[-1, 128]], compare_op=ALU.is_equal,
                                               fill=0.0, base=0, channel_multiplier=1), writes=[const_b])
        S.op('dve', mk('tensor_copy', out=ident_bf, in_=ident_f), writes=[const_b])
        S.op('dve', mk('memset', junk, 0.0), writes=[const_b])
        S.op('dve', mk('memset', eps_ap, EPS), writes=[const_b])
        pdma(gains, G.gains_d, "c0", writes=[const_b])
        pdma(halo, G.halo_d, "c1", writes=[const_b])
        pdma(convw, G.convw_d, "c2", writes=[const_b])
        pdma(sinks, G.sinks_d, "c3", writes=[const_b])

    class ModStream:
        def __init__(self):
            top = ARENA - 42 * 1024
            self.top = top
            sav = A.off
            A.off = PH0 + 48 * 1024
            self.cT = A.get(KC * NR * 4, F32, (KC, NR))
            A.off = top
            self.scT = A.get(KC * NR * 2, BF16, (KC, NR))
            self.bada = A.get(288 * 4, F32)
            self.modT = A.get(288 * NR * 4, F32, (288, NR))
            self.mtok = [A.get(512 * 4, F32, None, parts=NR) for _ in range(2)]
            self.mst = [A.get(1024 * 4, F32, (2, 512)) for _ in range(2)]
            self.mwb = [A.get(1024 * 2, BF16, (2, 512)) for _ in range(4)]
            A.off = sav
            self.b_c, self.b_sc, self.b_bada, self.b_mod = Buf(), Buf(), Buf(), Buf()
            self.b_mtok = [Buf(), Buf()]
            self.b_mst = [Buf(), Buf()]
            self.b_mwb = [Buf() for _ in range(4)]
            self.loaded = 0
            self.g = 0
            pdma(self.cT, G.cT_d, "m0", writes=[self.b_c])
            pdma(self.bada, G.bada_d, "m1", writes=[self.b_bada])
            S.op('act', mk('activation', out=self.scT, in_=self.cT, func=AF.Silu), reads=[self.b_c], writes=[self.b_sc])

        def _load_upto(self, jmax):
            jmax = min(jmax, 72 * 16)
            while self.loaded < jmax:
                j = self.loaded
                self.loaded += 1
                g, t = divmod(j, 16)
                ss, ws = j % 2, j % 4
                src = G.wada_d[g][:, t * 1024:(t + 1) * 1024].rearrange("p (k j) -> p k j", k=2)
                S.op('sp', mk('dma_start', out=self.mst[ss], in_=src), writes=[self.b_mst[ss]], dma="stm%d" % ss)
                ceng = 'act' if (st["cast"] % 2 == 0) else 'dve'
                st["cast"] += 1
                S.op(ceng, mk('copy' if ceng == 'act' else 'tensor_copy', out=self.mwb[ws], in_=self.mst[ss]),
                     reads=[self.b_mst[ss]], writes=[self.b_mwb[ws]])

        def step(self):
            g = self.g
            self.g += 1
            bk = 4 + g % 2
            i = g % 2
            for t in range(16):
                j = g * 16 + t
                self._load_upto(j + 4)
                ws = j % 4
                for kk in range(2):
                    kw = {}
                    first = (t == 0 and kk == 0)
                    last = (t == 15 and kk == 1)
                    if first:
                        kw["writes"] = [bank_b[bk]]
                        kw["dep_only"] = [self.b_sc]
                    if kk == 0:
                        kw["dep_only"] = kw.get("dep_only", []) + [self.b_mwb[ws]]
                    else:
                        kw["reg_only"] = [self.b_mwb[ws]]
                    tk = S.op('pe', mk('matmul', ps[0:NR, bk, 0:512], lhsT=self.scT[:, 2 * t + kk, :], rhs=self.mwb[ws][:, kk, :],
                                       start=first, stop=last), **kw)
                    if last:
                        bank_b[bk].w = tk
                        bank_b[bk].r = []
            S.op('act', mk('copy', out=self.mtok[i], in_=ps[0:NR, bk, 0:512]), reads=[bank_b[bk]], writes=[self.b_mtok[i]])
            for c in range(4):
                kw = {"writes": [bank_b[6]]} if c == 0 else {}
                tk = S.op('pe', mk('transpose', ps[:, 6, c * NR:(c + 1) * NR], self.mtok[i][:, c * 128:(c + 1) * 128], ident_f[0:NR, 0:NR]),
                          dep_only=[self.b_mtok[i], const_b], reg_only=([self.b_mtok[i]] if c == 3 else []), **kw)
            bank_b[6].w = tk
            bank_b[6].r = []
            S.op('dve', mk('tensor_tensor', out=self.modT[:, 4 * g:4 * g + 4, :], in0=ps[:, 6, 0:4 * NR].rearrange("p (c r) -> p c r", c=4),
                           in1=self.bada[:, 4 * g:4 * g + 4].unsqueeze(2).broadcast_to([128, 4, NR]), op=ALU.add),
                 reads=[bank_b[6], self.b_bada], writes=[self.b_mod])

        def flush(self, j0, j1):
            pdma(G.mod_d[:, j0:j1, :, :], self.modT[:, j0 * KC:j1 * KC, :].rearrange("p (j c) r -> p j c r", j=j1 - j0), "m2", reads=[self.b_mod])

    def phase_relbias():
        A.off = PH0
        tab = A.get(16 * 4, F32, None, parts=33)
        tabrep = A.get(16 * 128 * 4, F32, (16, 128), parts=33)
        oneh = A.get(384 * 4, F32, None, parts=33)
        Rsb = A.get(16 * 384 * 4, F32, (16, 384))
        b_tab, b_oh, b_R = Buf(), Buf(), Buf()
        S.op('dve', mk('memset', tab[32:33, :], 1.0), writes=[b_tab])
        pdma(tab[0:32, :], G.relb_d, "m3", writes=[b_tab])
        pdma(oneh, G.onehot_d, "m4", writes=[b_oh])
        S.op('dve', mk('tensor_copy', out=tabrep, in_=tab[:, 0:16].unsqueeze(2).broadcast_to([33, 16, 128])),
             reads=[b_tab], writes=[b_tab])
        for h in range(16):
            bk = 4 + (h % 2)
            S.op('pe', mk('matmul', ps[:, bk, 0:384], lhsT=tabrep[:, h, :], rhs=oneh, start=True, stop=True),
                 reads=[b_tab, b_oh], writes=[bank_b[bk]])
            S.op('dve', mk('tensor_copy', out=Rsb[:, h, :], in_=ps[:, bk, 0:384]),
                 reads=[bank_b[bk]], writes=[b_R])
        pdma(G.R_d, Rsb, "m5", reads=[b_R])

    def load_par(s):
        pdma(par[:, 0:2], G.mod_d[:, 3 * s:3 * s + 2, :, :], "par", writes=[par_b])
        S.op('dve', mk('tensor_scalar', out=par[:, 1], in0=par[:, 1], scalar1=1.0, scalar2=None, op0=ALU.add),
             writes=[par_b])
        S.op('dve', mk('tensor_tensor', out=par[:, 1], in0=par[:, 1],
                                              in1=gains[:, s, :].unsqueeze(2).broadcast_to([128, KC, NR]), op=ALU.mult),
             writes=[par_b], reads=[const_b])

    def load_gate(s):
        coef = (0.5, 1.0, 0.5)[s]
        pdma(par[:, 2], G.mod_d[:, 3 * s + 2, :, :], "parg", writes=[parg_b])
        if coef != 1.0:
            S.op('dve', mk('tensor_scalar', out=par[:, 2], in0=par[:, 2], scalar1=coef, scalar2=None, op0=ALU.mult),
                 writes=[parg_b])

    def norm_stats(x_d, T, xbuf, xb_b, sq, sq_b, rstd, rstd_b, nxb=NXB):
        groups = col_groups(0, T)
        ssb = [4, 5, 6]
        for c in range(KC):
            pdma(xbuf[c % nxb][:, 0:T], x_d[c], "xb%d" % (c % nxb), writes=[xb_b[c % nxb]])
            S.op('act', mk('activation', out=sq[c % nxb][:, 0:T], in_=xbuf[c % nxb][:, 0:T], func=AF.Square),
                 reads=[xb_b[c % nxb]], writes=[sq_b[c % nxb]])
            for gi_, (c0, cl) in enumerate(groups):
                kw = {}
                if c == 0:
                    kw["writes"] = [bank_b[ssb[gi_]]]
                t = S.op('pe', lambda e, c=c, bk=ssb[gi_], c0=c0, cl=cl: e.matmul(ps[:, bk, 0:cl], lhsT=ones_f, rhs=sq[c % nxb][:, c0:c0 + cl],
                                                                                  start=(c == 0), stop=(c == KC - 1)),
                         dep_only=[sq_b[c % nxb], const_b], reg_only=[sq_b[c % nxb]], **kw)
                if c == KC - 1:
                    bank_b[ssb[gi_]].w = t
                    bank_b[ssb[gi_]].r = []
        for gi_, (c0, cl) in enumerate(groups):
            S.op('act', lambda e, bk=ssb[gi_], c0=c0, cl=cl: e.activation(out=rstd[:, c0:c0 + cl], in_=ps[:, bk, 0:cl], func=AF.Sqrt,
                                                                         scale=1.0 / D, bias=eps_ap),
                 reads=[bank_b[ssb[gi_]], const_b], writes=[rstd_b])
        S.op('dve', mk('reciprocal', out=rstd[:, 0:T], in_=rstd[:, 0:T]), writes=[rstd_b])

    def phase_norm(x_d, T, s, hT, h_b, nxb=NXB):
        SC = T - 128
        A.off = PH0 + KC * T0 * 2
        xbuf = [A.get(T0 * 4, F32) for _ in range(nxb)]
        sq = [A.get(T0 * 4, F32) for _ in range(nxb)]
        rstd = A.get(T0 * 4, F32)
        tmp = [A.get(T0 * 4, F32) for _ in range(2)]
        tmp3 = A.get(128 * 4, F32, (16, 8))
        xb_b = [Buf() for _ in range(nxb)]
        sq_b = [Buf() for _ in range(nxb)]
        tmp_b = [Buf(), Buf()]
        rstd_b, t3_b = Buf(), Buf()
        load_par(s)
        norm_stats(x_d, T, xbuf, xb_b, sq, sq_b, rstd, rstd_b, nxb)
        for c in range(KC):
            i = c % 2
            xi = c % nxb
            pdma(xbuf[xi][:, 0:T], x_d[c], "xb%d" % xi, writes=[xb_b[xi]])
            S.op('dve', mk('tensor_tensor', out=tmp[i][:, 0:T], in0=xbuf[xi][:, 0:T], in1=rstd[:, 0:T], op=ALU.mult),
                 reads=[xb_b[xi], rstd_b], writes=[tmp_b[i]])
            S.op('act', mk('activation', out=hT[:, c, 0:SC], in_=tmp[i][:, 0:SC], func=AF.Identity,
                                                         scale=par[:, 1, c, 0:1], bias=par[:, 0, c, 0:1]),
                 reads=[tmp_b[i], par_b], writes=[h_b])
            S.op('dve', mk('tensor_tensor', out=tmp3, in0=tmp[i][:, SC:T].rearrange("p (s t) -> p s t", t=8),
                                                            in1=par[:, 1, c, 1:NR].unsqueeze(2).broadcast_to([128, 16, 8]), op=ALU.mult),
                 reads=[tmp_b[i], par_b], writes=[t3_b])
            S.op('dve', mk('tensor_tensor', out=hT[:, c, SC:T].rearrange("p (s t) -> p s t", t=8), in0=tmp3,
                                                       in1=par[:, 0, c, 1:NR].unsqueeze(2).broadcast_to([128, 16, 8]), op=ALU.add),
                 reads=[t3_b, par_b], writes=[h_b])

    def make_resid(T, x_old_d, old_off, x_new_d, c0h, lenh, xo, xo_b, xn, xn_b, rt, rt_b, tag):
        SC = T - 128

        def pre_n(d):
            i = d % 2
            pdma(xo[i][:, 0:lenh], x_old_d[d][:, old_off + c0h:old_off + c0h + lenh], tag + "xo%d" % i, writes=[xo_b[i]])

        def epi(d, g, banks, c0, cl):
            i = d % 2
            bk = banks[0]
            lo, hi = c0, min(c0 + cl, SC)
            if hi > lo:
                S.op('dve', mk('scalar_tensor_tensor', out=xn[i][:, lo - c0h:hi - c0h], in0=ps[:, bk, lo - c0:hi - c0],
                                                              scalar=par[:, 2, d, 0:1], in1=xo[i][:, lo - c0h:hi - c0h],
                                                              op0=ALU.mult, op1=ALU.add),
                     reads=[bank_b[bk], xo_b[i], parg_b], writes=[xn_b[i]])
            lo, hi = max(c0, SC), c0 + cl
            if hi > lo:
                s0, s1 = (lo - SC) // 8, (hi - SC) // 8
                ns = s1 - s0
                S.op('dve', mk('tensor_tensor', out=rt[:, 0:ns, :], in0=ps[:, bk, lo - c0:hi - c0].rearrange("p (s t) -> p s t", t=8),
                                                      in1=par[:, 2, d, 1 + s0:1 + s1].unsqueeze(2).broadcast_to([128, ns, 8]), op=ALU.mult),
                     reads=[bank_b[bk], parg_b], writes=[rt_b])
                S.op('dve', mk('tensor_tensor', out=xn[i][:, lo - c0h:hi - c0h].rearrange("p (s t) -> p s t", t=8), in0=rt[:, 0:ns, :],
                                                      in1=xo[i][:, lo - c0h:hi - c0h].rearrange("p (s t) -> p s t", t=8), op=ALU.add),
                     reads=[rt_b, xo_b[i]], writes=[xn_b[i]])

        def post_n(d):
            i = d % 2
            pdma(x_new_d[d][:, c0h:c0h + lenh], xn[i][:, 0:lenh], tag + "xn%d" % i, reads=[xn_b[i]])

        return pre_n, epi, post_n

    def phase_ffn(w1_d, w3_d, w2_d, T, x_old_d, old_off, x_new_d, hT, h_b, sl, up_only=False, up_hook=None):
        A.off = PH0 + KC * T0 * 2
        aout = [A.get(T0 * 2, BF16) for _ in range(2)]
        sg = [A.get(512 * 4, F32) for _ in range(2)]
        ao_b = [Buf(), Buf()]
        sg_b = [Buf(), Buf()]
        groups = col_groups(0, T)
        cnt = {"g": 0}

        def epi_up(f, g, banks, c0, cl):
            j = cnt["g"] % 2
            cnt["g"] += 1
            i = f % 2
            S.op('act', mk('activation', out=sg[j][:, 0:cl], in_=ps[:, banks[0], 0:cl], func=AF.Silu),
                 reads=[bank_b[banks[0]]], writes=[sg_b[j]])
            S.op('dve', mk('tensor_tensor', out=aout[i][:, c0:c0 + cl], in0=sg[j][:, 0:cl], in1=ps[:, banks[1], 0:cl], op=ALU.mult),
                 reads=[sg_b[j], bank_b[banks[1]]], writes=[ao_b[i]])

        def post_up(f):
            i = f % 2
            pdma(G.aT_d[f][:, 0:T], aout[i][:, 0:T], "ao%d" % i, reads=[ao_b[i]])
            if up_hook is not None:
                up_hook(f)

        gemm([w1_d, w3_d], KC, range(FC), lambda k, c0, cl: hT[:, k, c0:c0 + cl], groups, epi_up, post_n=post_up, act_reads=[h_b])
        if up_hook is not None:
            up_hook(None)
        barrier()
        if up_only:
            return
        load_gate(sl)
        half = T // 2
        for hf in range(2):
            c0h = hf * half
            A.off = PH0
            aTh = A.get(FC * half * 2, BF16, (FC, half))
            xo = [A.get(half * 4, F32) for _ in range(2)]
            xn = [A.get(half * 4, F32) for _ in range(2)]
            rt = A.get(128 * 4, F32, (16, 8))
            a_bs = [Buf() for _ in range(6)]
            xo_b, xn_b, rt_b = [Buf(), Buf()], [Buf(), Buf()], Buf()
            f0 = 0
            qi = 0
            while f0 < FC:
                f1 = min(FC, f0 + 16)
                pdma(aTh[:, f0:f1, :], G.aT_d[f0:f1, :, c0h:c0h + half].rearrange("f p t -> p f t"), "ah%d" % qi, writes=[a_bs[qi]],
                     deps=([a_bs[qi - 1].w] if (qi > 0 and os.environ.get('KSERA')) else []))
                f0 = f1
                qi += 1
            pre_n, epi, post_n = make_resid(T, x_old_d, old_off, x_new_d, c0h, half, xo, xo_b, xn, xn_b, rt, rt_b, "d")
            gemm([w2_d], FC, range(KC), lambda k, c0, cl: aTh[:, k, c0 - c0h:c0 - c0h + cl], col_groups(c0h, half), epi,
                 pre_n=pre_n, post_n=post_n, k_outer=True, act_reads_k=lambda k0_: [a_bs[k0_ // 16]])
            barrier()

    def phase_mixer(hT, h_b, stop=None):
        A.off = PH0 + KC * T0 * 2
        kT = A.get(4 * T0 * 2, BF16, (4, T0))
        vT = A.get(4 * T0 * 2, BF16, (4, T0))
        vtok = A.get(9 * 4 * 128 * 2, BF16, (9, 4, 128))
        kf = A.get(4 * 256 * 4, F32, (4, 256))
        vf = A.get(4 * 256 * 4, F32, (4, 256))
        mixb = [A.get(T1 * 2, BF16) for _ in range(2)]
        M1 = A.off
        stT = A.get(16 * 16 * 2 * 4, F32, (16, 16, 2))
        convo = A.get(16 * NR * 2 * 4, F32, (16, NR, 2))
        gcb = A.get(T0 * 4, F32)
        ub = A.get(T0 * 4, F32)
        yc = A.get(T1 * 4, F32)
        ucat = A.get(160 * 4, F32, (16, 10))
        k_b, v_b, vt_b, kf_b, vf_b = Buf(), Buf(), Buf(), Buf(), Buf()
        st_b, co_b, gc_b, u_b, yc_b, uc_b = Buf(), Buf(), Buf(), Buf(), Buf(), Buf()
        mb_b = [Buf(), Buf()]
        pdma(stT, G.stT_d, "x0", writes=[st_b])
        pdma(G.kwsc_d, G.ck_d[:, 8:128, :], "o0")
        pdma(G.vwsc_d, G.cv_d[:, 8:128, :], "o1")

        groups = col_groups(0, T0)
        act = lambda k, c0, cl: hT[:, k, c0:c0 + cl]

        def epi_conv(n, g, banks, c0, cl):
            bk = banks[0]
            if 40 <= n < 56:
                S.op('act', mk('copy', out=gcb[:, c0:c0 + cl], in_=ps[:, bk, 0:cl]), reads=[bank_b[bk]], writes=[gc_b])
            elif n >= 56:
                c = n - 56
                S.op('dve', mk('tensor_tensor', out=ub[:, c0:c0 + cl], in0=gcb[:, c0:c0 + cl], in1=ps[:, bk, 0:cl], op=ALU.mult),
                     reads=[bank_b[bk], gc_b], writes=[u_b])
                if g == len(groups) - 1:
                    S.op('dve', mk('tensor_scalar', out=ub[:, 0:128], in0=ub[:, 0:128], scalar1=halo[:, 0:1], scalar2=None, op0=ALU.mult),
                         reads=[const_b], writes=[u_b])
                    S.op('dve', mk('tensor_scalar', out=yc[:, 0:1024], in0=ub[:, 126:1150], scalar1=convw[:, c, 0:1], scalar2=None, op0=ALU.mult),
                         reads=[u_b, const_b], writes=[yc_b])
                    for j in (1, 2):
                        S.op('dve', mk('scalar_tensor_tensor', out=yc[:, 0:1024], in0=ub[:, 126 + j:1150 + j], scalar=convw[:, c, j:j + 1],
                                                                          in1=yc[:, 0:1024], op0=ALU.mult, op1=ALU.add),
                             reads=[u_b, const_b], writes=[yc_b])
                    us = ub[:, 1152:1280].rearrange("p (s t) -> p s t", t=8)
                    S.op('dve', mk('tensor_copy', out=ucat[:, :, 0:2], in_=stT[:, c, :, :]), reads=[st_b], writes=[uc_b])
                    S.op('dve', mk('tensor_copy', out=ucat[:, :, 2:10], in_=us), reads=[u_b], writes=[uc_b])
                    ys = yc[:, 1024:1152].rearrange("p (s t) -> p s t", t=8)
                    S.op('dve', mk('tensor_scalar', out=ys, in0=ucat[:, :, 0:8], scalar1=convw[:, c, 0:1], scalar2=None, op0=ALU.mult),
                         reads=[uc_b, const_b], writes=[yc_b])
                    for j in (1, 2):
                        S.op('dve', mk('scalar_tensor_tensor', out=ys, in0=ucat[:, :, j:j + 8], scalar=convw[:, c, j:j + 1], in1=ys,
                                                                          op0=ALU.mult, op1=ALU.add),
                             reads=[uc_b, const_b], writes=[yc_b])
                    S.op('act', mk('copy', out=convo[:, c, 0, :], in_=ub[:, 1150:1152]), reads=[u_b], writes=[co_b])
                    S.op('act', mk('copy', out=convo[:, c, 1:NR, :], in_=us[:, :, 6:8]), reads=[u_b], writes=[co_b])
            else:
                c = n - 24
                i = c % 2
                lo, hi = max(c0, 128), c0 + cl
                S.op('dve', mk('tensor_tensor', out=mixb[i][:, lo - 128:hi - 128], in0=yc[:, lo - 128:hi - 128], in1=ps[:, bk, lo - c0:hi - c0], op=ALU.mult),
                     reads=[bank_b[bk], yc_b], writes=[mb_b[i]])
                if g == len(groups) - 1:
                    pdma(G.mix_d[16 + c], mixb[i], "mx%d" % i, reads=[mb_b[i]])

        order = []
        for c in range(16):
            order += [40 + c, 56 + c, 24 + c]
        gemm([G.win_d], KC, order, act, groups, epi_conv, act_reads=[h_b])
        pdma(G.convo_d, convo, "o2", reads=[co_b])
        barrier()
        if stop == "mix_conv":
            return
        A.off = M1
        qb = [A.get(T0 * 2, BF16) for _ in range(2)]
        qTs = A.get(16 * 128 * 2, BF16, (16, 16, 8))
        mixS = A.get(16 * 128 * 2, BF16, (16, 128))
        biasP = [A.get(256 * 4, F32) for _ in range(2)]
        biasS = A.get(4 * 136 * 4, F32, (4, 136), parts=32)
        sinkS = A.get(16, F32, None, parts=32)
        s_sb = [A.get(256 * 4, F32) for _ in range(2)]
        p_sb = [A.get(256 * 4, F32) for _ in range(2)]
        pn_sb = [A.get(256 * 2, BF16) for _ in range(2)]
        pT_sb = [A.get(256 * 2, BF16, (2, 128)) for _ in range(2)]
        small = [A.get(8 * 4, F32) for _ in range(2)]
        kc_f = A.get(512 * 4, F32)
        vc_f = A.get(512 * 4, F32)
        kc_b = A.get(512 * 2, BF16)
        vc_b = A.get(512 * 2, BF16)
        kcT = A.get(512 * 2, BF16, (4, 128))
        vnw = A.get(512 * 2, BF16, (4, 128), parts=8)
        pT2 = A.get(32 * 2, BF16, None, parts=8)
        q_b = [Buf(), Buf()]
        qs_b, ms_b = Buf(), Buf()
        bp_b = [Buf(), Buf()]
        bs_b = Buf()
        s_b, p_b, pn_b, pt_b, sm_b = [Buf(), Buf()], [Buf(), Buf()], [Buf(), Buf()], [Buf(), Buf()], [Buf(), Buf()]
        kcf_b, vcf_b, kcb_b, vcb_b, kct_b, vnw_b, pt2_b = Buf(), Buf(), Buf(), Buf(), Buf(), Buf(), Buf()
        if "sink" not in os.environ.get("KSKIP", ""):
            pdma(sinkS, G.sinkS_d, "x1", writes=[bs_b])
        SK = os.environ.get("KSKIP", "")
        for kv in range(4 if "bias" not in SK else 0):
            for g in range(4):
                src = bass.AP(tensor=G.R_d.tensor, offset=(4 * kv + g) * 384 + 127, ap=[[16 * 384 - 1, 8], [1, 136]])
                pdma(biasS[8 * g:8 * g + 8, kv, :], src, "x2", writes=[bs_b])

        def epi_kv(n, g, banks, c0, cl):
            bk = banks[0]
            isk = n < 20
            kv = n - 16 if isk else n - 20
            dst, db = (kT, k_b) if isk else (vT, v_b)
            ff, fb = (kf, kf_b) if isk else (vf, vf_b)
            S.op('act', mk('copy', out=dst[:, kv, c0:c0 + cl], in_=ps[:, bk, 0:cl]), reads=[bank_b[bk]], writes=[db])
            if g == 2:
                S.op('act', mk('copy', out=ff[:, kv, :], in_=ps[:, bk, 0:256]), reads=[bank_b[bk]], writes=[fb])

        gemm([G.win_d], KC, list(range(16, 24)), act, groups, epi_kv, act_reads=[h_b])
        if "kvout" not in SK:
          pdma(G.kwin_d.rearrange("k p t -> p k t"), kf[:, :, 0:128], "o3", reads=[kf_b])
        if "kvout" not in SK:
          pdma(G.knew_d.rearrange("k p t -> p k t"), kf[:, :, 128:256], "o4", reads=[kf_b])
        if "kvout" not in SK:
          pdma(G.vwin_d.rearrange("k p t -> p k t"), vf[:, :, 0:128], "o5", reads=[vf_b])
        if "kvout" not in SK:
          pdma(G.vnew_d.rearrange("k p t -> p k t"), vf[:, :, 128:256], "o6", reads=[vf_b])
        for bb in range(9 if "vtok" not in SK else 0):
            bk = 4 + bb % 2
            pv = ps[:, bk, :].bitcast(BF16)
            for kv in range(4):
                S.op('pe', mk('transpose', pv[:, kv * 128:(kv + 1) * 128], vT[:, kv, bb * 128:(bb + 1) * 128], ident_bf),
                     reads=[v_b, const_b], writes=[bank_b[bk]])
            S.op('act', mk('copy', out=vtok[:, bb, :, :], in_=pv[:, 0:512].rearrange("p (k d) -> p k d", k=4)),
                 reads=[bank_b[bk]], writes=[vt_b])

        if stop == "mix_kv":
            return
        def softmax_rows(P, W, sc_bk, bias_ap, bias_bufs, sink_ap, j, halo_fix):
            s_, p_, pn_, sm = s_sb[j], p_sb[j], pn_sb[j], small[j]
            S.op('dve', mk('scalar_tensor_tensor', out=s_[0:P, 0:W], in0=ps[0:P, sc_bk, 0:W], scalar=SCALE, in1=bias_ap,
                                                          op0=ALU.mult, op1=ALU.add),
                 reads=[bank_b[sc_bk]] + bias_bufs, writes=[s_b[j]])
            if halo_fix:
                S.op('dve', mk('tensor_scalar', out=s_[0:P, 0:128], in0=s_[0:P, 0:128], scalar1=halo[0:P, 1:2], scalar2=None, op0=ALU.add),
                     reads=[const_b], writes=[s_b[j]])
            S.op('dve', mk('tensor_reduce', out=sm[0:P, 0:1], in_=s_[0:P, 0:W], op=ALU.max, axis=AX.X), reads=[s_b[j]], writes=[sm_b[j]])
            S.op('dve', mk('tensor_scalar', out=sm[0:P, 1:2], in0=sm[0:P, 0:1], scalar1=sink_ap, scalar2=-1.0, op0=ALU.max, op1=ALU.mult),
                 reads=bias_bufs, writes=[sm_b[j]])
            S.op('act', mk('activation', out=p_[0:P, 0:W], in_=s_[0:P, 0:W], func=AF.Exp, bias=sm[0:P, 1:2], scale=1.0,
                                               accum_out=sm[0:P, 2:3]),
                 reads=[s_b[j], sm_b[j]], writes=[p_b[j], sm_b[j]])
            S.op('act', mk('activation', out=sm[0:P, 3:4], in_=sm[0:P, 1:2], func=AF.Exp, bias=sink_ap, scale=1.0),
                 reads=bias_bufs, writes=[sm_b[j]])
            S.op('dve', mk('tensor_tensor', out=sm[0:P, 4:5], in0=sm[0:P, 2:3], in1=sm[0:P, 3:4], op=ALU.add), writes=[sm_b[j]])
            S.op('dve', mk('reciprocal', out=sm[0:P, 5:6], in_=sm[0:P, 4:5]), writes=[sm_b[j]])
            S.op('dve', mk('tensor_scalar', out=pn_[0:P, 0:W], in0=p_[0:P, 0:W], scalar1=sm[0:P, 5:6], scalar2=None, op0=ALU.mult),
                 reads=[p_b[j], sm_b[j]], writes=[pn_b[j]])

        ucnt = {"u": 0}

        def attn_head(h):
            kv = h // 4
            i = h % 2
            hb = h % 2
            src = bass.AP(tensor=G.R_d.tensor, offset=h * 384 + 127, ap=[[16 * 384 - 1, 128], [1, 256]])
            pdma(biasP[hb], src, "bp%d" % hb, writes=[bp_b[hb]])
            S.op('dve', mk('tensor_copy', out=qTs[:, :, h, :], in_=qb[i][:, 1152:1280].rearrange("p (s t) -> p s t", t=8)), reads=[q_b[i]], writes=[qs_b])

            def qk(bb):
                bk = 4 + (ucnt["u"] % 2)
                S.op('pe', mk('matmul', ps[:, bk, 0:256], lhsT=qb[i][:, bb * 128:(bb + 1) * 128], rhs=kT[:, kv, (bb - 1) * 128:(bb + 1) * 128],
                                              start=True, stop=True),
                     reads=[q_b[i], k_b], writes=[bank_b[bk]])
                return bk

            nxt = qk(1)
            for bb in range(1, 9):
                j = ucnt["u"] % 2
                ucnt["u"] += 1
                sc_bk = nxt
                softmax_rows(128, 256, sc_bk, biasP[hb], [bp_b[hb], const_b], sinks[:, h:h + 1], j, bb == 1)
                if bb < 8:
                    nxt = qk(bb + 1)
                ptv = ps[:, 6, :].bitcast(BF16)
                for t in range(2):
                    S.op('pe', mk('transpose', ptv[:, t * 128:(t + 1) * 128], pn_sb[j][:, t * 128:(t + 1) * 128], ident_bf),
                         reads=[pn_b[j], const_b], writes=[bank_b[6]])
                S.op('act', mk('copy', out=pT_sb[j], in_=ptv[:, 0:256].rearrange("p (t q) -> p t q", t=2)),
                     reads=[bank_b[6]], writes=[pt_b[j]])
                for t in range(2):
                    S.op('pe', mk('matmul', ps[:, 7, 0:128], lhsT=vtok[:, bb - 1 + t, kv, :], rhs=pT_sb[j][:, t, :], start=(t == 0), stop=(t == 1)),
                         reads=[pt_b[j], vt_b], writes=([bank_b[7]] if t == 0 else []))
                bank_b[7].w = ('c', 'pe', len(S.ops['pe']) - 1)
                bank_b[7].r = []
                S.op('act', mk('copy', out=mixb[i][:, (bb - 1) * 128:bb * 128], in_=ps[:, 7, 0:128]), reads=[bank_b[7]], writes=[mb_b[i]])

        def epi_q(n, g, banks, c0, cl):
            i = n % 2
            S.op('act', mk('copy', out=qb[i][:, c0:c0 + cl], in_=ps[:, banks[0], 0:cl]), reads=[bank_b[banks[0]]], writes=[q_b[i]])

        def post_q(h):
            attn_head(h)
            pdma(G.mix_d[h][:, 0:1024], mixb[h % 2][:, 0:1024], "mx%d" % (h % 2), reads=[mb_b[h % 2]])

        gemm([G.win_d], KC, list(range(16)), act, groups, epi_q, post_n=post_q, act_reads=[h_b])

        if stop == "mix_attn":
            return
        for s in range(16):
            pdma(kc_f, G.ck_d[s], "sk", writes=[kcf_b])
            pdma(vc_f, G.cv_d[s], "sv", writes=[vcf_b])
            S.op('act', mk('copy', out=kc_b, in_=kc_f), reads=[kcf_b], writes=[kcb_b])
            S.op('dve', mk('tensor_copy', out=vc_b, in_=vc_f), reads=[vcf_b], writes=[vcb_b])
            bk = 4 + s % 2
            pv = ps[:, bk, :].bitcast(BF16)
            for kv in range(4):
                S.op('pe', mk('transpose', pv[:, kv * 128:(kv + 1) * 128], kc_b[:, kv * 128:(kv + 1) * 128], ident_bf),
                     reads=[kcb_b, const_b], writes=[bank_b[bk]])
            S.op('act', mk('copy', out=kcT, in_=pv[:, 0:512].rearrange("p (k d) -> p k d", k=4)), reads=[bank_b[bk]], writes=[kct_b])
            pv2 = ps[0:8, 6, :].bitcast(BF16)
            for kv in range(4):
                S.op('pe', mk('transpose', pv2[:, kv * 128:(kv + 1) * 128], vT[:, kv, 1152 + 8 * s:1160 + 8 * s], ident_bf),
                     reads=[v_b, const_b], writes=[bank_b[6]])
            S.op('act', mk('copy', out=vnw, in_=pv2[:, 0:512].rearrange("p (k d) -> p k d", k=4)), reads=[bank_b[6]], writes=[vnw_b])
            for kv in range(4):
                j = ucnt["u"] % 2
                ucnt["u"] += 1
                sbk = 4 + j
                ql = qTs[:, s, 4 * kv:4 * kv + 4, :].rearrange("p g t -> p (g t)")
                S.op('pe', mk('matmul', ps[0:32, sbk, 0:128], lhsT=ql, rhs=kcT[:, kv, :], start=True, stop=True),
                     reads=[qs_b, kct_b], writes=[bank_b[sbk]])
                S.op('pe', mk('matmul', ps[0:32, sbk, 128:136], lhsT=ql, rhs=kT[:, kv, 1152 + 8 * s:1160 + 8 * s], start=True, stop=True),
                     reads=[qs_b, k_b], writes=[bank_b[sbk]])
                softmax_rows(32, 136, sbk, biasS[:, kv, :], [bs_b], sinkS[:, kv:kv + 1], j, False)
                ptv = ps[:, 6, :].bitcast(BF16)
                S.op('pe', mk('transpose', ptv[:, 0:32], pn_sb[j][0:32, 0:128], ident_bf[0:32, 0:32]),
                     reads=[pn_b[j], const_b], writes=[bank_b[6]])
                S.op('pe', mk('transpose', ptv[0:8, 32:64], pn_sb[j][0:32, 128:136], ident_bf[0:32, 0:32]),
                     reads=[pn_b[j], const_b], writes=[bank_b[6]])
                S.op('act', mk('copy', out=pT_sb[j][:, 0, 0:32], in_=ptv[:, 0:32]), reads=[bank_b[6]], writes=[pt_b[j]])
                S.op('act', mk('copy', out=pT2, in_=ptv[0:8, 32:64]), reads=[bank_b[6]], writes=[pt2_b])
                S.op('pe', mk('matmul', ps[:, 7, 0:32], lhsT=vc_b[:, kv * 128:(kv + 1) * 128], rhs=pT_sb[j][:, 0, 0:32], start=True, stop=False),
                     reads=[vcb_b, pt_b[j]], writes=[bank_b[7]])
                S.op('pe', mk('matmul', ps[:, 7, 0:32], lhsT=vnw[:, kv, :], rhs=pT2, start=False, stop=True),
                     reads=[vnw_b, pt2_b], writes=[bank_b[7]])
                S.op('act', mk('copy', out=mixS[:, 4 * kv:4 * kv + 4, 8 * s:8 * s + 8], in_=ps[:, 7, 0:32].rearrange("p (g t) -> p g t", g=4)),
                     reads=[bank_b[7]], writes=[ms_b])
        pdma(G.mix_d[0:16, :, 1024:1152].rearrange("h p t -> p h t"), mixS, "o7", reads=[ms_b])
        barrier()

    def phase_outproj(hT, h_b):
        A.off = PH0
        mixed = A.get(KC * T1 * 2, BF16, (KC, T1))
        xo = [A.get(T1 * 4, F32) for _ in range(2)]
        xn = [A.get(T1 * 4, F32) for _ in range(2)]
        rt = A.get(128 * 4, F32, (16, 8))
        m_b = Buf()
        xo_b, xn_b, rt_b = [Buf(), Buf()], [Buf(), Buf()], Buf()
        for q in range(4):
            pdma(mixed[:, 8 * q:8 * q + 8, :], G.mix_d[8 * q:8 * q + 8].rearrange("c p t -> p c t"), "ml%d" % q, writes=[m_b])
        load_gate(1)
        pre_n, epi, post_n = make_resid(T1, G.x1_d, 128, G.x2_d, 0, T1, xo, xo_b, xn, xn_b, rt, rt_b, "w")
        gemm([G.wout_d], KC, range(KC), lambda k, c0, cl: mixed[:, k, c0:c0 + cl], col_groups(0, T1), epi, pre_n=pre_n, post_n=post_n, act_reads=[m_b])
        barrier()

    def phase_final():
        A.off = PH0
        xbuf = [A.get(T0 * 4, F32) for _ in range(NXB)]
        sq = [A.get(T0 * 4, F32) for _ in range(NXB)]
        rstd = A.get(T0 * 4, F32)
        yb = [A.get(T0 * 4, F32) for _ in range(NXB)]
        xb_b, sq_b, yb_b = [Buf() for _ in range(NXB)], [Buf() for _ in range(NXB)], [Buf() for _ in range(NXB)]
        rstd_b = Buf()
        norm_stats(G.x3_d, T1, xbuf, xb_b, sq, sq_b, rstd, rstd_b)
        for c in range(KC):
            i = c % NXB
            pdma(xbuf[i][:, 0:T1], G.x3_d[c], "xb%d" % i, writes=[xb_b[i]])
            S.op('dve', mk('scalar_tensor_tensor', out=yb[i][:, 0:T1], in0=xbuf[i][:, 0:T1], scalar=gains[:, 3, c:c + 1],
                                                                   in1=rstd[:, 0:T1], op0=ALU.mult, op1=ALU.mult),
                 reads=[xb_b[i], rstd_b, const_b], writes=[yb_b[i]])
            t = pdma(G.yT_d[c], yb[i][:, 0:T1], "yo%d" % i, reads=[yb_b[i]])

    hT = arena[:, PH0:PH0 + KC * T0 * 2].bitcast(BF16).rearrange("p (c t) -> p c t", c=KC)
    hT1 = arena[:, PH0:PH0 + KC * T1 * 2].bitcast(BF16).rearrange("p (c t) -> p c t", c=KC)
    h_b = Buf()
    def _run_phases():
        setup()
        if stop_after == "setup":
            pdma(G.dbg_c_d, ident_f, "dbgc", reads=[const_b])
            return
        phase_relbias()
        ms = ModStream()
        for _ in range(16):
            ms.step()
        ms.flush(0, 2)
        barrier()
        if stop_after == "mod":
            for _ in range(56):
                ms.step()
            ms.flush(2, 9)
            return
        phase_norm(G.xT_d, T0, 0, hT, h_b, nxb=2)
        if stop_after == "norm1":
            pdma(G.dbg_h_d, hT, "dbgh", reads=[h_b])
            return

        def up_hook(f):
            if f is None:
                while ms.g < 72:
                    ms.step()
                ms.flush(2, 9)
            elif ms.g < 72 and (f % 3 != 2):
                ms.step()

        phase_ffn(G.w1a_d, G.w3a_d, G.w2a_d, T0, G.xT_d, 0, G.x1_d, hT, h_b, 0, up_only=(stop_after == "ffn1up"), up_hook=up_hook)
        if stop_after in ("ffn1up", "ffn1"):
            return
        phase_norm(G.x1_d, T0, 1, hT, h_b, nxb=2)
        phase_mixer(hT, h_b, stop_after)
        if stop_after in ("mixer", "mix_conv", "mix_kv", "mix_attn"):
            return
        phase_outproj(hT, h_b)
        if stop_after == "outproj":
            return
        phase_norm(G.x2_d, T1, 2, hT1, h_b, nxb=2)
        phase_ffn(G.w1b_d, G.w3b_d, G.w2b_d, T1, G.x2_d, 0, G.x3_d, hT1, h_b, 2)
        phase_final()

    _run_phases()
    barrier()

    sem_cm = {e: nc.semaphore("s_" + e) for e in ENG}
    sems = {e: cm.__enter__() for e, cm in sem_cm.items()}
    dkeys = sorted(S.dma_cnt.keys())
    dcm = {k: nc.semaphore("d_" + k) for k in dkeys}
    dma_sems = {k: cm.__enter__() for k, cm in dcm.items()}
    if os.environ.get('KSIM'):
        S.simulate()
    with nc.Block() as block:
        S.emit(nc, block, sems, dma_sems)
    for cm in list(dcm.values()) + list(sem_cm.values()):
        cm.__exit__(None, None, None)
    psum_cm.__exit__(None, None, None)
    arena_cm.__exit__(None, None, None)
    return nc


def _wl(w):
    K, N = w.shape
    return np.ascontiguousarray(w.reshape(K // 128, 128, N // 128, 128).transpose(2, 1, 0, 3)).reshape(N // 128, 128, K)


def _fm(a):
    r, f = a.shape
    return np.ascontiguousarray(a.T).reshape(f // 128, 128, r)


_PROG = {}


def make_in_maps(x_prompt, x_sample, c_prompt, c_sample, cache_k, cache_v, state_conv, rel_bias,
           g_ffn1, w1_ffn1, w3_ffn1, w2_ffn1, g_mix, w_in, sinks, conv_w, w_out,
           g_ffn2, w1_ffn2, w3_ffn2, w2_ffn2, w_ada, b_ada, g_final):
    f32 = np.float32
    A_ = lambda a: np.asarray(a, dtype=f32)
    xp = A_(x_prompt)[0]
    xs = A_(x_sample)
    cp, cs = A_(c_prompt), A_(c_sample)
    ck, cv, sc = A_(cache_k)[0], A_(cache_v)[0], A_(state_conv)[0]
    shared = {
        "badaT": np.ascontiguousarray(A_(b_ada)[0].reshape(288, 128).T),
        "w_ada": np.ascontiguousarray(A_(w_ada)[0].reshape(KC, 128, 72, 512).transpose(2, 1, 0, 3)).reshape(72, 128, KC * 512),
        "w1a": _wl(A_(w1_ffn1)[0]), "w3a": _wl(A_(w3_ffn1)[0]), "w2a": _wl(A_(w2_ffn1)[0]),
        "w_in": _wl(A_(w_in)[0]), "w_out": _wl(A_(w_out)[0]),
        "w1b": _wl(A_(w1_ffn2)[0]), "w3b": _wl(A_(w3_ffn2)[0]), "w2b": _wl(A_(w2_ffn2)[0]),
        "gains": np.ascontiguousarray(np.stack([A_(g_ffn1)[0], A_(g_mix)[0], A_(g_ffn2)[0], A_(g_final)]).reshape(4, KC, 128).transpose(2, 0, 1)),
        "convw": np.ascontiguousarray(A_(conv_w)[0].reshape(3, 16, 128).transpose(2, 1, 0)),
        "sinks_bc": np.ascontiguousarray(np.broadcast_to(A_(sinks)[0][None, :], (128, 16))),
        "sinkS": np.ascontiguousarray(np.repeat(A_(sinks)[0].reshape(4, 4).T, 8, axis=0)),
        "relb": np.ascontiguousarray(A_(rel_bias)),
    }
    m = np.arange(384)
    dist = 255 - m
    valid = (dist >= 0) & (dist <= 128)
    bucket = t5_bucket_np(dist)
    oh = np.zeros((33, 384), f32)
    oh[bucket[valid], m[valid]] = 1.0
    oh[32, ~valid] = NEG
    shared["onehot"] = oh

    in_maps = []
    for i in range(NCORES):
        p0 = 1024 * i
        halo_rows = xp[p0 - 128:p0] if i > 0 else xp[0:128]
        rows = np.concatenate([halo_rows, xp[p0:p0 + 1024], xs[16 * i:16 * i + 16].reshape(128, D)], axis=0)
        crow = np.concatenate([cp[0:1], cs[16 * i:16 * i + 16]], axis=0)
        mp = dict(shared)
        mp["xT"] = _fm(rows)
        mp["cT"] = np.ascontiguousarray(_fm(crow).transpose(1, 0, 2))
        mp["halo"] = np.ascontiguousarray(np.broadcast_to(np.array([[0.0, NEG]] if i == 0 else [[1.0, 0.0]], f32), (128, 2)))
        mp["stT"] = np.ascontiguousarray(sc[16 * i:16 * i + 16].reshape(16, 2, 16, 128).transpose(3, 2, 0, 1))
        mp["ck"] = np.ascontiguousarray(ck[16 * i:16 * i + 16].reshape(16, 128, 512))
        mp["cv"] = np.ascontiguousarray(cv[16 * i:16 * i + 16].reshape(16, 128, 512))
        in_maps.append(mp)

    return in_maps


def kernel(**inputs):
    f32 = np.float32
    in_maps = make_in_maps(**inputs)
    if "nc" not in _PROG:
        _PROG["nc"] = build_program()
    res = run_bass_kernel_spmd(_PROG["nc"], in_maps, core_ids=list(range(NCORES)))
    R = res.results

    def tm(a):
        C, P, T = a.shape
        return np.ascontiguousarray(a.reshape(C * P, T).T)

    y_prompt = np.empty((1, 8192, D), f32)
    y_sample = np.empty((128, 8, D), f32)
    k_ws = np.empty((1, 128, 128, 4, 128), f32)
    v_ws = np.empty((1, 128, 128, 4, 128), f32)
    conv_s = np.empty((1, 128, 2, 2048), f32)
    for i in range(NCORES):
        y = tm(R[i]["yT"])
        y_prompt[0, 1024 * i:1024 * i + 1024] = y[0:1024]
        y_sample[16 * i:16 * i + 16] = y[1024:1152].reshape(16, 8, D)
        kn = tm(R[i]["knewT"]).reshape(16, 8, 4, 128)
        vn = tm(R[i]["vnewT"]).reshape(16, 8, 4, 128)
        k_ws[0, 16 * i:16 * i + 16, 0:120] = R[i]["kwsc"].reshape(16, 120, 4, 128)
        k_ws[0, 16 * i:16 * i + 16, 120:128] = kn
        v_ws[0, 16 * i:16 * i + 16, 0:120] = R[i]["vwsc"].reshape(16, 120, 4, 128)
        v_ws[0, 16 * i:16 * i + 16, 120:128] = vn
        co = R[i]["convo"]
        conv_s[0, 16 * i:16 * i + 16] = co[:, :, 1:, :].transpose(2, 3, 1, 0).reshape(16, 2, 2048)
    last = R[NCORES - 1]
    k_wp = tm(last["kwinT"]).reshape(1, 1, 128, 4, 128)
    v_wp = tm(last["vwinT"]).reshape(1, 1, 128, 4, 128)
    conv_p = np.ascontiguousarray(last["convo"][:, :, 0, :].transpose(2, 1, 0)).reshape(1, 1, 2, 2048)
    return (y_prompt, y_sample, k_wp, v_wp, conv_p, k_ws, v_ws, conv_s)
```

```python
import math
import os
import numpy as np
import concourse.bass as bass
import concourse.mybir as mybir
from concourse.bass_utils import run_bass_kernel_spmd

F32 = mybir.dt.float32
BF16 = mybir.dt.bfloat16
U8 = mybir.dt.uint8
AF = mybir.ActivationFunctionType
ALU = mybir.AluOpType
AX = mybir.AxisListType

NCORES = 8
D = 4096
KC = 32
DFF = 11008
FC = 86
T0 = 1280
T1 = 1152
NR = 17
PIECE = 8
NXB = int(os.environ.get('KNXB', '4'))
NS = 4
NW = 10
EPS = 1e-6
NEG = -1e30
SCALE = 128 ** -0.5
ENG = ('pe', 'act', 'dve', 'pool', 'sp')


class Buf:
    __slots__ = ('w', 'r')

    def __init__(self):
        self.w = None
        self.r = []


class Sched:
    def __init__(self):
        self.ops = {e: [] for e in ENG}
        self.dma_cnt = {}
        self.out_tickets = []

    def op(self, eng, fn, reads=(), writes=(), deps=(), dma=None, dep_only=(), reg_only=()):
        d = list(deps)
        for b in list(reads) + list(dep_only):
            if b.w is not None:
                d.append(b.w)
        for b in writes:
            if b.w is not None:
                d.append(b.w)
            d.extend(b.r)
        idx = len(self.ops[eng])
        if dma is not None:
            n = self.dma_cnt.get(dma, 0) + 1
            self.dma_cnt[dma] = n
            t = ('d', dma, 16 * n)
        else:
            t = ('c', eng, idx)
        self.ops[eng].append([fn, d, False, dma])
        for b in list(reads) + list(reg_only):
            b.r.append(t)
        for b in writes:
            b.w = t
            b.r = []
        return t


    def simulate(self):
        for e in ENG:
            for o in self.ops[e]:
                for t in o[1]:
                    if t[0] == 'c' and not (t[1] == 'pe' and e == 'pe'):
                        assert self.ops[t[1]][t[2]][3] is None, ("compute ticket on a DMA op", e, t)
                        self.ops[t[1]][t[2]][2] = True
        cum = {}
        for e in ENG:
            c = 0
            arr = []
            for o in self.ops[e]:
                if o[2] and o[3] is None:
                    c += 1
                arr.append(c)
            cum[e] = arr
        semc = {e: 0 for e in ENG}
        semd = {}
        pc = {e: 0 for e in ENG}
        progress = True
        while progress:
            progress = False
            for e in ENG:
                while pc[e] < len(self.ops[e]):
                    fn, deps, inc, dma = self.ops[e][pc[e]]
                    ok = True
                    for t in deps:
                        if t[0] == 'c':
                            if t[1] == 'pe' and e == 'pe':
                                continue
                            if semc[t[1]] < cum[t[1]][t[2]]:
                                ok = False
                                break
                        else:
                            if semd.get(t[1], 0) < t[2]:
                                ok = False
                                break
                    if not ok:
                        break
                    if dma is not None:
                        semd[dma] = semd.get(dma, 0) + 16
                    elif inc:
                        semc[e] += 1
                    pc[e] += 1
                    progress = True
        stuck = {e: (pc[e], len(self.ops[e])) for e in ENG if pc[e] < len(self.ops[e])}
        if stuck:
            print("DEADLOCK", stuck)
            for e in stuck:
                fn, deps, inc, dma = self.ops[e][pc[e]]
                for t in deps:
                    if t[0] == 'c' and not (t[1] == 'pe' and e == 'pe'):
                        if semc[t[1]] < cum[t[1]][t[2]]:
                            print("  ", e, "op", pc[e], "waits", t, "need", cum[t[1]][t[2]], "have", semc[t[1]], "target pc", pc[t[1]])
                    elif t[0] == 'd' and semd.get(t[1], 0) < t[2]:
                        print("  ", e, "op", pc[e], "waits dma", t, "have", semd.get(t[1], 0))
        else:
            print("simulate: no deadlock; ops", {e: len(self.ops[e]) for e in ENG})
        return not stuck

    def emit(self, nc, block, sems, dma_sems):
        for e in ENG:
            for o in self.ops[e]:
                for t in o[1]:
                    if t[0] == 'c' and not (t[1] == 'pe' and e == 'pe'):
                        self.ops[t[1]][t[2]][2] = True
        cum = {}
        for e in ENG:
            c = 0
            arr = []
            for o in self.ops[e]:
                if o[2]:
                    c += 1
                arr.append(c)
            cum[e] = arr
        ops = self.ops

        def run(e, eng):
            waited = {}
            for o in ops[e]:
                fn, deps, inc, dma = o
                need = {}
                for t in deps:
                    if t[0] == 'c':
                        if t[1] == 'pe' and e == 'pe':
                            continue
                        key = ('c', t[1])
                        val = cum[t[1]][t[2]]
                    else:
                        key = ('d', t[1])
                        val = t[2]
                    if val > need.get(key, 0):
                        need[key] = val
                for key, val in need.items():
                    if waited.get(key, 0) < val:
                        sem = sems[key[1]] if key[0] == 'c' else dma_sems[key[1]]
                        eng.wait_ge(sem, val)
                        waited[key] = val
                ins = fn(eng)
                if dma is not None:
                    ins.then_inc(dma_sems[dma], 16)
                elif inc:
                    ins.then_inc(sems[e], 1)

        @block.tensor
        def _(eng):
            run('pe', eng)

        @block.scalar
        def _(eng):
            run('act', eng)

        @block.vector
        def _(eng):
            run('dve', eng)

        @block.gpsimd
        def _(eng):
            run('pool', eng)

        @block.sync
        def _(eng):
            run('sp', eng)


def mk(method, *args, **kw):
    return lambda e: getattr(e, method)(*args, **kw)


def col_groups(c0, n):
    out = []
    while n > 0:
        m = min(512, n)
        out.append((c0, m))
        c0 += m
        n -= m
    return out


def t5_bucket_np(dist):
    n = np.maximum(dist, 0)
    nf = np.maximum(n, 1).astype(np.float32)
    large = 16 + (np.log(nf / np.float32(16)) / np.float32(math.log(128 / 16)) * np.float32(16)).astype(np.int32)
    large = np.minimum(large, 31)
    return np.where(n < 16, n, large)


def build_program(stop_after=None, debug=False):
    nc = bass.Bass("TRN2", target_bir_lowering=False)
    S = Sched()

    def din(name, shape, dt=F32):
        return nc.dram_tensor(name, list(shape), dt, kind="ExternalInput").ap()

    def dout(name, shape, dt=F32):
        return nc.dram_tensor(name, list(shape), dt, kind="ExternalOutput").ap()

    def dint(name, shape, dt=F32):
        return nc.dram_tensor(name, list(shape), dt, kind="Internal").ap()

    DSPEC = {
        'xT_d': ('in', 'xT', [KC, 128, T0], F32),
        'cT_d': ('in', 'cT', [128, KC, NR], F32),
        'bada_d': ('in', 'badaT', [128, 288], F32),
        'wada_d': ('in', 'w_ada', [72, 128, KC * 512], F32),
        'w1a_d': ('in', 'w1a', [FC, 128, D], F32),
        'w3a_d': ('in', 'w3a', [FC, 128, D], F32),
        'w2a_d': ('in', 'w2a', [KC, 128, DFF], F32),
        'win_d': ('in', 'w_in', [72, 128, D], F32),
        'wout_d': ('in', 'w_out', [KC, 128, D], F32),
        'w1b_d': ('in', 'w1b', [FC, 128, D], F32),
        'w3b_d': ('in', 'w3b', [FC, 128, D], F32),
        'w2b_d': ('in', 'w2b', [KC, 128, DFF], F32),
        'gains_d': ('in', 'gains', [128, 4, KC], F32),
        'convw_d': ('in', 'convw', [128, 16, 3], F32),
        'sinks_d': ('in', 'sinks_bc', [128, 16], F32),
        'sinkS_d': ('in', 'sinkS', [32, 4], F32),
        'relb_d': ('in', 'relb', [32, 16], F32),
        'onehot_d': ('in', 'onehot', [33, 384], F32),
        'halo_d': ('in', 'halo', [128, 2], F32),
        'stT_d': ('in', 'stT', [128, 16, 16, 2], F32),
        'ck_d': ('in', 'ck', [16, 128, 512], F32),
        'cv_d': ('in', 'cv', [16, 128, 512], F32),
        'yT_d': ('out', 'yT', [KC, 128, T1], F32),
        'kwin_d': ('out', 'kwinT', [4, 128, 128], F32),
        'vwin_d': ('out', 'vwinT', [4, 128, 128], F32),
        'knew_d': ('out', 'knewT', [4, 128, 128], F32),
        'vnew_d': ('out', 'vnewT', [4, 128, 128], F32),
        'kwsc_d': ('out', 'kwsc', [16, 120, 512], F32),
        'vwsc_d': ('out', 'vwsc', [16, 120, 512], F32),
        'convo_d': ('out', 'convo', [128, 16, NR, 2], F32),
        'aT_d': ('int', 'aT_s', [FC, 128, T0], BF16),
        'x1_d': ('int', 'x1T_s', [KC, 128, T0], F32),
        'x2_d': ('int', 'x2T_s', [KC, 128, T1], F32),
        'x3_d': ('int', 'x3T_s', [KC, 128, T1], F32),
        'mix_d': ('int', 'mix_s', [KC, 128, T1], BF16),
        'mod_d': ('int', 'mod_s', [128, 9, KC, NR], F32),
        'R_d': ('int', 'R_s', [128, 16, 384], F32),
        'dbg_h_d': ('out', 'dbg_h', [128, KC, T0], BF16),
        'dbg_c_d': ('out', 'dbg_c', [128, 128], F32),
    }

    class _G:
        def __init__(self):
            self.c = {}

        def __getattr__(self, var):
            c = self.__dict__["c"]
            if var not in c:
                kind, nm, shape, dt = DSPEC[var]
                k = {"in": "ExternalInput", "out": "ExternalOutput", "int": ("ExternalOutput" if debug else "Internal")}[kind]
                c[var] = nc.dram_tensor(nm, list(shape), dt, kind=k).ap()
            return c[var]

    G = _G()

    ARENA = int(os.environ.get('KARENA', '207')) * 1024
    arena_cm = nc.sbuf_tensor("arena", [128, ARENA], U8)
    psum_cm = nc.psum_tensor("ps", [128, 8, 512], F32)
    arena = arena_cm.__enter__()
    ps = psum_cm.__enter__()

    class Alloc:
        def __init__(self):
            self.off = 0

        def get(self, nbytes, dt, shape=None, parts=128):
            o = self.off
            self.off = (o + nbytes + 63) // 64 * 64
            assert self.off <= ARENA, ("arena overflow", self.off)
            v = arena[0:parts, o:o + nbytes].bitcast(dt)
            if shape is not None and len(shape) > 1:
                names = " ".join("abcdefg"[i] for i in range(len(shape)))
                kw = {"abcdefg"[i]: shape[i] for i in range(len(shape) - 1)}
                v = v.rearrange("p (%s) -> p %s" % (names, names), **kw)
            return v

    A = Alloc()
    ident_bf = A.get(256, BF16)
    ident_f = A.get(512, F32)
    ones_f = A.get(512, F32)
    gains = A.get(4 * KC * 4, F32, (4, KC))
    halo = A.get(8, F32)
    convw = A.get(16 * 3 * 4, F32, (16, 3))
    sinks = A.get(64, F32)
    par = A.get(3 * KC * NR * 4, F32, (3, KC, NR))
    junk = A.get(64, F32)
    eps_ap = A.get(4, F32)
    stage = [A.get(PIECE * 128 * 4, F32, (PIECE, 128)) for _ in range(NS)]
    wbf = [A.get(PIECE * 128 * 2, BF16, (PIECE, 128)) for _ in range(NW)]
    stage_b = [Buf() for _ in range(NS)]
    wbf_b = [Buf() for _ in range(NW)]
    PH0 = A.off
    bank_b = [Buf() for _ in range(8)]
    par_b = Buf()
    parg_b = Buf()
    const_b = Buf()

    st = {"wi": 0, "gi": 0, "cast": 0}

    def barrier():
        ts = []
        for e in ('pe', 'act', 'dve', 'pool'):
            for idx in range(len(S.ops[e]) - 1, -1, -1):
                if S.ops[e][idx][3] is None:
                    ts.append(('c', e, idx))
                    break
        for k, n in S.dma_cnt.items():
            if not k.startswith("st"):
                ts.append(('d', k, 16 * n))
        S.op('pe', mk('matmul', ps[:, 7, 0:8], lhsT=ident_f[:, 0:128], rhs=ident_f[:, 0:8], start=True, stop=True), deps=ts)
        S.op('act', mk('copy', out=junk[:, 0:4], in_=junk[:, 4:8]), deps=ts)
        S.op('dve', mk('tensor_copy', out=junk[:, 8:12], in_=junk[:, 12:16]), deps=ts)
        S.op('pool', mk('memset', junk[:, 14:16], 0.0), deps=ts)

    def pdma(out, in_, key, reads=(), writes=(), deps=()):
        return S.op('pool', mk('dma_start', out=out, in_=in_), reads=reads, writes=writes, deps=deps, dma=key)

    def gemm(mats, nk, n_list, act, groups, epilogue, pre_n=None, post_n=None, act_reads=(), k_outer=False, act_reads_k=None):
        pieces = []
        k0 = 0
        while k0 < nk:
            m = min(PIECE, nk - k0)
            pieces.append((k0, m))
            k0 += m
        n_list = list(n_list)
        R = len(mats) * len(pieces)
        tiles = [(n, mi, pi) for n in n_list for mi in range(len(mats)) for pi in range(len(pieces))]
        slot_of = {}
        ld = {"j": 0}

        def load_upto(jmax):
            jmax = min(jmax, len(tiles))
            while ld["j"] < jmax:
                n_, mi, pi = tiles[ld["j"]]
                ld["j"] += 1
                k0, m = pieces[pi]
                wi = st["wi"]
                st["wi"] += 1
                ss, ws = wi % NS, wi % NW
                src = mats[mi][n_][:, k0 * 128:(k0 + m) * 128].rearrange("p (k j) -> p k j", k=m)
                S.op('sp', lambda e, o=stage[ss][:, 0:m, :], i=src: e.dma_start(out=o, in_=i),
                     writes=[stage_b[ss]], dma="st%d" % ss)
                ceng = 'act' if (st["cast"] % 2 == 0) else 'dve'
                st["cast"] += 1
                if ceng == 'act':
                    S.op('act', lambda e, o=wbf[ws][:, 0:m, :], i=stage[ss][:, 0:m, :]: e.copy(out=o, in_=i),
                         reads=[stage_b[ss]], writes=[wbf_b[ws]])
                else:
                    S.op('dve', lambda e, o=wbf[ws][:, 0:m, :], i=stage[ss][:, 0:m, :]: e.tensor_copy(out=o, in_=i),
                         reads=[stage_b[ss]], writes=[wbf_b[ws]])
                slot_of[(n_, mi, pi)] = ws

        for ni, n in enumerate(n_list):
            if pre_n is not None:
                pre_n(n)
            if not k_outer:
                assert R <= NW
                load_upto(ni * R + NW)
                slots = [[slot_of[(n, mi, pi)] for pi in range(len(pieces))] for mi in range(len(mats))]
            ng = len(groups)
            if k_outer:
                assert len(mats) == 1 and ng <= 2
                bks = []
                for _g in range(ng):
                    bks.append(st["gi"] % 4)
                    st["gi"] += 1
                cnt = 0
                for pi, (k0, m) in enumerate(pieces):
                    load_upto(ni * R + pi + NW)
                    ws = slot_of[(n, 0, pi)]
                    for kk in range(m):
                        first = (cnt == 0)
                        last = (cnt == nk - 1)
                        for gi_, (c0, cl) in enumerate(groups):
                            bk = bks[gi_]
                            kw = {}
                            ark = act_reads_k(k0) if act_reads_k is not None else []
                            if first:
                                kw["writes"] = [bank_b[bk]]
                                kw["dep_only"] = list(act_reads)
                            if kk == 0 and gi_ == 0:
                                kw["dep_only"] = kw.get("dep_only", []) + [wbf_b[ws]] + ark
                            if kk == m - 1 and gi_ == ng - 1:
                                kw["reg_only"] = [wbf_b[ws]] + ark
                            if last and gi_ == ng - 1:
                                kw["reg_only"] = kw.get("reg_only", []) + list(act_reads)
                            t = S.op('pe', lambda e, o=ps[:, bk, 0:cl], l=wbf[ws][:, kk, :], r=act(k0 + kk, c0, cl), f=first, la=last:
                                     e.matmul(o, lhsT=l, rhs=r, start=f, stop=la), **kw)
                            if last:
                                bank_b[bk].w = t
                                bank_b[bk].r = []
                        cnt += 1
                for gi_, (c0, cl) in enumerate(groups):
                    epilogue(n, gi_, [bks[gi_]], c0, cl)
                if post_n is not None:
                    post_n(n)
                continue
            ng = len(groups)
            for gi_, (c0, cl) in enumerate(groups):
                banks = []
                for mi in range(len(mats)):
                    bk = st["gi"] % 4
                    st["gi"] += 1
                    banks.append(bk)
                    tot = nk
                    cnt = 0
                    for pi, (k0, m) in enumerate(pieces):
                        ws = slots[mi][pi]
                        for kk in range(m):
                            first = (cnt == 0)
                            last = (cnt == tot - 1)
                            kw = {}
                            if first:
                                kw["writes"] = [bank_b[bk]]
                                kw["dep_only"] = list(act_reads)
                            if kk == 0:
                                kw["dep_only"] = kw.get("dep_only", []) + [wbf_b[ws]]
                            if kk == m - 1 and gi_ == ng - 1:
                                kw["reg_only"] = [wbf_b[ws]]
                            if last:
                                kw["reg_only"] = kw.get("reg_only", []) + list(act_reads)
                            t = S.op('pe', lambda e, o=ps[:, bk, 0:cl], l=wbf[ws][:, kk, :], r=act(k0 + kk, c0, cl), f=first, la=last:
                                     e.matmul(o, lhsT=l, rhs=r, start=f, stop=la), **kw)
                            if last:
                                bank_b[bk].w = t
                                bank_b[bk].r = []
                            cnt += 1
                epilogue(n, gi_, banks, c0, cl)
            if post_n is not None:
                post_n(n)

    def setup():
        S.op('dve', mk('memset', ones_f, 1.0), writes=[const_b])
        S.op('dve', mk('memset', ident_f, 0.0), writes=[const_b])
        S.op('pool', mk('affine_select', out=ident_f, in_=ones_f, pattern=[[-1, 128]], compare_op=ALU.is_equal,
                                               fill=0.0, base=0, channel_multiplier=1), writes=[const_b])
        S.op('dve', mk('tensor_copy', out=ident_bf, in_=ident_f), writes=[const_b])
        S.op('dve', mk('memset', junk, 0.0), writes=[const_b])
        S.op('dve', mk('memset', eps_ap, EPS), writes=[const_b])
        pdma(gains, G.gains_d, "c0", writes=[const_b])
        pdma(halo, G.halo_d, "c1", writes=[const_b])
        pdma(convw, G.convw_d, "c2", writes=[const_b])
        pdma(sinks, G.sinks_d, "c3", writes=[const_b])

    class ModStream:
        def __init__(self):
            top = ARENA - 42 * 1024
            self.top = top
            sav = A.off
            A.off = PH0 + 48 * 1024
            self.cT = A.get(KC * NR * 4, F32, (KC, NR))
            A.off = top
            self.scT = A.get(KC * NR * 2, BF16, (KC, NR))
            self.bada = A.get(288 * 4, F32)
            self.modT = A.get(288 * NR * 4, F32, (288, NR))
            self.mtok = [A.get(512 * 4, F32, None, parts=NR) for _ in range(2)]
            self.mst = [A.get(1024 * 4, F32, (2, 512)) for _ in range(2)]
            self.mwb = [A.get(1024 * 2, BF16, (2, 512)) for _ in range(4)]
            A.off = sav
            self.b_c, self.b_sc, self.b_bada, self.b_mod = Buf(), Buf(), Buf(), Buf()
            self.b_mtok = [Buf(), Buf()]
            self.b_mst = [Buf(), Buf()]
            self.b_mwb = [Buf() for _ in range(4)]
            self.loaded = 0
            self.g = 0
            pdma(self.cT, G.cT_d, "m0", writes=[self.b_c])
            pdma(self.bada, G.bada_d, "m1", writes=[self.b_bada])
            S.op('act', mk('activation', out=self.scT, in_=self.cT, func=AF.Silu), reads=[self.b_c], writes=[self.b_sc])

        def _load_upto(self, jmax):
            jmax = min(jmax, 72 * 16)
            while self.loaded < jmax:
                j = self.loaded
                self.loaded += 1
                g, t = divmod(j, 16)
                ss, ws = j % 2, j % 4
                src = G.wada_d[g][:, t * 1024:(t + 1) * 1024].rearrange("p (k j) -> p k j", k=2)
                S.op('sp', mk('dma_start', out=self.mst[ss], in_=src), writes=[self.b_mst[ss]], dma="stm%d" % ss)
                ceng = 'act' if (st["cast"] % 2 == 0) else 'dve'
                st["cast"] += 1
                S.op(ceng, mk('copy' if ceng == 'act' else 'tensor_copy', out=self.mwb[ws], in_=self.mst[ss]),
                     reads=[self.b_mst[ss]], writes=[self.b_mwb[ws]])

        def step(self):
            g = self.g
            self.g += 1
            bk = 4 + g % 2
            i = g % 2
            for t in range(16):
                j = g * 16 + t
                self._load_upto(j + 4)
                ws = j % 4
                for kk in range(2):
                    kw = {}
                    first = (t == 0 and kk == 0)
                    last = (t == 15 and kk == 1)
                    if first:
                        kw["writes"] = [bank_b[bk]]
                        kw["dep_only"] = [self.b_sc]
                    if kk == 0:
                        kw["dep_only"] = kw.get("dep_only", []) + [self.b_mwb[ws]]
                    else:
                        kw["reg_only"] = [self.b_mwb[ws]]
                    tk = S.op('pe', mk('matmul', ps[0:NR, bk, 0:512], lhsT=self.scT[:, 2 * t + kk, :], rhs=self.mwb[ws][:, kk, :],
                                       start=first, stop=last), **kw)
                    if last:
                        bank_b[bk].w = tk
                        bank_b[bk].r = []
            S.op('act', mk('copy', out=self.mtok[i], in_=ps[0:NR, bk, 0:512]), reads=[bank_b[bk]], writes=[self.b_mtok[i]])
            for c in range(4):
                kw = {"writes": [bank_b[6]]} if c == 0 else {}
                tk = S.op('pe', mk('transpose', ps[:, 6, c * NR:(c + 1) * NR], self.mtok[i][:, c * 128:(c + 1) * 128], ident_f[0:NR, 0:NR]),
                          dep_only=[self.b_mtok[i], const_b], reg_only=([self.b_mtok[i]] if c == 3 else []), **kw)
            bank_b[6].w = tk
            bank_b[6].r = []
            S.op('dve', mk('tensor_tensor', out=self.modT[:, 4 * g:4 * g + 4, :], in0=ps[:, 6, 0:4 * NR].rearrange("p (c r) -> p c r", c=4),
                           in1=self.bada[:, 4 * g:4 * g + 4].unsqueeze(2).broadcast_to([128, 4, NR]), op=ALU.add),
                 reads=[bank_b[6], self.b_bada], writes=[self.b_mod])

        def flush(self, j0, j1):
            pdma(G.mod_d[:, j0:j1, :, :], self.modT[:, j0 * KC:j1 * KC, :].rearrange("p (j c) r -> p j c r", j=j1 - j0), "m2", reads=[self.b_mod])

    def phase_relbias():
        A.off = PH0
        tab = A.get(16 * 4, F32, None, parts=33)
        tabrep = A.get(16 * 128 * 4, F32, (16, 128), parts=33)
        oneh = A.get(384 * 4, F32, None, parts=33)
        Rsb = A.get(16 * 384 * 4, F32, (16, 384))
        b_tab, b_oh, b_R = Buf(), Buf(), Buf()
        S.op('dve', mk('memset', tab[32:33, :], 1.0), writes=[b_tab])
        pdma(tab[0:32, :], G.relb_d, "m3", writes=[b_tab])
        pdma(oneh, G.onehot_d, "m4", writes=[b_oh])
        S.op('dve', mk('tensor_copy', out=tabrep, in_=tab[:, 0:16].unsqueeze(2).broadcast_to([33, 16, 128])),
             reads=[b_tab], writes=[b_tab])
        for h in range(16):
            bk = 4 + (h % 2)
            S.op('pe', mk('matmul', ps[:, bk, 0:384], lhsT=tabrep[:, h, :], rhs=oneh, start=True, stop=True),
                 reads=[b_tab, b_oh], writes=[bank_b[bk]])
            S.op('dve', mk('tensor_copy', out=Rsb[:, h, :], in_=ps[:, bk, 0:384]),
                 reads=[bank_b[bk]], writes=[b_R])
        pdma(G.R_d, Rsb, "m5", reads=[b_R])

    def load_par(s):
        pdma(par[:, 0:2], G.mod_d[:, 3 * s:3 * s + 2, :, :], "par", writes=[par_b])
        S.op('dve', mk('tensor_scalar', out=par[:, 1], in0=par[:, 1], scalar1=1.0, scalar2=None, op0=ALU.add),
             writes=[par_b])
        S.op('dve', mk('tensor_tensor', out=par[:, 1], in0=par[:, 1],
                                              in1=gains[:, s, :].unsqueeze(2).broadcast_to([128, KC, NR]), op=ALU.mult),
             writes=[par_b], reads=[const_b])

    def load_gate(s):
        coef = (0.5, 1.0, 0.5)[s]
        pdma(par[:, 2], G.mod_d[:, 3 * s + 2, :, :], "parg", writes=[parg_b])
        if coef != 1.0:
            S.op('dve', mk('tensor_scalar', out=par[:, 2], in0=par[:, 2], scalar1=coef, scalar2=None, op0=ALU.mult),
                 writes=[parg_b])

    def norm_stats(x_d, T, xbuf, xb_b, sq, sq_b, rstd, rstd_b, nxb=NXB):
        groups = col_groups(0, T)
        ssb = [4, 5, 6]
        for c in range(KC):
            pdma(xbuf[c % nxb][:, 0:T], x_d[c], "xb%d" % (c % nxb), writes=[xb_b[c % nxb]])
            S.op('act', mk('activation', out=sq[c % nxb][:, 0:T], in_=xbuf[c % nxb][:, 0:T], func=AF.Square),
                 reads=[xb_b[c % nxb]], writes=[sq_b[c % nxb]])
            for gi_, (c0, cl) in enumerate(groups):
                kw = {}
                if c == 0:
                    kw["writes"] = [bank_b[ssb[gi_]]]
                t = S.op('pe', lambda e, c=c, bk=ssb[gi_], c0=c0, cl=cl: e.matmul(ps[:, bk, 0:cl], lhsT=ones_f, rhs=sq[c % nxb][:, c0:c0 + cl],
                                                                                  start=(c == 0), stop=(c == KC - 1)),
                         dep_only=[sq_b[c % nxb], const_b], reg_only=[sq_b[c % nxb]], **kw)
                if c == KC - 1:
                    bank_b[ssb[gi_]].w = t
                    bank_b[ssb[gi_]].r = []
        for gi_, (c0, cl) in enumerate(groups):
            S.op('act', lambda e, bk=ssb[gi_], c0=c0, cl=cl: e.activation(out=rstd[:, c0:c0 + cl], in_=ps[:, bk, 0:cl], func=AF.Sqrt,
                                                                         scale=1.0 / D, bias=eps_ap),
                 reads=[bank_b[ssb[gi_]], const_b], writes=[rstd_b])
        S.op('dve', mk('reciprocal', out=rstd[:, 0:T], in_=rstd[:, 0:T]), writes=[rstd_b])

    def phase_norm(x_d, T, s, hT, h_b, nxb=NXB):
        SC = T - 128
        A.off = PH0 + KC * T0 * 2
        xbuf = [A.get(T0 * 4, F32) for _ in range(nxb)]
        sq = [A.get(T0 * 4, F32) for _ in range(nxb)]
        rstd = A.get(T0 * 4, F32)
        tmp = [A.get(T0 * 4, F32) for _ in range(2)]
        tmp3 = A.get(128 * 4, F32, (16, 8))
        xb_b = [Buf() for _ in range(nxb)]
        sq_b = [Buf() for _ in range(nxb)]
        tmp_b = [Buf(), Buf()]
        rstd_b, t3_b = Buf(), Buf()
        load_par(s)
        norm_stats(x_d, T, xbuf, xb_b, sq, sq_b, rstd, rstd_b, nxb)
        for c in range(KC):
            i = c % 2
            xi = c % nxb
            pdma(xbuf[xi][:, 0:T], x_d[c], "xb%d" % xi, writes=[xb_b[xi]])
            S.op('dve', mk('tensor_tensor', out=tmp[i][:, 0:T], in0=xbuf[xi][:, 0:T], in1=rstd[:, 0:T], op=ALU.mult),
                 reads=[xb_b[xi], rstd_b], writes=[tmp_b[i]])
            S.op('act', mk('activation', out=hT[:, c, 0:SC], in_=tmp[i][:, 0:SC], func=AF.Identity,
                                                         scale=par[:, 1, c, 0:1], bias=par[:, 0, c, 0:1]),
                 reads=[tmp_b[i], par_b], writes=[h_b])
            S.op('dve', mk('tensor_tensor', out=tmp3, in0=tmp[i][:, SC:T].rearrange("p (s t) -> p s t", t=8),
                                                            in1=par[:, 1, c, 1:NR].unsqueeze(2).broadcast_to([128, 16, 8]), op=ALU.mult),
                 reads=[tmp_b[i], par_b], writes=[t3_b])
            S.op('dve', mk('tensor_tensor', out=hT[:, c, SC:T].rearrange("p (s t) -> p s t", t=8), in0=tmp3,
                                                       in1=par[:, 0, c, 1:NR].unsqueeze(2).broadcast_to([128, 16, 8]), op=ALU.add),
                 reads=[t3_b, par_b], writes=[h_b])

    def make_resid(T, x_old_d, old_off, x_new_d, c0h, lenh, xo, xo_b, xn, xn_b, rt, rt_b, tag):
        SC = T - 128

        def pre_n(d):
            i = d % 2
            pdma(xo[i][:, 0:lenh], x_old_d[d][:, old_off + c0h:old_off + c0h + lenh], tag + "xo%d" % i, writes=[xo_b[i]])

        def epi(d, g, banks, c0, cl):
            i = d % 2
            bk = banks[0]
            lo, hi = c0, min(c0 + cl, SC)
            if hi > lo:
                S.op('dve', mk('scalar_tensor_tensor', out=xn[i][:, lo - c0h:hi - c0h], in0=ps[:, bk, lo - c0:hi - c0],
                                                              scalar=par[:, 2, d, 0:1], in1=xo[i][:, lo - c0h:hi - c0h],
                                                              op0=ALU.mult, op1=ALU.add),
                     reads=[bank_b[bk], xo_b[i], parg_b], writes=[xn_b[i]])
            lo, hi = max(c0, SC), c0 + cl
            if hi > lo:
                s0, s1 = (lo - SC) // 8, (hi - SC) // 8
                ns = s1 - s0
                S.op('dve', mk('tensor_tensor', out=rt[:, 0:ns, :], in0=ps[:, bk, lo - c0:hi - c0].rearrange("p (s t) -> p s t", t=8),
                                                      in1=par[:, 2, d, 1 + s0:1 + s1].unsqueeze(2).broadcast_to([128, ns, 8]), op=ALU.mult),
                     reads=[bank_b[bk], parg_b], writes=[rt_b])
                S.op('dve', mk('tensor_tensor', out=xn[i][:, lo - c0h:hi - c0h].rearrange("p (s t) -> p s t", t=8), in0=rt[:, 0:ns, :],
                                                      in1=xo[i][:, lo - c0h:hi - c0h].rearrange("p (s t) -> p s t", t=8), op=ALU.add),
                     reads=[rt_b, xo_b[i]], writes=[xn_b[i]])

        def post_n(d):
            i = d % 2
            pdma(x_new_d[d][:, c0h:c0h + lenh], xn[i][:, 0:lenh], tag + "xn%d" % i, reads=[xn_b[i]])

        return pre_n, epi, post_n

    def phase_ffn(w1_d, w3_d, w2_d, T, x_old_d, old_off, x_new_d, hT, h_b, sl, up_only=False, up_hook=None):
        A.off = PH0 + KC * T0 * 2
        aout = [A.get(T0 * 2, BF16) for _ in range(2)]
        sg = [A.get(512 * 4, F32) for _ in range(2)]
        ao_b = [Buf(), Buf()]
        sg_b = [Buf(), Buf()]
        groups = col_groups(0, T)
        cnt = {"g": 0}

        def epi_up(f, g, banks, c0, cl):
            j = cnt["g"] % 2
            cnt["g"] += 1
            i = f % 2
            S.op('act', mk('activation', out=sg[j][:, 0:cl], in_=ps[:, banks[0], 0:cl], func=AF.Silu),
                 reads=[bank_b[banks[0]]], writes=[sg_b[j]])
            S.op('dve', mk('tensor_tensor', out=aout[i][:, c0:c0 + cl], in0=sg[j][:, 0:cl], in1=ps[:, banks[1], 0:cl], op=ALU.mult),
                 reads=[sg_b[j], bank_b[banks[1]]], writes=[ao_b[i]])

        def post_up(f):
            i = f % 2
            pdma(G.aT_d[f][:, 0:T], aout[i][:, 0:T], "ao%d" % i, reads=[ao_b[i]])
            if up_hook is not None:
                up_hook(f)

        gemm([w1_d, w3_d], KC, range(FC), lambda k, c0, cl: hT[:, k, c0:c0 + cl], groups, epi_up, post_n=post_up, act_reads=[h_b])
        if up_hook is not None:
            up_hook(None)
        barrier()
        if up_only:
            return
        load_gate(sl)
        half = T // 2
        for hf in range(2):
            c0h = hf * half
            A.off = PH0
            aTh = A.get(FC * half * 2, BF16, (FC, half))
            xo = [A.get(half * 4, F32) for _ in range(2)]
            xn = [A.get(half * 4, F32) for _ in range(2)]
            rt = A.get(128 * 4, F32, (16, 8))
            a_bs = [Buf() for _ in range(6)]
            xo_b, xn_b, rt_b = [Buf(), Buf()], [Buf(), Buf()], Buf()
            f0 = 0
            qi = 0
            while f0 < FC:
                f1 = min(FC, f0 + 16)
                pdma(aTh[:, f0:f1, :], G.aT_d[f0:f1, :, c0h:c0h + half].rearrange("f p t -> p f t"), "ah%d" % qi, writes=[a_bs[qi]],
                     deps=([a_bs[qi - 2].w] if qi >= 2 else []))
                f0 = f1
                qi += 1
            pre_n, epi, post_n = make_resid(T, x_old_d, old_off, x_new_d, c0h, half, xo, xo_b, xn, xn_b, rt, rt_b, "d")
            gemm([w2_d], FC, range(KC), lambda k, c0, cl: aTh[:, k, c0 - c0h:c0 - c0h + cl], col_groups(c0h, half), epi,
                 pre_n=pre_n, post_n=post_n, k_outer=True, act_reads_k=lambda k0_: [a_bs[k0_ // 16]])
            barrier()

    def phase_mixer(hT, h_b, stop=None):
        A.off = PH0 + KC * T0 * 2
        kT = A.get(4 * T0 * 2, BF16, (4, T0))
        vT = A.get(4 * T0 * 2, BF16, (4, T0))
        vtok = A.get(9 * 4 * 128 * 2, BF16, (9, 4, 128))
        kf = A.get(4 * 256 * 4, F32, (4, 256))
        vf = A.get(4 * 256 * 4, F32, (4, 256))
        mixb = [A.get(T1 * 2, BF16) for _ in range(2)]
        M1 = A.off
        stT = A.get(16 * 16 * 2 * 4, F32, (16, 16, 2))
        convo = A.get(16 * NR * 2 * 4, F32, (16, NR, 2))
        gcb = A.get(T0 * 4, F32)
        ub = A.get(T0 * 4, F32)
        yc = A.get(T1 * 4, F32)
        ucat = A.get(160 * 4, F32, (16, 10))
        k_b, v_b, vt_b, kf_b, vf_b = Buf(), Buf(), Buf(), Buf(), Buf()
        st_b, co_b, gc_b, u_b, yc_b, uc_b = Buf(), Buf(), Buf(), Buf(), Buf(), Buf()
        mb_b = [Buf(), Buf()]
        pdma(stT, G.stT_d, "x0", writes=[st_b])
        pdma(G.kwsc_d, G.ck_d[:, 8:128, :], "o0")
        pdma(G.vwsc_d, G.cv_d[:, 8:128, :], "o1")

        groups = col_groups(0, T0)
        act = lambda k, c0, cl: hT[:, k, c0:c0 + cl]

        def epi_conv(n, g, banks, c0, cl):
            bk = banks[0]
            if 40 <= n < 56:
                S.op('act', mk('copy', out=gcb[:, c0:c0 + cl], in_=ps[:, bk, 0:cl]), reads=[bank_b[bk]], writes=[gc_b])
            elif n >= 56:
                c = n - 56
                S.op('dve', mk('tensor_tensor', out=ub[:, c0:c0 + cl], in0=gcb[:, c0:c0 + cl], in1=ps[:, bk, 0:cl], op=ALU.mult),
                     reads=[bank_b[bk], gc_b], writes=[u_b])
                if g == len(groups) - 1:
                    S.op('dve', mk('tensor_scalar', out=ub[:, 0:128], in0=ub[:, 0:128], scalar1=halo[:, 0:1], scalar2=None, op0=ALU.mult),
                         reads=[const_b], writes=[u_b])
                    S.op('dve', mk('tensor_scalar', out=yc[:, 0:1024], in0=ub[:, 126:1150], scalar1=convw[:, c, 0:1], scalar2=None, op0=ALU.mult),
                         reads=[u_b, const_b], writes=[yc_b])
                    for j in (1, 2):
                        S.op('dve', mk('scalar_tensor_tensor', out=yc[:, 0:1024], in0=ub[:, 126 + j:1150 + j], scalar=convw[:, c, j:j + 1],
                                                                          in1=yc[:, 0:1024], op0=ALU.mult, op1=ALU.add),
                             reads=[u_b, const_b], writes=[yc_b])
                    us = ub[:, 1152:1280].rearrange("p (s t) -> p s t", t=8)
                    S.op('dve', mk('tensor_copy', out=ucat[:, :, 0:2], in_=stT[:, c, :, :]), reads=[st_b], writes=[uc_b])
                    S.op('dve', mk('tensor_copy', out=ucat[:, :, 2:10], in_=us), reads=[u_b], writes=[uc_b])
                    ys = yc[:, 1024:1152].rearrange("p (s t) -> p s t", t=8)
                    S.op('dve', mk('tensor_scalar', out=ys, in0=ucat[:, :, 0:8], scalar1=convw[:, c, 0:1], scalar2=None, op0=ALU.mult),
                         reads=[uc_b, const_b], writes=[yc_b])
                    for j in (1, 2):
                        S.op('dve', mk('scalar_tensor_tensor', out=ys, in0=ucat[:, :, j:j + 8], scalar=convw[:, c, j:j + 1], in1=ys,
                                                                          op0=ALU.mult, op1=ALU.add),
                             reads=[uc_b, const_b], writes=[yc_b])
                    S.op('act', mk('copy', out=convo[:, c, 0, :], in_=ub[:, 1150:1152]), reads=[u_b], writes=[co_b])
                    S.op('act', mk('copy', out=convo[:, c, 1:NR, :], in_=us[:, :, 6:8]), reads=[u_b], writes=[co_b])
            else:
                c = n - 24
                i = c % 2
                lo, hi = max(c0, 128), c0 + cl
                S.op('dve', mk('tensor_tensor', out=mixb[i][:, lo - 128:hi - 128], in0=yc[:, lo - 128:hi - 128], in1=ps[:, bk, lo - c0:hi - c0], op=ALU.mult),
                     reads=[bank_b[bk], yc_b], writes=[mb_b[i]])
                if g == len(groups) - 1:
                    pdma(G.mix_d[16 + c], mixb[i], "mx%d" % i, reads=[mb_b[i]])

        order = []
        for c in range(16):
            order += [40 + c, 56 + c, 24 + c]
        gemm([G.win_d], KC, order, act, groups, epi_conv, act_reads=[h_b])
        pdma(G.convo_d, convo, "o2", reads=[co_b])
        barrier()
        if stop == "mix_conv":
            return
        A.off = M1
        qb = [A.get(T0 * 2, BF16) for _ in range(2)]
        qTs = A.get(16 * 128 * 2, BF16, (16, 16, 8))
        mixS = A.get(16 * 128 * 2, BF16, (16, 128))
        biasP = [A.get(256 * 4, F32) for _ in range(2)]
        biasS = A.get(4 * 136 * 4, F32, (4, 136), parts=32)
        sinkS = A.get(16, F32, None, parts=32)
        s_sb = [A.get(256 * 4, F32) for _ in range(2)]
        p_sb = [A.get(256 * 4, F32) for _ in range(2)]
        pn_sb = [A.get(256 * 2, BF16) for _ in range(2)]
        pT_sb = [A.get(256 * 2, BF16, (2, 128)) for _ in range(2)]
        small = [A.get(8 * 4, F32) for _ in range(2)]
        kc_f = A.get(512 * 4, F32)
        vc_f = A.get(512 * 4, F32)
        kc_b = A.get(512 * 2, BF16)
        vc_b = A.get(512 * 2, BF16)
        kcT = A.get(512 * 2, BF16, (4, 128))
        vnw = A.get(512 * 2, BF16, (4, 128), parts=8)
        pT2 = A.get(32 * 2, BF16, None, parts=8)
        q_b = [Buf(), Buf()]
        qs_b, ms_b = Buf(), Buf()
        bp_b = [Buf(), Buf()]
        bs_b = Buf()
        s_b, p_b, pn_b, pt_b, sm_b = [Buf(), Buf()], [Buf(), Buf()], [Buf(), Buf()], [Buf(), Buf()], [Buf(), Buf()]
        kcf_b, vcf_b, kcb_b, vcb_b, kct_b, vnw_b, pt2_b = Buf(), Buf(), Buf(), Buf(), Buf(), Buf(), Buf()
        if "sink" not in os.environ.get("KSKIP", ""):
            pdma(sinkS, G.sinkS_d, "x1", writes=[bs_b])
        SK = os.environ.get("KSKIP", "")
        for kv in range(4 if "bias" not in SK else 0):
            for g in range(4):
                src = bass.AP(tensor=G.R_d.tensor, offset=(4 * kv + g) * 384 + 127, ap=[[16 * 384 - 1, 8], [1, 136]])
                pdma(biasS[8 * g:8 * g + 8, kv, :], src, "x2", writes=[bs_b])

        def epi_kv(n, g, banks, c0, cl):
            bk = banks[0]
            isk = n < 20
            kv = n - 16 if isk else n - 20
            dst, db = (kT, k_b) if isk else (vT, v_b)
            ff, fb = (kf, kf_b) if isk else (vf, vf_b)
            S.op('act', mk('copy', out=dst[:, kv, c0:c0 + cl], in_=ps[:, bk, 0:cl]), reads=[bank_b[bk]], writes=[db])
            if g == 2:
                S.op('act', mk('copy', out=ff[:, kv, :], in_=ps[:, bk, 0:256]), reads=[bank_b[bk]], writes=[fb])

        gemm([G.win_d], KC, list(range(16, 24)), act, groups, epi_kv, act_reads=[h_b])
        if "kvout" not in SK:
          pdma(G.kwin_d.rearrange("k p t -> p k t"), kf[:, :, 0:128], "o3", reads=[kf_b])
        if "kvout" not in SK:
          pdma(G.knew_d.rearrange("k p t -> p k t"), kf[:, :, 128:256], "o4", reads=[kf_b])
        if "kvout" not in SK:
          pdma(G.vwin_d.rearrange("k p t -> p k t"), vf[:, :, 0:128], "o5", reads=[vf_b])
        if "kvout" not in SK:
          pdma(G.vnew_d.rearrange("k p t -> p k t"), vf[:, :, 128:256], "o6", reads=[vf_b])
        for bb in range(9 if "vtok" not in SK else 0):
            bk = 4 + bb % 2
            pv = ps[:, bk, :].bitcast(BF16)
            for kv in range(4):
                S.op('pe', mk('transpose', pv[:, kv * 128:(kv + 1) * 128], vT[:, kv, bb * 128:(bb + 1) * 128], ident_bf),
                     reads=[v_b, const_b], writes=[bank_b[bk]])
            S.op('act', mk('copy', out=vtok[:, bb, :, :], in_=pv[:, 0:512].rearrange("p (k d) -> p k d", k=4)),
                 reads=[bank_b[bk]], writes=[vt_b])

        if stop == "mix_kv":
            return
        def softmax_rows(P, W, sc_bk, bias_ap, bias_bufs, sink_ap, j, halo_fix):
            s_, p_, pn_, sm = s_sb[j], p_sb[j], pn_sb[j], small[j]
            S.op('dve', mk('scalar_tensor_tensor', out=s_[0:P, 0:W], in0=ps[0:P, sc_bk, 0:W], scalar=SCALE, in1=bias_ap,
                                                          op0=ALU.mult, op1=ALU.add),
                 reads=[bank_b[sc_bk]] + bias_bufs, writes=[s_b[j]])
            if halo_fix:
                S.op('dve', mk('tensor_scalar', out=s_[0:P, 0:128], in0=s_[0:P, 0:128], scalar1=halo[0:P, 1:2], scalar2=None, op0=ALU.add),
                     reads=[const_b], writes=[s_b[j]])
            S.op('dve', mk('tensor_reduce', out=sm[0:P, 0:1], in_=s_[0:P, 0:W], op=ALU.max, axis=AX.X), reads=[s_b[j]], writes=[sm_b[j]])
            S.op('dve', mk('tensor_scalar', out=sm[0:P, 1:2], in0=sm[0:P, 0:1], scalar1=sink_ap, scalar2=-1.0, op0=ALU.max, op1=ALU.mult),
                 reads=bias_bufs, writes=[sm_b[j]])
            S.op('act', mk('activation', out=p_[0:P, 0:W], in_=s_[0:P, 0:W], func=AF.Exp, bias=sm[0:P, 1:2], scale=1.0,
                                               accum_out=sm[0:P, 2:3]),
                 reads=[s_b[j], sm_b[j]], writes=[p_b[j], sm_b[j]])
            S.op('act', mk('activation', out=sm[0:P, 3:4], in_=sm[0:P, 1:2], func=AF.Exp, bias=sink_ap, scale=1.0),
                 reads=bias_bufs, writes=[sm_b[j]])
            S.op('dve', mk('tensor_tensor', out=sm[0:P, 4:5], in0=sm[0:P, 2:3], in1=sm[0:P, 3:4], op=ALU.add), writes=[sm_b[j]])
            S.op('dve', mk('reciprocal', out=sm[0:P, 5:6], in_=sm[0:P, 4:5]), writes=[sm_b[j]])
            S.op('dve', mk('tensor_scalar', out=pn_[0:P, 0:W], in0=p_[0:P, 0:W], scalar1=sm[0:P, 5:6], scalar2=None, op0=ALU.mult),
                 reads=[p_b[j], sm_b[j]], writes=[pn_b[j]])

        ucnt = {"u": 0}

        def attn_head(h):
            kv = h // 4
            i = h % 2
            hb = h % 2
            src = bass.AP(tensor=G.R_d.tensor, offset=h * 384 + 127, ap=[[16 * 384 - 1, 128], [1, 256]])
            pdma(biasP[hb], src, "bp%d" % hb, writes=[bp_b[hb]])
            S.op('dve', mk('tensor_copy', out=qTs[:, :, h, :], in_=qb[i][:, 1152:1280].rearrange("p (s t) -> p s t", t=8)), reads=[q_b[i]], writes=[qs_b])

            def qk(bb):
                bk = 4 + (ucnt["u"] % 2)
                S.op('pe', mk('matmul', ps[:, bk, 0:256], lhsT=qb[i][:, bb * 128:(bb + 1) * 128], rhs=kT[:, kv, (bb - 1) * 128:(bb + 1) * 128],
                                              start=True, stop=True),
                     reads=[q_b[i], k_b], writes=[bank_b[bk]])
                return bk

            nxt = qk(1)
            for bb in range(1, 9):
                j = ucnt["u"] % 2
                ucnt["u"] += 1
                sc_bk = nxt
                softmax_rows(128, 256, sc_bk, biasP[hb], [bp_b[hb], const_b], sinks[:, h:h + 1], j, bb == 1)
                if bb < 8:
                    nxt = qk(bb + 1)
                ptv = ps[:, 6, :].bitcast(BF16)
                for t in range(2):
                    S.op('pe', mk('transpose', ptv[:, t * 128:(t + 1) * 128], pn_sb[j][:, t * 128:(t + 1) * 128], ident_bf),
                         reads=[pn_b[j], const_b], writes=[bank_b[6]])
                S.op('act', mk('copy', out=pT_sb[j], in_=ptv[:, 0:256].rearrange("p (t q) -> p t q", t=2)),
                     reads=[bank_b[6]], writes=[pt_b[j]])
                for t in range(2):
                    S.op('pe', mk('matmul', ps[:, 7, 0:128], lhsT=vtok[:, bb - 1 + t, kv, :], rhs=pT_sb[j][:, t, :], start=(t == 0), stop=(t == 1)),
                         reads=[pt_b[j], vt_b], writes=([bank_b[7]] if t == 0 else []))
                bank_b[7].w = ('c', 'pe', len(S.ops['pe']) - 1)
                bank_b[7].r = []
                S.op('act', mk('copy', out=mixb[i][:, (bb - 1) * 128:bb * 128], in_=ps[:, 7, 0:128]), reads=[bank_b[7]], writes=[mb_b[i]])

        def epi_q(n, g, banks, c0, cl):
            i = n % 2
            S.op('act', mk('copy', out=qb[i][:, c0:c0 + cl], in_=ps[:, banks[0], 0:cl]), reads=[bank_b[banks[0]]], writes=[q_b[i]])

        def post_q(h):
            attn_head(h)
            pdma(G.mix_d[h][:, 0:1024], mixb[h % 2][:, 0:1024], "mx%d" % (h % 2), reads=[mb_b[h % 2]])

        gemm([G.win_d], KC, list(range(16)), act, groups, epi_q, post_n=post_q, act_reads=[h_b])

        if stop == "mix_attn":
            return
        for s in range(16):
            pdma(kc_f, G.ck_d[s], "sk", writes=[kcf_b])
            pdma(vc_f, G.cv_d[s], "sv", writes=[vcf_b])
            S.op('act', mk('copy', out=kc_b, in_=kc_f), reads=[kcf_b], writes=[kcb_b])
            S.op('dve', mk('tensor_copy', out=vc_b, in_=vc_f), reads=[vcf_b], writes=[vcb_b])
            bk = 4 + s % 2
            pv = ps[:, bk, :].bitcast(BF16)
            for kv in range(4):
                S.op('pe', mk('transpose', pv[:, kv * 128:(kv + 1) * 128], kc_b[:, kv * 128:(kv + 1) * 128], ident_bf),
                     reads=[kcb_b, const_b], writes=[bank_b[bk]])
            S.op('act', mk('copy', out=kcT, in_=pv[:, 0:512].rearrange("p (k d) -> p k d", k=4)), reads=[bank_b[bk]], writes=[kct_b])
            pv2 = ps[0:8, 6, :].bitcast(BF16)
            for kv in range(4):
                S.op('pe', mk('transpose', pv2[:, kv * 128:(kv + 1) * 128], vT[:, kv, 1152 + 8 * s:1160 + 8 * s], ident_bf),
                     reads=[v_b, const_b], writes=[bank_b[6]])
            S.op('act', mk('copy', out=vnw, in_=pv2[:, 0:512].rearrange("p (k d) -> p k d", k=4)), reads=[bank_b[6]], writes=[vnw_b])
            for kv in range(4):
                j = ucnt["u"] % 2
                ucnt["u"] += 1
                sbk = 4 + j
                ql = qTs[:, s, 4 * kv:4 * kv + 4, :].rearrange("p g t -> p (g t)")
                S.op('pe', mk('matmul', ps[0:32, sbk, 0:128], lhsT=ql, rhs=kcT[:, kv, :], start=True, stop=True),
                     reads=[qs_b, kct_b], writes=[bank_b[sbk]])
                S.op('pe', mk('matmul', ps[0:32, sbk, 128:136], lhsT=ql, rhs=kT[:, kv, 1152 + 8 * s:1160 + 8 * s], start=True, stop=True),
                     reads=[qs_b, k_b], writes=[bank_b[sbk]])
                softmax_rows(32, 136, sbk, biasS[:, kv, :], [bs_b], sinkS[:, kv:kv + 1], j, False)
                ptv = ps[:, 6, :].bitcast(BF16)
                S.op('pe', mk('transpose', ptv[:, 0:32], pn_sb[j][0:32, 0:128], ident_bf[0:32, 0:32]),
                     reads=[pn_b[j], const_b], writes=[bank_b[6]])
                S.op('pe', mk('transpose', ptv[0:8, 32:64], pn_sb[j][0:32, 128:136], ident_bf[0:32, 0:32]),
                     reads=[pn_b[j], const_b], writes=[bank_b[6]])
                S.op('act', mk('copy', out=pT_sb[j][:, 0, 0:32], in_=ptv[:, 0:32]), reads=[bank_b[6]], writes=[pt_b[j]])
                S.op('act', mk('copy', out=pT2, in_=ptv[0:8, 32:64]), reads=[bank_b[6]], writes=[pt2_b])
                S.op('pe', mk('matmul', ps[:, 7, 0:32], lhsT=vc_b[:, kv * 128:(kv + 1) * 128], rhs=pT_sb[j][:, 0, 0:32], start=True, stop=False),
                     reads=[vcb_b, pt_b[j]], writes=[bank_b[7]])
                S.op('pe', mk('matmul', ps[:, 7, 0:32], lhsT=vnw[:, kv, :], rhs=pT2, start=False, stop=True),
                     reads=[vnw_b, pt2_b], writes=[bank_b[7]])
                S.op('act', mk('copy', out=mixS[:, 4 * kv:4 * kv + 4, 8 * s:8 * s + 8], in_=ps[:, 7, 0:32].rearrange("p (g t) -> p g t", g=4)),
                     reads=[bank_b[7]], writes=[ms_b])
        pdma(G.mix_d[0:16, :, 1024:1152].rearrange("h p t -> p h t"), mixS, "o7", reads=[ms_b])
        barrier()

    def phase_outproj(hT, h_b):
        A.off = PH0
        mixed = A.get(KC * T1 * 2, BF16, (KC, T1))
        xo = [A.get(T1 * 4, F32) for _ in range(2)]
        xn = [A.get(T1 * 4, F32) for _ in range(2)]
        rt = A.get(128 * 4, F32, (16, 8))
        m_b = Buf()
        xo_b, xn_b, rt_b = [Buf(), Buf()], [Buf(), Buf()], Buf()
        for q in range(4):
            pdma(mixed[:, 8 * q:8 * q + 8, :], G.mix_d[8 * q:8 * q + 8].rearrange("c p t -> p c t"), "ml%d" % q, writes=[m_b])
        load_gate(1)
        pre_n, epi, post_n = make_resid(T1, G.x1_d, 128, G.x2_d, 0, T1, xo, xo_b, xn, xn_b, rt, rt_b, "w")
        gemm([G.wout_d], KC, range(KC), lambda k, c0, cl: mixed[:, k, c0:c0 + cl], col_groups(0, T1), epi, pre_n=pre_n, post_n=post_n, act_reads=[m_b])
        barrier()

    def phase_final():
        A.off = PH0
        xbuf = [A.get(T0 * 4, F32) for _ in range(NXB)]
        sq = [A.get(T0 * 4, F32) for _ in range(NXB)]
        rstd = A.get(T0 * 4, F32)
        yb = [A.get(T0 * 4, F32) for _ in range(NXB)]
        xb_b, sq_b, yb_b = [Buf() for _ in range(NXB)], [Buf() for _ in range(NXB)], [Buf() for _ in range(NXB)]
        rstd_b = Buf()
        norm_stats(G.x3_d, T1, xbuf, xb_b, sq, sq_b, rstd, rstd_b)
        for c in range(KC):
            i = c % NXB
            pdma(xbuf[i][:, 0:T1], G.x3_d[c], "xb%d" % i, writes=[xb_b[i]])
            S.op('dve', mk('scalar_tensor_tensor', out=yb[i][:, 0:T1], in0=xbuf[i][:, 0:T1], scalar=gains[:, 3, c:c + 1],
                                                                   in1=rstd[:, 0:T1], op0=ALU.mult, op1=ALU.mult),
                 reads=[xb_b[i], rstd_b, const_b], writes=[yb_b[i]])
            t = pdma(G.yT_d[c], yb[i][:, 0:T1], "yo%d" % i, reads=[yb_b[i]])

    hT = arena[:, PH0:PH0 + KC * T0 * 2].bitcast(BF16).rearrange("p (c t) -> p c t", c=KC)
    hT1 = arena[:, PH0:PH0 + KC * T1 * 2].bitcast(BF16).rearrange("p (c t) -> p c t", c=KC)
    h_b = Buf()
    def _run_phases():
        setup()
        if stop_after == "setup":
            pdma(G.dbg_c_d, ident_f, "dbgc", reads=[const_b])
            return
        phase_relbias()
        ms = ModStream()
        for _ in range(16):
            ms.step()
        ms.flush(0, 2)
        barrier()
        if stop_after == "mod":
            for _ in range(56):
                ms.step()
            ms.flush(2, 9)
            return
        phase_norm(G.xT_d, T0, 0, hT, h_b, nxb=2)
        if stop_after == "norm1":
            pdma(G.dbg_h_d, hT, "dbgh", reads=[h_b])
            return

        def up_hook(f):
            if f is None:
                while ms.g < 72:
                    ms.step()
                ms.flush(2, 9)
            elif ms.g < 72 and (f % 3 != 2):
                ms.step()

        phase_ffn(G.w1a_d, G.w3a_d, G.w2a_d, T0, G.xT_d, 0, G.x1_d, hT, h_b, 0, up_only=(stop_after == "ffn1up"), up_hook=up_hook)
        if stop_after in ("ffn1up", "ffn1"):
            return
        phase_norm(G.x1_d, T0, 1, hT, h_b, nxb=2)
        phase_mixer(hT, h_b, stop_after)
        if stop_after in ("mixer", "mix_conv", "mix_kv", "mix_attn"):
            return
        phase_outproj(hT, h_b)
        if stop_after == "outproj":
            return
        phase_norm(G.x2_d, T1, 2, hT1, h_b, nxb=2)
        phase_ffn(G.w1b_d, G.w3b_d, G.w2b_d, T1, G.x2_d, 0, G.x3_d, hT1, h_b, 2)
        phase_final()

    _run_phases()
    barrier()

    sem_cm = {e: nc.semaphore("s_" + e) for e in ENG}
    sems = {e: cm.__enter__() for e, cm in sem_cm.items()}
    dkeys = sorted(S.dma_cnt.keys())
    dcm = {k: nc.semaphore("d_" + k) for k in dkeys}
    dma_sems = {k: cm.__enter__() for k, cm in dcm.items()}
    if os.environ.get('KSIM'):
        S.simulate()
    with nc.Block() as block:
        S.emit(nc, block, sems, dma_sems)
    for cm in list(dcm.values()) + list(sem_cm.values()):
        cm.__exit__(None, None, None)
    psum_cm.__exit__(None, None, None)
    arena_cm.__exit__(None, None, None)
    return nc


def _wl(w):
    K, N = w.shape
    return np.ascontiguousarray(w.reshape(K // 128, 128, N // 128, 128).transpose(2, 1, 0, 3)).reshape(N // 128, 128, K)


def _fm(a):
    r, f = a.shape
    return np.ascontiguousarray(a.T).reshape(f // 128, 128, r)


_PROG = {}


def make_in_maps(x_prompt, x_sample, c_prompt, c_sample, cache_k, cache_v, state_conv, rel_bias,
           g_ffn1, w1_ffn1, w3_ffn1, w2_ffn1, g_mix, w_in, sinks, conv_w, w_out,
           g_ffn2, w1_ffn2, w3_ffn2, w2_ffn2, w_ada, b_ada, g_final):
    f32 = np.float32
    A_ = lambda a: np.asarray(a, dtype=f32)
    xp = A_(x_prompt)[0]
    xs = A_(x_sample)
    cp, cs = A_(c_prompt), A_(c_sample)
    ck, cv, sc = A_(cache_k)[0], A_(cache_v)[0], A_(state_conv)[0]
    shared = {
        "badaT": np.ascontiguousarray(A_(b_ada)[0].reshape(288, 128).T),
        "w_ada": np.ascontiguousarray(A_(w_ada)[0].reshape(KC, 128, 72, 512).transpose(2, 1, 0, 3)).reshape(72, 128, KC * 512),
        "w1a": _wl(A_(w1_ffn1)[0]), "w3a": _wl(A_(w3_ffn1)[0]), "w2a": _wl(A_(w2_ffn1)[0]),
        "w_in": _wl(A_(w_in)[0]), "w_out": _wl(A_(w_out)[0]),
        "w1b": _wl(A_(w1_ffn2)[0]), "w3b": _wl(A_(w3_ffn2)[0]), "w2b": _wl(A_(w2_ffn2)[0]),
        "gains": np.ascontiguousarray(np.stack([A_(g_ffn1)[0], A_(g_mix)[0], A_(g_ffn2)[0], A_(g_final)]).reshape(4, KC, 128).transpose(2, 0, 1)),
        "convw": np.ascontiguousarray(A_(conv_w)[0].reshape(3, 16, 128).transpose(2, 1, 0)),
        "sinks_bc": np.ascontiguousarray(np.broadcast_to(A_(sinks)[0][None, :], (128, 16))),
        "sinkS": np.ascontiguousarray(np.repeat(A_(sinks)[0].reshape(4, 4).T, 8, axis=0)),
        "relb": np.ascontiguousarray(A_(rel_bias)),
    }
    m = np.arange(384)
    dist = 255 - m
    valid = (dist >= 0) & (dist <= 128)
    bucket = t5_bucket_np(dist)
    oh = np.zeros((33, 384), f32)
    oh[bucket[valid], m[valid]] = 1.0
    oh[32, ~valid] = NEG
    shared["onehot"] = oh

    in_maps = []
    for i in range(NCORES):
        p0 = 1024 * i
        halo_rows = xp[p0 - 128:p0] if i > 0 else xp[0:128]
        rows = np.concatenate([halo_rows, xp[p0:p0 + 1024], xs[16 * i:16 * i + 16].reshape(128, D)], axis=0)
        crow = np.concatenate([cp[0:1], cs[16 * i:16 * i + 16]], axis=0)
        mp = dict(shared)
        mp["xT"] = _fm(rows)
        mp["cT"] = np.ascontiguousarray(_fm(crow).transpose(1, 0, 2))
        mp["halo"] = np.ascontiguousarray(np.broadcast_to(np.array([[0.0, NEG]] if i == 0 else [[1.0, 0.0]], f32), (128, 2)))
        mp["stT"] = np.ascontiguousarray(sc[16 * i:16 * i + 16].reshape(16, 2, 16, 128).transpose(3, 2, 0, 1))
        mp["ck"] = np.ascontiguousarray(ck[16 * i:16 * i + 16].reshape(16, 128, 512))
        mp["cv"] = np.ascontiguousarray(cv[16 * i:16 * i + 16].reshape(16, 128, 512))
        in_maps.append(mp)

    return in_maps


def kernel(**inputs):
    f32 = np.float32
    in_maps = make_in_maps(**inputs)
    if "nc" not in _PROG:
        _PROG["nc"] = build_program()
    res = run_bass_kernel_spmd(_PROG["nc"], in_maps, core_ids=list(range(NCORES)))
    R = res.results

    def tm(a):
        C, P, T = a.shape
        return np.ascontiguousarray(a.reshape(C * P, T).T)

    y_prompt = np.empty((1, 8192, D), f32)
    y_sample = np.empty((128, 8, D), f32)
    k_ws = np.empty((1, 128, 128, 4, 128), f32)
    v_ws = np.empty((1, 128, 128, 4, 128), f32)
    conv_s = np.empty((1, 128, 2, 2048), f32)
    for i in range(NCORES):
        y = tm(R[i]["yT"])
        y_prompt[0, 1024 * i:1024 * i + 1024] = y[0:1024]
        y_sample[16 * i:16 * i + 16] = y[1024:1152].reshape(16, 8, D)
        kn = tm(R[i]["knewT"]).reshape(16, 8, 4, 128)
        vn = tm(R[i]["vnewT"]).reshape(16, 8, 4, 128)
        k_ws[0, 16 * i:16 * i + 16, 0:120] = R[i]["kwsc"].reshape(16, 120, 4, 128)
        k_ws[0, 16 * i:16 * i + 16, 120:128] = kn
        v_ws[0, 16 * i:16 * i + 16, 0:120] = R[i]["vwsc"].reshape(16, 120, 4, 128)
        v_ws[0, 16 * i:16 * i + 16, 120:128] = vn
        co = R[i]["convo"]
        conv_s[0, 16 * i:16 * i + 16] = co[:, :, 1:, :].transpose(2, 3, 1, 0).reshape(16, 2, 2048)
    last = R[NCORES - 1]
    k_wp = tm(last["kwinT"]).reshape(1, 1, 128, 4, 128)
    v_wp = tm(last["vwinT"]).reshape(1, 1, 128, 4, 128)
    conv_p = np.ascontiguousarray(last["convo"][:, :, 0, :].transpose(2, 1, 0)).reshape(1, 1, 2, 2048)
    return (y_prompt, y_sample, k_wp, v_wp, conv_p, k_ws, v_ws, conv_s)
```

```python
import math
import os
import numpy as np
import concourse.bass as bass
import concourse.mybir as mybir
from concourse.bass_utils import run_bass_kernel_spmd

F32 = mybir.dt.float32
BF16 = mybir.dt.bfloat16
U8 = mybir.dt.uint8
AF = mybir.ActivationFunctionType
ALU = mybir.AluOpType
AX = mybir.AxisListType

NCORES = 8
D = 4096
KC = 32
DFF = 11008
FC = 86
T0 = 1280
T1 = 1152
NR = 17
PIECE = 8
NXB = int(os.environ.get('KNXB', '4'))
NS = 4
NW = 10
EPS = 1e-6
NEG = -1e30
SCALE = 128 ** -0.5
ENG = ('pe', 'act', 'dve', 'pool', 'sp')


class Buf:
    __slots__ = ('w', 'r')

    def __init__(self):
        self.w = None
        self.r = []


class Sched:
    def __init__(self):
        self.ops = {e: [] for e in ENG}
        self.dma_cnt = {}
        self.out_tickets = []

    def op(self, eng, fn, reads=(), writes=(), deps=(), dma=None, dep_only=(), reg_only=()):
        d = list(deps)
        for b in list(reads) + list(dep_only):
            if b.w is not None:
                d.append(b.w)
        for b in writes:
            if b.w is not None:
                d.append(b.w)
            d.extend(b.r)
        idx = len(self.ops[eng])
        if dma is not None:
            n = self.dma_cnt.get(dma, 0) + 1
            self.dma_cnt[dma] = n
            t = ('d', dma, 16 * n)
        else:
            t = ('c', eng, idx)
        self.ops[eng].append([fn, d, False, dma])
        for b in list(reads) + list(reg_only):
            b.r.append(t)
        for b in writes:
            b.w = t
            b.r = []
        return t


    def simulate(self):
        for e in ENG:
            for o in self.ops[e]:
                for t in o[1]:
                    if t[0] == 'c' and not (t[1] == 'pe' and e == 'pe'):
                        assert self.ops[t[1]][t[2]][3] is None, ("compute ticket on a DMA op", e, t)
                        self.ops[t[1]][t[2]][2] = True
        cum = {}
        for e in ENG:
            c = 0
            arr = []
            for o in self.ops[e]:
                if o[2] and o[3] is None:
                    c += 1
                arr.append(c)
            cum[e] = arr
        semc = {e: 0 for e in ENG}
        semd = {}
        pc = {e: 0 for e in ENG}
        progress = True
        while progress:
            progress = False
            for e in ENG:
                while pc[e] < len(self.ops[e]):
                    fn, deps, inc, dma = self.ops[e][pc[e]]
                    ok = True
                    for t in deps:
                        if t[0] == 'c':
                            if t[1] == 'pe' and e == 'pe':
                                continue
                            if semc[t[1]] < cum[t[1]][t[2]]:
                                ok = False
                                break
                        else:
                            if semd.get(t[1], 0) < t[2]:
                                ok = False
                                break
                    if not ok:
                        break
                    if dma is not None:
                        semd[dma] = semd.get(dma, 0) + 16
                    elif inc:
                        semc[e] += 1
                    pc[e] += 1
                    progress = True
        stuck = {e: (pc[e], len(self.ops[e])) for e in ENG if pc[e] < len(self.ops[e])}
        if stuck:
            print("DEADLOCK", stuck)
            for e in stuck:
                fn, deps, inc, dma = self.ops[e][pc[e]]
                for t in deps:
                    if t[0] == 'c' and not (t[1] == 'pe' and e == 'pe'):
                        if semc[t[1]] < cum[t[1]][t[2]]:
                            print("  ", e, "op", pc[e], "waits", t, "need", cum[t[1]][t[2]], "have", semc[t[1]], "target pc", pc[t[1]])
                    elif t[0] == 'd' and semd.get(t[1], 0) < t[2]:
                        print("  ", e, "op", pc[e], "waits dma", t, "have", semd.get(t[1], 0))
        else:
            print("simulate: no deadlock; ops", {e: len(self.ops[e]) for e in ENG})
        return not stuck

    def emit(self, nc, block, sems, dma_sems):
        for e in ENG:
            for o in self.ops[e]:
                for t in o[1]:
                    if t[0] == 'c' and not (t[1] == 'pe' and e == 'pe'):
                        self.ops[t[1]][t[2]][2] = True
        cum = {}
        for e in ENG:
            c = 0
            arr = []
            for o in self.ops[e]:
                if o[2]:
                    c += 1
                arr.append(c)
            cum[e] = arr
        ops = self.ops

        def run(e, eng):
            waited = {}
            for o in ops[e]:
                fn, deps, inc, dma = o
                need = {}
                for t in deps:
                    if t[0] == 'c':
                        if t[1] == 'pe' and e == 'pe':
                            continue
                        key = ('c', t[1])
                        val = cum[t[1]][t[2]]
                    else:
                        key = ('d', t[1])
                        val = t[2]
                    if val > need.get(key, 0):
                        need[key] = val
                for key, val in need.items():
                    if waited.get(key, 0) < val:
                        sem = sems[key[1]] if key[0] == 'c' else dma_sems[key[1]]
                        eng.wait_ge(sem, val)
                        waited[key] = val
                ins = fn(eng)
                if dma is not None:
                    ins.then_inc(dma_sems[dma], 16)
                elif inc:
                    ins.then_inc(sems[e], 1)

        @block.tensor
        def _(eng):
            run('pe', eng)

        @block.scalar
        def _(eng):
            run('act', eng)

        @block.vector
        def _(eng):
            run('dve', eng)

        @block.gpsimd
        def _(eng):
            run('pool', eng)

        @block.sync
        def _(eng):
            run('sp', eng)


def mk(method, *args, **kw):
    return lambda e: getattr(e, method)(*args, **kw)


def col_groups(c0, n):
    out = []
    while n > 0:
        m = min(512, n)
        out.append((c0, m))
        c0 += m
        n -= m
    return out


def t5_bucket_np(dist):
    n = np.maximum(dist, 0)
    nf = np.maximum(n, 1).astype(np.float32)
    large = 16 + (np.log(nf / np.float32(16)) / np.float32(math.log(128 / 16)) * np.float32(16)).astype(np.int32)
    large = np.minimum(large, 31)
    return np.where(n < 16, n, large)


def build_program(stop_after=None, debug=False):
    nc = bass.Bass("TRN2", target_bir_lowering=False)
    S = Sched()

    def din(name, shape, dt=F32):
        return nc.dram_tensor(name, list(shape), dt, kind="ExternalInput").ap()

    def dout(name, shape, dt=F32):
        return nc.dram_tensor(name, list(shape), dt, kind="ExternalOutput").ap()

    def dint(name, shape, dt=F32):
        return nc.dram_tensor(name, list(shape), dt, kind="Internal").ap()

    DSPEC = {
        'xT_d': ('in', 'xT', [KC, 128, T0], F32),
        'cT_d': ('in', 'cT', [128, KC, NR], F32),
        'bada_d': ('in', 'badaT', [128, 288], F32),
        'wada_d': ('in', 'w_ada', [72, 128, KC * 512], F32),
        'w1a_d': ('in', 'w1a', [FC, 128, D], F32),
        'w3a_d': ('in', 'w3a', [FC, 128, D], F32),
        'w2a_d': ('in', 'w2a', [KC, 128, DFF], F32),
        'win_d': ('in', 'w_in', [72, 128, D], F32),
        'wout_d': ('in', 'w_out', [KC, 128, D], F32),
        'w1b_d': ('in', 'w1b', [FC, 128, D], F32),
        'w3b_d': ('in', 'w3b', [FC, 128, D], F32),
        'w2b_d': ('in', 'w2b', [KC, 128, DFF], F32),
        'gains_d': ('in', 'gains', [128, 4, KC], F32),
        'convw_d': ('in', 'convw', [128, 16, 3], F32),
        'sinks_d': ('in', 'sinks_bc', [128, 16], F32),
        'sinkS_d': ('in', 'sinkS', [32, 4], F32),
        'relb_d': ('in', 'relb', [32, 16], F32),
        'onehot_d': ('in', 'onehot', [33, 384], F32),
        'halo_d': ('in', 'halo', [128, 2], F32),
        'stT_d': ('in', 'stT', [128, 16, 16, 2], F32),
        'ck_d': ('in', 'ck', [16, 128, 512], F32),
        'cv_d': ('in', 'cv', [16, 128, 512], F32),
        'yT_d': ('out', 'yT', [KC, 128, T1], F32),
        'kwin_d': ('out', 'kwinT', [4, 128, 128], F32),
        'vwin_d': ('out', 'vwinT', [4, 128, 128], F32),
        'knew_d': ('out', 'knewT', [4, 128, 128], F32),
        'vnew_d': ('out', 'vnewT', [4, 128, 128], F32),
        'kwsc_d': ('out', 'kwsc', [16, 120, 512], F32),
        'vwsc_d': ('out', 'vwsc', [16, 120, 512], F32),
        'convo_d': ('out', 'convo', [128, 16, NR, 2], F32),
        'aT_d': ('int', 'aT_s', [FC, 128, T0], BF16),
        'x1_d': ('int', 'x1T_s', [KC, 128, T0], F32),
        'x2_d': ('int', 'x2T_s', [KC, 128, T1], F32),
        'x3_d': ('int', 'x3T_s', [KC, 128, T1], F32),
        'mix_d': ('int', 'mix_s', [KC, 128, T1], BF16),
        'mod_d': ('int', 'mod_s', [128, 9, KC, NR], F32),
        'R_d': ('int', 'R_s', [128, 16, 384], F32),
        'dbg_h_d': ('out', 'dbg_h', [128, KC, T0], BF16),
        'dbg_c_d': ('out', 'dbg_c', [128, 128], F32),
    }

    class _G:
        def __init__(self):
            self.c = {}

        def __getattr__(self, var):
            c = self.__dict__["c"]
            if var not in c:
                kind, nm, shape, dt = DSPEC[var]
                k = {"in": "ExternalInput", "out": "ExternalOutput", "int": ("ExternalOutput" if debug else "Internal")}[kind]
                c[var] = nc.dram_tensor(nm, list(shape), dt, kind=k).ap()
            return c[var]

    G = _G()

    ARENA = int(os.environ.get('KARENA', '207')) * 1024
    arena_cm = nc.sbuf_tensor("arena", [128, ARENA], U8)
    psum_cm = nc.psum_tensor("ps", [128, 8, 512], F32)
    arena = arena_cm.__enter__()
    ps = psum_cm.__enter__()

    class Alloc:
        def __init__(self):
            self.off = 0

        def get(self, nbytes, dt, shape=None, parts=128):
            o = self.off
            self.off = (o + nbytes + 63) // 64 * 64
            assert self.off <= ARENA, ("arena overflow", self.off)
            v = arena[0:parts, o:o + nbytes].bitcast(dt)
            if shape is not None and len(shape) > 1:
                names = " ".join("abcdefg"[i] for i in range(len(shape)))
                kw = {"abcdefg"[i]: shape[i] for i in range(len(shape) - 1)}
                v = v.rearrange("p (%s) -> p %s" % (names, names), **kw)
            return v

    A = Alloc()
    ident_bf = A.get(256, BF16)
    ident_f = A.get(512, F32)
    ones_f = A.get(512, F32)
    gains = A.get(4 * KC * 4, F32, (4, KC))
    halo = A.get(8, F32)
    convw = A.get(16 * 3 * 4, F32, (16, 3))
    sinks = A.get(64, F32)
    par = A.get(3 * KC * NR * 4, F32, (3, KC, NR))
    junk = A.get(64, F32)
    eps_ap = A.get(4, F32)
    stage = [A.get(PIECE * 128 * 4, F32, (PIECE, 128)) for _ in range(NS)]
    wbf = [A.get(PIECE * 128 * 2, BF16, (PIECE, 128)) for _ in range(NW)]
    stage_b = [Buf() for _ in range(NS)]
    wbf_b = [Buf() for _ in range(NW)]
    PH0 = A.off
    bank_b = [Buf() for _ in range(8)]
    par_b = Buf()
    parg_b = Buf()
    const_b = Buf()

    st = {"wi": 0, "gi": 0, "cast": 0}

    def barrier():
        ts = []
        for e in ('pe', 'act', 'dve', 'pool'):
            for idx in range(len(S.ops[e]) - 1, -1, -1):
                if S.ops[e][idx][3] is None:
                    ts.append(('c', e, idx))
                    break
        for k, n in S.dma_cnt.items():
            if not k.startswith("st"):
                ts.append(('d', k, 16 * n))
        S.op('pe', mk('matmul', ps[:, 7, 0:8], lhsT=ident_f[:, 0:128], rhs=ident_f[:, 0:8], start=True, stop=True), deps=ts)
        S.op('act', mk('copy', out=junk[:, 0:4], in_=junk[:, 4:8]), deps=ts)
        S.op('dve', mk('tensor_copy', out=junk[:, 8:12], in_=junk[:, 12:16]), deps=ts)
        S.op('pool', mk('memset', junk[:, 14:16], 0.0), deps=ts)

    def pdma(out, in_, key, reads=(), writes=(), deps=()):
        return S.op('pool', mk('dma_start', out=out, in_=in_), reads=reads, writes=writes, deps=deps, dma=key)

    def gemm(mats, nk, n_list, act, groups, epilogue, pre_n=None, post_n=None, act_reads=(), k_outer=False, act_reads_k=None):
        pieces = []
        k0 = 0
        while k0 < nk:
            m = min(PIECE, nk - k0)
            pieces.append((k0, m))
            k0 += m
        n_list = list(n_list)
        R = len(mats) * len(pieces)
        tiles = [(n, mi, pi) for n in n_list for mi in range(len(mats)) for pi in range(len(pieces))]
        slot_of = {}
        ld = {"j": 0}

        def load_upto(jmax):
            jmax = min(jmax, len(tiles))
            while ld["j"] < jmax:
                n_, mi, pi = tiles[ld["j"]]
                ld["j"] += 1
                k0, m = pieces[pi]
                wi = st["wi"]
                st["wi"] += 1
                ss, ws = wi % NS, wi % NW
                src = mats[mi][n_][:, k0 * 128:(k0 + m) * 128].rearrange("p (k j) -> p k j", k=m)
                S.op('sp', lambda e, o=stage[ss][:, 0:m, :], i=src: e.dma_start(out=o, in_=i),
                     writes=[stage_b[ss]], dma="st%d" % ss)
                ceng = 'act' if (st["cast"] % 2 == 0) else 'dve'
                st["cast"] += 1
                if ceng == 'act':
                    S.op('act', lambda e, o=wbf[ws][:, 0:m, :], i=stage[ss][:, 0:m, :]: e.copy(out=o, in_=i),
                         reads=[stage_b[ss]], writes=[wbf_b[ws]])
                else:
                    S.op('dve', lambda e, o=wbf[ws][:, 0:m, :], i=stage[ss][:, 0:m, :]: e.tensor_copy(out=o, in_=i),
                         reads=[stage_b[ss]], writes=[wbf_b[ws]])
                slot_of[(n_, mi, pi)] = ws

        for ni, n in enumerate(n_list):
            if pre_n is not None:
                pre_n(n)
            if not k_outer:
                assert R <= NW
                load_upto(ni * R + NW)
                slots = [[slot_of[(n, mi, pi)] for pi in range(len(pieces))] for mi in range(len(mats))]
            ng = len(groups)
            if k_outer:
                assert len(mats) == 1 and ng <= 2
                bks = []
                for _g in range(ng):
                    bks.append(st["gi"] % 4)
                    st["gi"] += 1
                cnt = 0
                for pi, (k0, m) in enumerate(pieces):
                    load_upto(ni * R + pi + NW)
                    ws = slot_of[(n, 0, pi)]
                    for kk in range(m):
                        first = (cnt == 0)
                        last = (cnt == nk - 1)
                        for gi_, (c0, cl) in enumerate(groups):
                            bk = bks[gi_]
                            kw = {}
                            ark = act_reads_k(k0) if act_reads_k is not None else []
                            if first:
                                kw["writes"] = [bank_b[bk]]
                                kw["dep_only"] = list(act_reads)
                            if kk == 0 and gi_ == 0:
                                kw["dep_only"] = kw.get("dep_only", []) + [wbf_b[ws]] + ark
                            if kk == m - 1 and gi_ == ng - 1:
                                kw["reg_only"] = [wbf_b[ws]] + ark
                            if last and gi_ == ng - 1:
                                kw["reg_only"] = kw.get("reg_only", []) + list(act_reads)
                            t = S.op('pe', lambda e, o=ps[:, bk, 0:cl], l=wbf[ws][:, kk, :], r=act(k0 + kk, c0, cl), f=first, la=last:
                                     e.matmul(o, lhsT=l, rhs=r, start=f, stop=la), **kw)
                            if last:
                                bank_b[bk].w = t
                                bank_b[bk].r = []
                        cnt += 1
                for gi_, (c0, cl) in enumerate(groups):
                    epilogue(n, gi_, [bks[gi_]], c0, cl)
                if post_n is not None:
                    post_n(n)
                continue
            ng = len(groups)
            for gi_, (c0, cl) in enumerate(groups):
                banks = []
                for mi in range(len(mats)):
                    bk = st["gi"] % 4
                    st["gi"] += 1
                    banks.append(bk)
                    tot = nk
                    cnt = 0
                    for pi, (k0, m) in enumerate(pieces):
                        ws = slots[mi][pi]
                        for kk in range(m):
                            first = (cnt == 0)
                            last = (cnt == tot - 1)
                            kw = {}
                            if first:
                                kw["writes"] = [bank_b[bk]]
                                kw["dep_only"] = list(act_reads)
                            if kk == 0:
                                kw["dep_only"] = kw.get("dep_only", []) + [wbf_b[ws]]
                            if kk == m - 1 and gi_ == ng - 1:
                                kw["reg_only"] = [wbf_b[ws]]
                            if last:
                                kw["reg_only"] = kw.get("reg_only", []) + list(act_reads)
                            t = S.op('pe', lambda e, o=ps[:, bk, 0:cl], l=wbf[ws][:, kk, :], r=act(k0 + kk, c0, cl), f=first, la=last:
                                     e.matmul(o, lhsT=l, rhs=r, start=f, stop=la), **kw)
                            if last:
                                bank_b[bk].w = t
                                bank_b[bk].r = []
                            cnt += 1
                epilogue(n, gi_, banks, c0, cl)
            if post_n is not None:
                post_n(n)

    def setup():
        S.op('dve', mk('memset', ones_f, 1.0), writes=[const_b])
        S.op('dve', mk('memset', ident_f, 0.0), writes=[const_b])
        S.op('pool', mk('affine_select', out=ident_f, in_=ones_f, pattern=[[-1, 128]], compare_op=ALU.is_equal,
                                               fill=0.0, base=0, channel_multiplier=1), writes=[const_b])
        S.op('dve', mk('tensor_copy', out=ident_bf, in_=ident_f), writes=[const_b])
        S.op('dve', mk('memset', junk, 0.0), writes=[const_b])
        S.op('dve', mk('memset', eps_ap, EPS), writes=[const_b])
        pdma(gains, G.gains_d, "c0", writes=[const_b])
        pdma(halo, G.halo_d, "c1", writes=[const_b])
        pdma(convw, G.convw_d, "c2", writes=[const_b])
        pdma(sinks, G.sinks_d, "c3", writes=[const_b])

    class ModStream:
        def __init__(self):
            top = ARENA - 42 * 1024
            self.top = top
            sav = A.off
            A.off = PH0 + 48 * 1024
            self.cT = A.get(KC * NR * 4, F32, (KC, NR))
            A.off = top
            self.scT = A.get(KC * NR * 2, BF16, (KC, NR))
            self.bada = A.get(288 * 4, F32)
            self.modT = A.get(288 * NR * 4, F32, (288, NR))
            self.mtok = [A.get(512 * 4, F32, None, parts=NR) for _ in range(2)]
            self.mst = [A.get(1024 * 4, F32, (2, 512)) for _ in range(2)]
            self.mwb = [A.get(1024 * 2, BF16, (2, 512)) for _ in range(4)]
            A.off = sav
            self.b_c, self.b_sc, self.b_bada, self.b_mod = Buf(), Buf(), Buf(), Buf()
            self.b_mtok = [Buf(), Buf()]
            self.b_mst = [Buf(), Buf()]
            self.b_mwb = [Buf() for _ in range(4)]
            self.loaded = 0
            self.g = 0
            self.cur = 0
            pdma(self.cT, G.cT_d, "m0", writes=[self.b_c])
            pdma(self.bada, G.bada_d, "m1", writes=[self.b_bada])
            S.op('act', mk('activation', out=self.scT, in_=self.cT, func=AF.Silu), reads=[self.b_c], writes=[self.b_sc])

        def _slots(self, j):
            if j < self.UPF:
                ss, ws = j % NS, j % NW
                return (stage[ss].rearrange("p k j -> p (k j)").rearrange("p (k j) -> p k j", k=2), stage_b[ss], "st%d" % ss,
                        wbf[ws].rearrange("p k j -> p (k j)").rearrange("p (k j) -> p k j", k=2), wbf_b[ws])
            ss, ws = j % 2, j % 4
            return (self.mst[ss], self.b_mst[ss], "stm%d" % ss, self.mwb[ws], self.b_mwb[ws])

        UPF = 256

        def _load_upto(self, jmax):
            jmax = min(jmax, 72 * 16)
            while self.loaded < jmax:
                j = self.loaded
                self.loaded += 1
                g, t = divmod(j, 16)
                sv, sb, key, wv, wb = self._slots(j)
                src = G.wada_d[g][:, t * 1024:(t + 1) * 1024].rearrange("p (k j) -> p k j", k=2)
                S.op('sp', mk('dma_start', out=sv, in_=src), writes=[sb], dma=key)
                ceng = 'act' if (st["cast"] % 2 == 0) else 'dve'
                st["cast"] += 1
                S.op(ceng, mk('copy' if ceng == 'act' else 'tensor_copy', out=wv, in_=sv), reads=[sb], writes=[wb])

        def advance(self, ntiles):
            for _ in range(ntiles):
                j = self.cur
                if j >= 72 * 16:
                    return
                g, t = divmod(j, 16)
                bk = 4 + g % 2
                i = g % 2
                self._load_upto(j + 4 if j < self.UPF else j + 1)
                sv, sb, key, wv, wb = self._slots(j)
                for kk in range(2):
                    kw = {}
                    first = (t == 0 and kk == 0)
                    last = (t == 15 and kk == 1)
                    if first:
                        kw["writes"] = [bank_b[bk]]
                        kw["dep_only"] = [self.b_sc]
                    if kk == 0:
                        kw["dep_only"] = kw.get("dep_only", []) + [wb]
                    else:
                        kw["reg_only"] = [wb]
                    tk = S.op('pe', mk('matmul', ps[0:NR, bk, 0:512], lhsT=self.scT[:, 2 * t + kk, :], rhs=wv[:, kk, :],
                                       start=first, stop=last), **kw)
                    if last:
                        bank_b[bk].w = tk
                        bank_b[bk].r = []
                self.cur = j + 1
                if t == 15:
                    self._finish_group(g, bk, i)
            if self.cur >= self.UPF:
                self._load_upto(self.cur + 4)

        def step(self):
            self.advance(16)

        def _finish_group(self, g, bk, i):
            self.g = g + 1
            S.op('act', mk('copy', out=self.mtok[i], in_=ps[0:NR, bk, 0:512]), reads=[bank_b[bk]], writes=[self.b_mtok[i]])
            for c in range(4):
                kw = {"writes": [bank_b[6]]} if c == 0 else {}
                tk = S.op('pe', mk('transpose', ps[:, 6, c * NR:(c + 1) * NR], self.mtok[i][:, c * 128:(c + 1) * 128], ident_f[0:NR, 0:NR]),
                          dep_only=[self.b_mtok[i], const_b], reg_only=([self.b_mtok[i]] if c == 3 else []), **kw)
            bank_b[6].w = tk
            bank_b[6].r = []
            S.op('dve', mk('tensor_tensor', out=self.modT[:, 4 * g:4 * g + 4, :], in0=ps[:, 6, 0:4 * NR].rearrange("p (c r) -> p c r", c=4),
                           in1=self.bada[:, 4 * g:4 * g + 4].unsqueeze(2).broadcast_to([128, 4, NR]), op=ALU.add),
                 reads=[bank_b[6], self.b_bada], writes=[self.b_mod])

        def flush(self, j0, j1):
            pdma(G.mod_d[:, j0:j1, :, :], self.modT[:, j0 * KC:j1 * KC, :].rearrange("p (j c) r -> p j c r", j=j1 - j0), "m2", reads=[self.b_mod])

    def phase_relbias():
        A.off = PH0
        tab = A.get(16 * 4, F32, None, parts=33)
        tabrep = A.get(16 * 128 * 4, F32, (16, 128), parts=33)
        oneh = A.get(384 * 4, F32, None, parts=33)
        Rsb = A.get(16 * 384 * 4, F32, (16, 384))
        b_tab, b_oh, b_R = Buf(), Buf(), Buf()
        S.op('dve', mk('memset', tab[32:33, :], 1.0), writes=[b_tab])
        pdma(tab[0:32, :], G.relb_d, "m3", writes=[b_tab])
        pdma(oneh, G.onehot_d, "m4", writes=[b_oh])
        S.op('dve', mk('tensor_copy', out=tabrep, in_=tab[:, 0:16].unsqueeze(2).broadcast_to([33, 16, 128])),
             reads=[b_tab], writes=[b_tab])
        for h in range(16):
            bk = 4 + (h % 2)
            S.op('pe', mk('matmul', ps[:, bk, 0:384], lhsT=tabrep[:, h, :], rhs=oneh, start=True, stop=True),
                 reads=[b_tab, b_oh], writes=[bank_b[bk]])
            S.op('dve', mk('tensor_copy', out=Rsb[:, h, :], in_=ps[:, bk, 0:384]),
                 reads=[bank_b[bk]], writes=[b_R])
        pdma(G.R_d, Rsb, "m5", reads=[b_R])

    def load_par(s):
        pdma(par[:, 0:2], G.mod_d[:, 3 * s:3 * s + 2, :, :], "par", writes=[par_b])
        S.op('dve', mk('tensor_scalar', out=par[:, 1], in0=par[:, 1], scalar1=1.0, scalar2=None, op0=ALU.add),
             writes=[par_b])
        S.op('dve', mk('tensor_tensor', out=par[:, 1], in0=par[:, 1],
                                              in1=gains[:, s, :].unsqueeze(2).broadcast_to([128, KC, NR]), op=ALU.mult),
             writes=[par_b], reads=[const_b])

    def load_gate(s):
        coef = (0.5, 1.0, 0.5)[s]
        pdma(par[:, 2], G.mod_d[:, 3 * s + 2, :, :], "parg", writes=[parg_b])
        if coef != 1.0:
            S.op('dve', mk('tensor_scalar', out=par[:, 2], in0=par[:, 2], scalar1=coef, scalar2=None, op0=ALU.mult),
                 writes=[parg_b])

    def norm_stats(x_d, T, xbuf, xb_b, sq, sq_b, rstd, rstd_b, nxb=NXB):
        groups = col_groups(0, T)
        ssb = [4, 5, 6]
        for c in range(KC):
            pdma(xbuf[c % nxb][:, 0:T], x_d[c], "xb%d" % (c % nxb), writes=[xb_b[c % nxb]])
            S.op('act', mk('activation', out=sq[c % nxb][:, 0:T], in_=xbuf[c % nxb][:, 0:T], func=AF.Square),
                 reads=[xb_b[c % nxb]], writes=[sq_b[c % nxb]])
            for gi_, (c0, cl) in enumerate(groups):
                kw = {}
                if c == 0:
                    kw["writes"] = [bank_b[ssb[gi_]]]
                t = S.op('pe', lambda e, c=c, bk=ssb[gi_], c0=c0, cl=cl: e.matmul(ps[:, bk, 0:cl], lhsT=ones_f, rhs=sq[c % nxb][:, c0:c0 + cl],
                                                                                  start=(c == 0), stop=(c == KC - 1)),
                         dep_only=[sq_b[c % nxb], const_b], reg_only=[sq_b[c % nxb]], **kw)
                if c == KC - 1:
                    bank_b[ssb[gi_]].w = t
                    bank_b[ssb[gi_]].r = []
        for gi_, (c0, cl) in enumerate(groups):
            S.op('act', lambda e, bk=ssb[gi_], c0=c0, cl=cl: e.activation(out=rstd[:, c0:c0 + cl], in_=ps[:, bk, 0:cl], func=AF.Sqrt,
                                                                         scale=1.0 / D, bias=eps_ap),
                 reads=[bank_b[ssb[gi_]], const_b], writes=[rstd_b])
        S.op('dve', mk('reciprocal', out=rstd[:, 0:T], in_=rstd[:, 0:T]), writes=[rstd_b])

    def phase_norm(x_d, T, s, hT, h_b, nxb=NXB):
        SC = T - 128
        A.off = PH0 + KC * T0 * 2
        xbuf = [A.get(T0 * 4, F32) for _ in range(nxb)]
        sq = [A.get(T0 * 4, F32) for _ in range(nxb)]
        rstd = A.get(T0 * 4, F32)
        tmp = [A.get(T0 * 4, F32) for _ in range(2)]
        tmp3 = A.get(128 * 4, F32, (16, 8))
        xb_b = [Buf() for _ in range(nxb)]
        sq_b = [Buf() for _ in range(nxb)]
        tmp_b = [Buf(), Buf()]
        rstd_b, t3_b = Buf(), Buf()
        load_par(s)
        norm_stats(x_d, T, xbuf, xb_b, sq, sq_b, rstd, rstd_b, nxb)
        for c in range(KC):
            i = c % 2
            xi = c % nxb
            pdma(xbuf[xi][:, 0:T], x_d[c], "xb%d" % xi, writes=[xb_b[xi]])
            S.op('dve', mk('tensor_tensor', out=tmp[i][:, 0:T], in0=xbuf[xi][:, 0:T], in1=rstd[:, 0:T], op=ALU.mult),
                 reads=[xb_b[xi], rstd_b], writes=[tmp_b[i]])
            S.op('act', mk('activation', out=hT[:, c, 0:SC], in_=tmp[i][:, 0:SC], func=AF.Identity,
                                                         scale=par[:, 1, c, 0:1], bias=par[:, 0, c, 0:1]),
                 reads=[tmp_b[i], par_b], writes=[h_b])
            S.op('dve', mk('tensor_tensor', out=tmp3, in0=tmp[i][:, SC:T].rearrange("p (s t) -> p s t", t=8),
                                                            in1=par[:, 1, c, 1:NR].unsqueeze(2).broadcast_to([128, 16, 8]), op=ALU.mult),
                 reads=[tmp_b[i], par_b], writes=[t3_b])
            S.op('dve', mk('tensor_tensor', out=hT[:, c, SC:T].rearrange("p (s t) -> p s t", t=8), in0=tmp3,
                                                       in1=par[:, 0, c, 1:NR].unsqueeze(2).broadcast_to([128, 16, 8]), op=ALU.add),
                 reads=[t3_b, par_b], writes=[h_b])

    def make_resid(T, x_old_d, old_off, x_new_d, c0h, lenh, xo, xo_b, xn, xn_b, rt, rt_b, tag):
        SC = T - 128

        def pre_n(d):
            i = d % 2
            pdma(xo[i][:, 0:lenh], x_old_d[d][:, old_off + c0h:old_off + c0h + lenh], tag + "xo%d" % i, writes=[xo_b[i]])

        def epi(d, g, banks, c0, cl):
            i = d % 2
            bk = banks[0]
            lo, hi = c0, min(c0 + cl, SC)
            if hi > lo:
                S.op('dve', mk('scalar_tensor_tensor', out=xn[i][:, lo - c0h:hi - c0h], in0=ps[:, bk, lo - c0:hi - c0],
                                                              scalar=par[:, 2, d, 0:1], in1=xo[i][:, lo - c0h:hi - c0h],
                                                              op0=ALU.mult, op1=ALU.add),
                     reads=[bank_b[bk], xo_b[i], parg_b], writes=[xn_b[i]])
            lo, hi = max(c0, SC), c0 + cl
            if hi > lo:
                s0, s1 = (lo - SC) // 8, (hi - SC) // 8
                ns = s1 - s0
                S.op('dve', mk('tensor_tensor', out=rt[:, 0:ns, :], in0=ps[:, bk, lo - c0:hi - c0].rearrange("p (s t) -> p s t", t=8),
                                                      in1=par[:, 2, d, 1 + s0:1 + s1].unsqueeze(2).broadcast_to([128, ns, 8]), op=ALU.mult),
                     reads=[bank_b[bk], parg_b], writes=[rt_b])
                S.op('dve', mk('tensor_tensor', out=xn[i][:, lo - c0h:hi - c0h].rearrange("p (s t) -> p s t", t=8), in0=rt[:, 0:ns, :],
                                                      in1=xo[i][:, lo - c0h:hi - c0h].rearrange("p (s t) -> p s t", t=8), op=ALU.add),
                     reads=[rt_b, xo_b[i]], writes=[xn_b[i]])

        def post_n(d):
            i = d % 2
            pdma(x_new_d[d][:, c0h:c0h + lenh], xn[i][:, 0:lenh], tag + "xn%d" % i, reads=[xn_b[i]])

        return pre_n, epi, post_n

    def phase_ffn(w1_d, w3_d, w2_d, T, x_old_d, old_off, x_new_d, hT, h_b, sl, up_only=False, up_hook=None):
        A.off = PH0 + KC * T0 * 2
        aout = [A.get(T0 * 2, BF16) for _ in range(2)]
        sg = [A.get(512 * 4, F32) for _ in range(2)]
        ao_b = [Buf(), Buf()]
        sg_b = [Buf(), Buf()]
        groups = col_groups(0, T)
        cnt = {"g": 0}

        def epi_up(f, g, banks, c0, cl):
            j = cnt["g"] % 2
            cnt["g"] += 1
            i = f % 2
            S.op('act', mk('activation', out=sg[j][:, 0:cl], in_=ps[:, banks[0], 0:cl], func=AF.Silu),
                 reads=[bank_b[banks[0]]], writes=[sg_b[j]])
            S.op('dve', mk('tensor_tensor', out=aout[i][:, c0:c0 + cl], in0=sg[j][:, 0:cl], in1=ps[:, banks[1], 0:cl], op=ALU.mult),
                 reads=[sg_b[j], bank_b[banks[1]]], writes=[ao_b[i]])
            if up_hook is not None:
                up_hook(f)

        def post_up(f):
            i = f % 2
            pdma(G.aT_d[f][:, 0:T], aout[i][:, 0:T], "ao%d" % i, reads=[ao_b[i]])

        gemm([w1_d, w3_d], KC, range(FC), lambda k, c0, cl: hT[:, k, c0:c0 + cl], groups, epi_up, post_n=post_up, act_reads=[h_b])
        if up_hook is not None:
            up_hook(None)
        barrier()
        if up_only:
            return
        load_gate(sl)
        half = T // 2
        for hf in range(2):
            c0h = hf * half
            A.off = PH0
            aTh = A.get(FC * half * 2, BF16, (FC, half))
            xo = [A.get(half * 4, F32) for _ in range(2)]
            xn = [A.get(half * 4, F32) for _ in range(2)]
            rt = A.get(128 * 4, F32, (16, 8))
            a_bs = [Buf() for _ in range(6)]
            xo_b, xn_b, rt_b = [Buf(), Buf()], [Buf(), Buf()], Buf()
            f0 = 0
            qi = 0
            while f0 < FC:
                f1 = min(FC, f0 + 16)
                pdma(aTh[:, f0:f1, :], G.aT_d[f0:f1, :, c0h:c0h + half].rearrange("f p t -> p f t"), "ah%d" % qi, writes=[a_bs[qi]],
                     deps=([a_bs[qi - 2].w] if qi >= 2 else []))
                f0 = f1
                qi += 1
            pre_n, epi, post_n = make_resid(T, x_old_d, old_off, x_new_d, c0h, half, xo, xo_b, xn, xn_b, rt, rt_b, "d")
            gemm([w2_d], FC, range(KC), lambda k, c0, cl: aTh[:, k, c0 - c0h:c0 - c0h + cl], col_groups(c0h, half), epi,
                 pre_n=pre_n, post_n=post_n, k_outer=True, act_reads_k=lambda k0_: [a_bs[k0_ // 16]])
            barrier()

    def phase_mixer(hT, h_b, stop=None):
        A.off = PH0 + KC * T0 * 2
        kT = A.get(4 * T0 * 2, BF16, (4, T0))
        vT = A.get(4 * T0 * 2, BF16, (4, T0))
        vtok = A.get(9 * 4 * 128 * 2, BF16, (9, 4, 128))
        kf = A.get(4 * 256 * 4, F32, (4, 256))
        vf = A.get(4 * 256 * 4, F32, (4, 256))
        mixb = [A.get(T1 * 2, BF16) for _ in range(2)]
        M1 = A.off
        stT = A.get(16 * 16 * 2 * 4, F32, (16, 16, 2))
        convo = A.get(16 * NR * 2 * 4, F32, (16, NR, 2))
        gcb = A.get(T0 * 4, F32)
        ub = A.get(T0 * 4, F32)
        yc = A.get(T1 * 4, F32)
        ucat = A.get(160 * 4, F32, (16, 10))
        k_b, v_b, vt_b, kf_b, vf_b = Buf(), Buf(), Buf(), Buf(), Buf()
        st_b, co_b, gc_b, u_b, yc_b, uc_b = Buf(), Buf(), Buf(), Buf(), Buf(), Buf()
        mb_b = [Buf(), Buf()]
        pdma(stT, G.stT_d, "x0", writes=[st_b])
        pdma(G.kwsc_d, G.ck_d[:, 8:128, :], "o0")
        pdma(G.vwsc_d, G.cv_d[:, 8:128, :], "o1")

        groups = col_groups(0, T0)
        act = lambda k, c0, cl: hT[:, k, c0:c0 + cl]

        def epi_conv(n, g, banks, c0, cl):
            bk = banks[0]
            if 40 <= n < 56:
                S.op('act', mk('copy', out=gcb[:, c0:c0 + cl], in_=ps[:, bk, 0:cl]), reads=[bank_b[bk]], writes=[gc_b])
            elif n >= 56:
                c = n - 56
                S.op('dve', mk('tensor_tensor', out=ub[:, c0:c0 + cl], in0=gcb[:, c0:c0 + cl], in1=ps[:, bk, 0:cl], op=ALU.mult),
                     reads=[bank_b[bk], gc_b], writes=[u_b])
                if g == len(groups) - 1:
                    S.op('dve', mk('tensor_scalar', out=ub[:, 0:128], in0=ub[:, 0:128], scalar1=halo[:, 0:1], scalar2=None, op0=ALU.mult),
                         reads=[const_b], writes=[u_b])
                    S.op('dve', mk('tensor_scalar', out=yc[:, 0:1024], in0=ub[:, 126:1150], scalar1=convw[:, c, 0:1], scalar2=None, op0=ALU.mult),
                         reads=[u_b, const_b], writes=[yc_b])
                    for j in (1, 2):
                        S.op('dve', mk('scalar_tensor_tensor', out=yc[:, 0:1024], in0=ub[:, 126 + j:1150 + j], scalar=convw[:, c, j:j + 1],
                                                                          in1=yc[:, 0:1024], op0=ALU.mult, op1=ALU.add),
                             reads=[u_b, const_b], writes=[yc_b])
                    us = ub[:, 1152:1280].rearrange("p (s t) -> p s t", t=8)
                    S.op('dve', mk('tensor_copy', out=ucat[:, :, 0:2], in_=stT[:, c, :, :]), reads=[st_b], writes=[uc_b])
                    S.op('dve', mk('tensor_copy', out=ucat[:, :, 2:10], in_=us), reads=[u_b], writes=[uc_b])
                    ys = yc[:, 1024:1152].rearrange("p (s t) -> p s t", t=8)
                    S.op('dve', mk('tensor_scalar', out=ys, in0=ucat[:, :, 0:8], scalar1=convw[:, c, 0:1], scalar2=None, op0=ALU.mult),
                         reads=[uc_b, const_b], writes=[yc_b])
                    for j in (1, 2):
                        S.op('dve', mk('scalar_tensor_tensor', out=ys, in0=ucat[:, :, j:j + 8], scalar=convw[:, c, j:j + 1], in1=ys,
                                                                          op0=ALU.mult, op1=ALU.add),
                             reads=[uc_b, const_b], writes=[yc_b])
                    S.op('act', mk('copy', out=convo[:, c, 0, :], in_=ub[:, 1150:1152]), reads=[u_b], writes=[co_b])
                    S.op('act', mk('copy', out=convo[:, c, 1:NR, :], in_=us[:, :, 6:8]), reads=[u_b], writes=[co_b])
            else:
                c = n - 24
                i = c % 2
                lo, hi = max(c0, 128), c0 + cl
                S.op('dve', mk('tensor_tensor', out=mixb[i][:, lo - 128:hi - 128], in0=yc[:, lo - 128:hi - 128], in1=ps[:, bk, lo - c0:hi - c0], op=ALU.mult),
                     reads=[bank_b[bk], yc_b], writes=[mb_b[i]])
                if g == len(groups) - 1:
                    pdma(G.mix_d[16 + c], mixb[i], "mx%d" % i, reads=[mb_b[i]])

        order = []
        for c in range(16):
            order += [40 + c, 56 + c, 24 + c]
        gemm([G.win_d], KC, order, act, groups, epi_conv, act_reads=[h_b])
        pdma(G.convo_d, convo, "o2", reads=[co_b])
        barrier()
        if stop == "mix_conv":
            return
        A.off = M1
        qb = [A.get(T0 * 2, BF16) for _ in range(2)]
        qTs = A.get(16 * 128 * 2, BF16, (16, 16, 8))
        mixS = A.get(16 * 128 * 2, BF16, (16, 128))
        biasP = [A.get(256 * 4, F32) for _ in range(2)]
        biasS = A.get(4 * 136 * 4, F32, (4, 136), parts=32)
        sinkS = A.get(16, F32, None, parts=32)
        s_sb = [A.get(256 * 4, F32) for _ in range(2)]
        p_sb = [A.get(256 * 4, F32) for _ in range(2)]
        pn_sb = [A.get(256 * 2, BF16) for _ in range(2)]
        pT_sb = [A.get(256 * 2, BF16, (2, 128)) for _ in range(2)]
        small = [A.get(8 * 4, F32) for _ in range(2)]
        kc_f = A.get(512 * 4, F32)
        vc_f = A.get(512 * 4, F32)
        kc_b = A.get(512 * 2, BF16)
        vc_b = A.get(512 * 2, BF16)
        kcT = A.get(512 * 2, BF16, (4, 128))
        vnw = A.get(512 * 2, BF16, (4, 128), parts=8)
        pT2 = A.get(32 * 2, BF16, None, parts=8)
        q_b = [Buf(), Buf()]
        qs_b, ms_b = Buf(), Buf()
        bp_b = [Buf(), Buf()]
        bs_b = Buf()
        s_b, p_b, pn_b, pt_b, sm_b = [Buf(), Buf()], [Buf(), Buf()], [Buf(), Buf()], [Buf(), Buf()], [Buf(), Buf()]
        kcf_b, vcf_b, kcb_b, vcb_b, kct_b, vnw_b, pt2_b = Buf(), Buf(), Buf(), Buf(), Buf(), Buf(), Buf()
        if "sink" not in os.environ.get("KSKIP", ""):
            pdma(sinkS, G.sinkS_d, "x1", writes=[bs_b])
        SK = os.environ.get("KSKIP", "")
        for kv in range(4 if "bias" not in SK else 0):
            for g in range(4):
                src = bass.AP(tensor=G.R_d.tensor, offset=(4 * kv + g) * 384 + 127, ap=[[16 * 384 - 1, 8], [1, 136]])
                pdma(biasS[8 * g:8 * g + 8, kv, :], src, "x2", writes=[bs_b])

        def epi_kv(n, g, banks, c0, cl):
            bk = banks[0]
            isk = n < 20
            kv = n - 16 if isk else n - 20
            dst, db = (kT, k_b) if isk else (vT, v_b)
            ff, fb = (kf, kf_b) if isk else (vf, vf_b)
            S.op('act', mk('copy', out=dst[:, kv, c0:c0 + cl], in_=ps[:, bk, 0:cl]), reads=[bank_b[bk]], writes=[db])
            if g == 2:
                S.op('act', mk('copy', out=ff[:, kv, :], in_=ps[:, bk, 0:256]), reads=[bank_b[bk]], writes=[fb])

        gemm([G.win_d], KC, list(range(16, 24)), act, groups, epi_kv, act_reads=[h_b])
        if "kvout" not in SK:
          pdma(G.kwin_d.rearrange("k p t -> p k t"), kf[:, :, 0:128], "o3", reads=[kf_b])
        if "kvout" not in SK:
          pdma(G.knew_d.rearrange("k p t -> p k t"), kf[:, :, 128:256], "o4", reads=[kf_b])
        if "kvout" not in SK:
          pdma(G.vwin_d.rearrange("k p t -> p k t"), vf[:, :, 0:128], "o5", reads=[vf_b])
        if "kvout" not in SK:
          pdma(G.vnew_d.rearrange("k p t -> p k t"), vf[:, :, 128:256], "o6", reads=[vf_b])
        for bb in range(9 if "vtok" not in SK else 0):
            bk = 4 + bb % 2
            pv = ps[:, bk, :].bitcast(BF16)
            for kv in range(4):
                S.op('pe', mk('transpose', pv[:, kv * 128:(kv + 1) * 128], vT[:, kv, bb * 128:(bb + 1) * 128], ident_bf),
                     reads=[v_b, const_b], writes=[bank_b[bk]])
            S.op('act', mk('copy', out=vtok[:, bb, :, :], in_=pv[:, 0:512].rearrange("p (k d) -> p k d", k=4)),
                 reads=[bank_b[bk]], writes=[vt_b])

        if stop == "mix_kv":
            return
        def softmax_rows(P, W, sc_bk, bias_ap, bias_bufs, sink_ap, j, halo_fix):
            s_, p_, pn_, sm = s_sb[j], p_sb[j], pn_sb[j], small[j]
            S.op('dve', mk('scalar_tensor_tensor', out=s_[0:P, 0:W], in0=ps[0:P, sc_bk, 0:W], scalar=SCALE, in1=bias_ap,
                                                          op0=ALU.mult, op1=ALU.add),
                 reads=[bank_b[sc_bk]] + bias_bufs, writes=[s_b[j]])
            if halo_fix:
                S.op('dve', mk('tensor_scalar', out=s_[0:P, 0:128], in0=s_[0:P, 0:128], scalar1=halo[0:P, 1:2], scalar2=None, op0=ALU.add),
                     reads=[const_b], writes=[s_b[j]])
            S.op('dve', mk('tensor_reduce', out=sm[0:P, 0:1], in_=s_[0:P, 0:W], op=ALU.max, axis=AX.X), reads=[s_b[j]], writes=[sm_b[j]])
            S.op('dve', mk('tensor_scalar', out=sm[0:P, 1:2], in0=sm[0:P, 0:1], scalar1=sink_ap, scalar2=-1.0, op0=ALU.max, op1=ALU.mult),
                 reads=bias_bufs, writes=[sm_b[j]])
            S.op('act', mk('activation', out=p_[0:P, 0:W], in_=s_[0:P, 0:W], func=AF.Exp, bias=sm[0:P, 1:2], scale=1.0,
                                               accum_out=sm[0:P, 2:3]),
                 reads=[s_b[j], sm_b[j]], writes=[p_b[j], sm_b[j]])
            S.op('act', mk('activation', out=sm[0:P, 3:4], in_=sm[0:P, 1:2], func=AF.Exp, bias=sink_ap, scale=1.0),
                 reads=bias_bufs, writes=[sm_b[j]])
            S.op('dve', mk('tensor_tensor', out=sm[0:P, 4:5], in0=sm[0:P, 2:3], in1=sm[0:P, 3:4], op=ALU.add), writes=[sm_b[j]])
            S.op('dve', mk('reciprocal', out=sm[0:P, 5:6], in_=sm[0:P, 4:5]), writes=[sm_b[j]])
            S.op('dve', mk('tensor_scalar', out=pn_[0:P, 0:W], in0=p_[0:P, 0:W], scalar1=sm[0:P, 5:6], scalar2=None, op0=ALU.mult),
                 reads=[p_b[j], sm_b[j]], writes=[pn_b[j]])

        ucnt = {"u": 0}

        def attn_head(h):
            kv = h // 4
            i = h % 2
            hb = h % 2
            src = bass.AP(tensor=G.R_d.tensor, offset=h * 384 + 127, ap=[[16 * 384 - 1, 128], [1, 256]])
            pdma(biasP[hb], src, "bp%d" % hb, writes=[bp_b[hb]])
            S.op('dve', mk('tensor_copy', out=qTs[:, :, h, :], in_=qb[i][:, 1152:1280].rearrange("p (s t) -> p s t", t=8)), reads=[q_b[i]], writes=[qs_b])

            def qk(bb):
                bk = 4 + (ucnt["u"] % 2)
                S.op('pe', mk('matmul', ps[:, bk, 0:256], lhsT=qb[i][:, bb * 128:(bb + 1) * 128], rhs=kT[:, kv, (bb - 1) * 128:(bb + 1) * 128],
                                              start=True, stop=True),
                     reads=[q_b[i], k_b], writes=[bank_b[bk]])
                return bk

            nxt = qk(1)
            for bb in range(1, 9):
                j = ucnt["u"] % 2
                ucnt["u"] += 1
                sc_bk = nxt
                softmax_rows(128, 256, sc_bk, biasP[hb], [bp_b[hb], const_b], sinks[:, h:h + 1], j, bb == 1)
                if bb < 8:
                    nxt = qk(bb + 1)
                ptv = ps[:, 6, :].bitcast(BF16)
                for t in range(2):
                    S.op('pe', mk('transpose', ptv[:, t * 128:(t + 1) * 128], pn_sb[j][:, t * 128:(t + 1) * 128], ident_bf),
                         reads=[pn_b[j], const_b], writes=[bank_b[6]])
                S.op('act', mk('copy', out=pT_sb[j], in_=ptv[:, 0:256].rearrange("p (t q) -> p t q", t=2)),
                     reads=[bank_b[6]], writes=[pt_b[j]])
                for t in range(2):
                    S.op('pe', mk('matmul', ps[:, 7, 0:128], lhsT=vtok[:, bb - 1 + t, kv, :], rhs=pT_sb[j][:, t, :], start=(t == 0), stop=(t == 1)),
                         reads=[pt_b[j], vt_b], writes=([bank_b[7]] if t == 0 else []))
                bank_b[7].w = ('c', 'pe', len(S.ops['pe']) - 1)
                bank_b[7].r = []
                S.op('act', mk('copy', out=mixb[i][:, (bb - 1) * 128:bb * 128], in_=ps[:, 7, 0:128]), reads=[bank_b[7]], writes=[mb_b[i]])

        def epi_q(n, g, banks, c0, cl):
            i = n % 2
            S.op('act', mk('copy', out=qb[i][:, c0:c0 + cl], in_=ps[:, banks[0], 0:cl]), reads=[bank_b[banks[0]]], writes=[q_b[i]])

        def post_q(h):
            attn_head(h)
            pdma(G.mix_d[h][:, 0:1024], mixb[h % 2][:, 0:1024], "mx%d" % (h % 2), reads=[mb_b[h % 2]])

        gemm([G.win_d], KC, list(range(16)), act, groups, epi_q, post_n=post_q, act_reads=[h_b])

        if stop == "mix_attn":
            return
        for s in range(16):
            pdma(kc_f, G.ck_d[s], "sk", writes=[kcf_b])
            pdma(vc_f, G.cv_d[s], "sv", writes=[vcf_b])
            S.op('act', mk('copy', out=kc_b, in_=kc_f), reads=[kcf_b], writes=[kcb_b])
            S.op('dve', mk('tensor_copy', out=vc_b, in_=vc_f), reads=[vcf_b], writes=[vcb_b])
            bk = 4 + s % 2
            pv = ps[:, bk, :].bitcast(BF16)
            for kv in range(4):
                S.op('pe', mk('transpose', pv[:, kv * 128:(kv + 1) * 128], kc_b[:, kv * 128:(kv + 1) * 128], ident_bf),
                     reads=[kcb_b, const_b], writes=[bank_b[bk]])
            S.op('act', mk('copy', out=kcT, in_=pv[:, 0:512].rearrange("p (k d) -> p k d", k=4)), reads=[bank_b[bk]], writes=[kct_b])
            pv2 = ps[0:8, 6, :].bitcast(BF16)
            for kv in range(4):
                S.op('pe', mk('transpose', pv2[:, kv * 128:(kv + 1) * 128], vT[:, kv, 1152 + 8 * s:1160 + 8 * s], ident_bf),
                     reads=[v_b, const_b], writes=[bank_b[6]])
            S.op('act', mk('copy', out=vnw, in_=pv2[:, 0:512].rearrange("p (k d) -> p k d", k=4)), reads=[bank_b[6]], writes=[vnw_b])
            for kv in range(4):
                j = ucnt["u"] % 2
                ucnt["u"] += 1
                sbk = 4 + j
                ql = qTs[:, s, 4 * kv:4 * kv + 4, :].rearrange("p g t -> p (g t)")
                S.op('pe', mk('matmul', ps[0:32, sbk, 0:128], lhsT=ql, rhs=kcT[:, kv, :], start=True, stop=True),
                     reads=[qs_b, kct_b], writes=[bank_b[sbk]])
                S.op('pe', mk('matmul', ps[0:32, sbk, 128:136], lhsT=ql, rhs=kT[:, kv, 1152 + 8 * s:1160 + 8 * s], start=True, stop=True),
                     reads=[qs_b, k_b], writes=[bank_b[sbk]])
                softmax_rows(32, 136, sbk, biasS[:, kv, :], [bs_b], sinkS[:, kv:kv + 1], j, False)
                ptv = ps[:, 6, :].bitcast(BF16)
                S.op('pe', mk('transpose', ptv[:, 0:32], pn_sb[j][0:32, 0:128], ident_bf[0:32, 0:32]),
                     reads=[pn_b[j], const_b], writes=[bank_b[6]])
                S.op('pe', mk('transpose', ptv[0:8, 32:64], pn_sb[j][0:32, 128:136], ident_bf[0:32, 0:32]),
                     reads=[pn_b[j], const_b], writes=[bank_b[6]])
                S.op('act', mk('copy', out=pT_sb[j][:, 0, 0:32], in_=ptv[:, 0:32]), reads=[bank_b[6]], writes=[pt_b[j]])
                S.op('act', mk('copy', out=pT2, in_=ptv[0:8, 32:64]), reads=[bank_b[6]], writes=[pt2_b])
                S.op('pe', mk('matmul', ps[:, 7, 0:32], lhsT=vc_b[:, kv * 128:(kv + 1) * 128], rhs=pT_sb[j][:, 0, 0:32], start=True, stop=False),
                     reads=[vcb_b, pt_b[j]], writes=[bank_b[7]])
                S.op('pe', mk('matmul', ps[:, 7, 0:32], lhsT=vnw[:, kv, :], rhs=pT2, start=False, stop=True),
                     reads=[vnw_b, pt2_b], writes=[bank_b[7]])
                S.op('act', mk('copy', out=mixS[:, 4 * kv:4 * kv + 4, 8 * s:8 * s + 8], in_=ps[:, 7, 0:32].rearrange("p (g t) -> p g t", g=4)),
                     reads=[bank_b[7]], writes=[ms_b])
        pdma(G.mix_d[0:16, :, 1024:1152].rearrange("h p t -> p h t"), mixS, "o7", reads=[ms_b])
        barrier()

    def phase_outproj(hT, h_b):
        A.off = PH0
        mixed = A.get(KC * T1 * 2, BF16, (KC, T1))
        xo = [A.get(T1 * 4, F32) for _ in range(2)]
        xn = [A.get(T1 * 4, F32) for _ in range(2)]
        rt = A.get(128 * 4, F32, (16, 8))
        m_b = Buf()
        xo_b, xn_b, rt_b = [Buf(), Buf()], [Buf(), Buf()], Buf()
        for q in range(4):
            pdma(mixed[:, 8 * q:8 * q + 8, :], G.mix_d[8 * q:8 * q + 8].rearrange("c p t -> p c t"), "ml%d" % q, writes=[m_b])
        load_gate(1)
        pre_n, epi, post_n = make_resid(T1, G.x1_d, 128, G.x2_d, 0, T1, xo, xo_b, xn, xn_b, rt, rt_b, "w")
        gemm([G.wout_d], KC, range(KC), lambda k, c0, cl: mixed[:, k, c0:c0 + cl], col_groups(0, T1), epi, pre_n=pre_n, post_n=post_n, act_reads=[m_b])
        barrier()

    def phase_final():
        A.off = PH0
        xbuf = [A.get(T0 * 4, F32) for _ in range(NXB)]
        sq = [A.get(T0 * 4, F32) for _ in range(NXB)]
        rstd = A.get(T0 * 4, F32)
        yb = [A.get(T0 * 4, F32) for _ in range(NXB)]
        xb_b, sq_b, yb_b = [Buf() for _ in range(NXB)], [Buf() for _ in range(NXB)], [Buf() for _ in range(NXB)]
        rstd_b = Buf()
        norm_stats(G.x3_d, T1, xbuf, xb_b, sq, sq_b, rstd, rstd_b)
        for c in range(KC):
            i = c % NXB
            pdma(xbuf[i][:, 0:T1], G.x3_d[c], "xb%d" % i, writes=[xb_b[i]])
            S.op('dve', mk('scalar_tensor_tensor', out=yb[i][:, 0:T1], in0=xbuf[i][:, 0:T1], scalar=gains[:, 3, c:c + 1],
                                                                   in1=rstd[:, 0:T1], op0=ALU.mult, op1=ALU.mult),
                 reads=[xb_b[i], rstd_b, const_b], writes=[yb_b[i]])
            t = pdma(G.yT_d[c], yb[i][:, 0:T1], "yo%d" % i, reads=[yb_b[i]])

    hT = arena[:, PH0:PH0 + KC * T0 * 2].bitcast(BF16).rearrange("p (c t) -> p c t", c=KC)
    hT1 = arena[:, PH0:PH0 + KC * T1 * 2].bitcast(BF16).rearrange("p (c t) -> p c t", c=KC)
    h_b = Buf()
    def _run_phases():
        setup()
        if stop_after == "setup":
            pdma(G.dbg_c_d, ident_f, "dbgc", reads=[const_b])
            return
        phase_relbias()
        ms = ModStream()
        for _ in range(16):
            ms.step()
        ms.flush(0, 2)
        barrier()
        if stop_after == "mod":
            for _ in range(56):
                ms.step()
            ms.flush(2, 9)
            return
        phase_norm(G.xT_d, T0, 0, hT, h_b, nxb=2)
        if stop_after == "norm1":
            pdma(G.dbg_h_d, hT, "dbgh", reads=[h_b])
            return

        def up_hook(f):
            if f is None:
                ms.advance(72 * 16)
                ms.flush(2, 9)
            else:
                ms.advance(4)

        phase_ffn(G.w1a_d, G.w3a_d, G.w2a_d, T0, G.xT_d, 0, G.x1_d, hT, h_b, 0, up_only=(stop_after == "ffn1up"), up_hook=up_hook)
        if stop_after in ("ffn1up", "ffn1"):
            return
        phase_norm(G.x1_d, T0, 1, hT, h_b, nxb=2)
        phase_mixer(hT, h_b, stop_after)
        if stop_after in ("mixer", "mix_conv", "mix_kv", "mix_attn"):
            return
        phase_outproj(hT, h_b)
        if stop_after == "outproj":
            return
        phase_norm(G.x2_d, T1, 2, hT1, h_b, nxb=2)
        phase_ffn(G.w1b_d, G.w3b_d, G.w2b_d, T1, G.x2_d, 0, G.x3_d, hT1, h_b, 2)
        phase_final()

    _run_phases()
    barrier()

    sem_cm = {e: nc.semaphore("s_" + e) for e in ENG}
    sems = {e: cm.__enter__() for e, cm in sem_cm.items()}
    dkeys = sorted(S.dma_cnt.keys())
    dcm = {k: nc.semaphore("d_" + k) for k in dkeys}
    dma_sems = {k: cm.__enter__() for k, cm in dcm.items()}
    if os.environ.get('KSIM'):
        S.simulate()
    with nc.Block() as block:
        S.emit(nc, block, sems, dma_sems)
    for cm in list(dcm.values()) + list(sem_cm.values()):
        cm.__exit__(None, None, None)
    psum_cm.__exit__(None, None, None)
    arena_cm.__exit__(None, None, None)
    return nc


def _wl(w):
    K, N = w.shape
    return np.ascontiguousarray(w.reshape(K // 128, 128, N // 128, 128).transpose(2, 1, 0, 3)).reshape(N // 128, 128, K)


def _fm(a):
    r, f = a.shape
    return np.ascontiguousarray(a.T).reshape(f // 128, 128, r)


_PROG = {}


def make_in_maps(x_prompt, x_sample, c_prompt, c_sample, cache_k, cache_v, state_conv, rel_bias,
           g_ffn1, w1_ffn1, w3_ffn1, w2_ffn1, g_mix, w_in, sinks, conv_w, w_out,
           g_ffn2, w1_ffn2, w3_ffn2, w2_ffn2, w_ada, b_ada, g_final):
    f32 = np.float32
    A_ = lambda a: np.asarray(a, dtype=f32)
    xp = A_(x_prompt)[0]
    xs = A_(x_sample)
    cp, cs = A_(c_prompt), A_(c_sample)
    ck, cv, sc = A_(cache_k)[0], A_(cache_v)[0], A_(state_conv)[0]
    shared = {
        "badaT": np.ascontiguousarray(A_(b_ada)[0].reshape(288, 128).T),
        "w_ada": np.ascontiguousarray(A_(w_ada)[0].reshape(KC, 128, 72, 512).transpose(2, 1, 0, 3)).reshape(72, 128, KC * 512),
        "w1a": _wl(A_(w1_ffn1)[0]), "w3a": _wl(A_(w3_ffn1)[0]), "w2a": _wl(A_(w2_ffn1)[0]),
        "w_in": _wl(A_(w_in)[0]), "w_out": _wl(A_(w_out)[0]),
        "w1b": _wl(A_(w1_ffn2)[0]), "w3b": _wl(A_(w3_ffn2)[0]), "w2b": _wl(A_(w2_ffn2)[0]),
        "gains": np.ascontiguousarray(np.stack([A_(g_ffn1)[0], A_(g_mix)[0], A_(g_ffn2)[0], A_(g_final)]).reshape(4, KC, 128).transpose(2, 0, 1)),
        "convw": np.ascontiguousarray(A_(conv_w)[0].reshape(3, 16, 128).transpose(2, 1, 0)),
        "sinks_bc": np.ascontiguousarray(np.broadcast_to(A_(sinks)[0][None, :], (128, 16))),
        "sinkS": np.ascontiguousarray(np.repeat(A_(sinks)[0].reshape(4, 4).T, 8, axis=0)),
        "relb": np.ascontiguousarray(A_(rel_bias)),
    }
    m = np.arange(384)
    dist = 255 - m
    valid = (dist >= 0) & (dist <= 128)
    bucket = t5_bucket_np(dist)
    oh = np.zeros((33, 384), f32)
    oh[bucket[valid], m[valid]] = 1.0
    oh[32, ~valid] = NEG
    shared["onehot"] = oh

    in_maps = []
    for i in range(NCORES):
        p0 = 1024 * i
        halo_rows = xp[p0 - 128:p0] if i > 0 else xp[0:128]
        rows = np.concatenate([halo_rows, xp[p0:p0 + 1024], xs[16 * i:16 * i + 16].reshape(128, D)], axis=0)
        crow = np.concatenate([cp[0:1], cs[16 * i:16 * i + 16]], axis=0)
        mp = dict(shared)
        mp["xT"] = _fm(rows)
        mp["cT"] = np.ascontiguousarray(_fm(crow).transpose(1, 0, 2))
        mp["halo"] = np.ascontiguousarray(np.broadcast_to(np.array([[0.0, NEG]] if i == 0 else [[1.0, 0.0]], f32), (128, 2)))
        mp["stT"] = np.ascontiguousarray(sc[16 * i:16 * i + 16].reshape(16, 2, 16, 128).transpose(3, 2, 0, 1))
        mp["ck"] = np.ascontiguousarray(ck[16 * i:16 * i + 16].reshape(16, 128, 512))
        mp["cv"] = np.ascontiguousarray(cv[16 * i:16 * i + 16].reshape(16, 128, 512))
        in_maps.append(mp)

    return in_maps


def kernel(**inputs):
    f32 = np.float32
    in_maps = make_in_maps(**inputs)
    if "nc" not in _PROG:
        _PROG["nc"] = build_program()
    res = run_bass_kernel_spmd(_PROG["nc"], in_maps, core_ids=list(range(NCORES)))
    R = res.results

    def tm(a):
        C, P, T = a.shape
        return np.ascontiguousarray(a.reshape(C * P, T).T)

    y_prompt = np.empty((1, 8192, D), f32)
    y_sample = np.empty((128, 8, D), f32)
    k_ws = np.empty((1, 128, 128, 4, 128), f32)
    v_ws = np.empty((1, 128, 128, 4, 128), f32)
    conv_s = np.empty((1, 128, 2, 2048), f32)
    for i in range(NCORES):
        y = tm(R[i]["yT"])
        y_prompt[0, 1024 * i:1024 * i + 1024] = y[0:1024]
        y_sample[16 * i:16 * i + 16] = y[1024:1152].reshape(16, 8, D)
        kn = tm(R[i]["knewT"]).reshape(16, 8, 4, 128)
        vn = tm(R[i]["vnewT"]).reshape(16, 8, 4, 128)
        k_ws[0, 16 * i:16 * i + 16, 0:120] = R[i]["kwsc"].reshape(16, 120, 4, 128)
        k_ws[0, 16 * i:16 * i + 16, 120:128] = kn
        v_ws[0, 16 * i:16 * i + 16, 0:120] = R[i]["vwsc"].reshape(16, 120, 4, 128)
        v_ws[0, 16 * i:16 * i + 16, 120:128] = vn
        co = R[i]["convo"]
        conv_s[0, 16 * i:16 * i + 16] = co[:, :, 1:, :].transpose(2, 3, 1, 0).reshape(16, 2, 2048)
    last = R[NCORES - 1]
    k_wp = tm(last["kwinT"]).reshape(1, 1, 128, 4, 128)
    v_wp = tm(last["vwinT"]).reshape(1, 1, 128, 4, 128)
    conv_p = np.ascontiguousarray(last["convo"][:, :, 0, :].transpose(2, 1, 0)).reshape(1, 1, 2, 2048)
    return (y_prompt, y_sample, k_wp, v_wp, conv_p, k_ws, v_ws, conv_s)
```
